# Optimizing a Trainium2 kernel written in Bass

```python
import math, functools
import jax, jax.numpy as jnp
from jax import lax
import numpy as np

D_MODEL = 1024
BATCH = 16
SEQ = 2048
DEPTH = 4
DEC_BATCH = 128
DEC_SEQ = 8
PAST_LEN = 8192
PAGE_SIZE = 128

N_MIXERS = 4
N_REP = DEPTH // N_MIXERS
ALPHA = (2.0 * DEPTH) ** 0.25
BETA = (8.0 * DEPTH) ** -0.25
LN_EPS = 1e-5
RMS_EPS = 1e-6
N_MOD = 9
D_FF = 2816
S5_GROUP = 16
S5_GROUPS = D_MODEL // S5_GROUP
S5_STATE = 64
SCAN_BLOCK = 128
DT_MIN = 0.001
DT_MAX = 0.1
GM_WIDTH = D_MODEL
GM_HEADS = 8
GM_HEAD_DIM = GM_WIDTH // GM_HEADS
CHUNK = 128
POOL_WINDOWS = (2, 4, 8, 16)
POOL_GROUPS = 4
POOL_GROUP_DIM = D_MODEL // POOL_GROUPS
POOL_BUF = max(POOL_WINDOWS) - 1
MLA_HEADS = 8
D_NOPE = 128
D_ROPE = 64
D_V = 128
KV_LORA = 256
Q_LORA = 384
ROPE_BASE = 10000.0
Q_BLOCK = 128
ATTN_SCALE = (D_NOPE + D_ROPE) ** -0.5

kernel_name = 'hybrid_s5_gmlp_pool_mla_decoder_step'


def layer_norm(x, g, b):
    xf = x.astype(jnp.float32)
    mu = jnp.mean(xf, -1, keepdims=True)
    var = jnp.mean(jnp.square(xf - mu), -1, keepdims=True)
    return ((xf - mu) * lax.rsqrt(var + LN_EPS) * g.astype(jnp.float32) + b.astype(jnp.float32)).astype(x.dtype)


def rms_norm(x, g):
    xf = x.astype(jnp.float32)
    return (xf * lax.rsqrt(jnp.mean(jnp.square(xf), -1, keepdims=True) + RMS_EPS) * g.astype(jnp.float32)).astype(x.dtype)


def rope(x, pos):
    half = D_ROPE // 2
    inv_freq = jnp.power(ROPE_BASE, -jnp.arange(half, dtype=jnp.float32) * (2.0 / D_ROPE))
    ang = pos.astype(jnp.float32)[:, None] * inv_freq[None, :]
    shape = (1, pos.shape[0]) + (1,) * (x.ndim - 3) + (half,)
    cos, sin = jnp.cos(ang).reshape(shape), jnp.sin(ang).reshape(shape)
    xf = x.astype(jnp.float32)
    x1, x2 = xf[..., :half], xf[..., half:]
    return jnp.concatenate([x1 * cos - x2 * sin, x1 * sin + x2 * cos], -1).astype(x.dtype)


def swiglu(h, w_gate, w_up, w_down):
    return (jax.nn.silu(h @ w_gate) * (h @ w_up)) @ w_down


def _linear_recurrence(left, right):
    a_l, b_l = left
    a_r, b_r = right
    return a_r * a_l, a_r * b_l + b_r


def s5_mix(h, h0_re, h0_im, a_re, a_im, log_dt, b_re, b_im, c_re, c_im, d_skip, w_out, w_gate):
    f32 = jnp.float32
    bsz, t, _ = h.shape
    blk = min(t, SCAN_BLOCK)
    nblk = t // blk
    lam = lax.complex(a_re.astype(f32), a_im.astype(f32))
    dt = jnp.exp(log_dt.astype(f32))[:, None]
    a_bar = jnp.exp(lam * dt)
    b_bar = ((a_bar - 1.0) / lam)[:, :, None] * lax.complex(b_re.astype(f32), b_im.astype(f32))
    c_mat = lax.complex(c_re.astype(f32), c_im.astype(f32))
    u = h.astype(f32)
    u_blocks = u.reshape(bsz, nblk, blk, S5_GROUPS, S5_GROUP).transpose(1, 0, 2, 3, 4).astype(jnp.complex64)
    a_seq = jnp.broadcast_to(a_bar, (bsz, blk, S5_GROUPS, S5_STATE))

    def step(state, u_blk):
        bu = jnp.einsum('gpc,btgc->btgp', b_bar, u_blk)
        bu = bu.at[:, 0].add(a_bar * state)
        _, states = lax.associative_scan(_linear_recurrence, (a_seq, bu), axis=1)
        y_blk = jnp.real(jnp.einsum('gcp,btgp->btgc', c_mat, states))
        return states[:, -1], y_blk

    h0 = lax.complex(h0_re.astype(f32), h0_im.astype(f32))
    last, y = lax.scan(step, h0, u_blocks)
    y = y.transpose(1, 0, 2, 3, 4).reshape(bsz, t, D_MODEL) + d_skip.astype(f32) * u
    z = jax.nn.gelu(y).astype(h.dtype)
    out = (z @ w_out) * jax.nn.sigmoid(z @ w_gate)
    return out, (jnp.real(last).astype(h.dtype), jnp.imag(last).astype(h.dtype))


def gmlp_mix(h, w_in, ln_g, ln_b, w_s, b_s, w_out):
    bsz, t, _ = h.shape
    z = jax.nn.gelu(h @ w_in)
    u, v = z[..., :GM_WIDTH], z[..., GM_WIDTH:]
    v = layer_norm(v, ln_g, ln_b)
    length = min(t, CHUNK)
    n = t // length
    ws = w_s[:, :length, :length] * jnp.tril(jnp.ones((length, length), w_s.dtype))
    vc = v.reshape(bsz, n, length, GM_HEADS, GM_HEAD_DIM)
    mixed = jnp.einsum('hts,bnshc->bnthc', ws, vc) + b_s[:, :length].T[:, :, None]
    out = (u * mixed.reshape(bsz, t, GM_WIDTH)) @ w_out
    return out, v


def pool_mix(h, buf, w_pool, scale):
    bsz, t, _ = h.shape
    z = h if buf is None else jnp.concatenate([buf, h], axis=1)
    lb = z.shape[1] - t
    zf = z.astype(jnp.float32).reshape(bsz, z.shape[1], POOL_GROUPS, POOL_GROUP_DIM)
    cs = jnp.concatenate([jnp.zeros_like(zf[:, :1]), jnp.cumsum(zf, axis=1)], axis=1)
    hi = lb + jnp.arange(t, dtype=jnp.int32) + 1
    win = jnp.array(POOL_WINDOWS, jnp.int32)
    lo = jnp.maximum(hi[:, None] - win[None, :], 0)
    grp = jnp.arange(POOL_GROUPS)
    window_sum = cs[:, hi] - cs[:, lo, grp]
    mean = window_sum / (hi[:, None] - lo).astype(jnp.float32)[None, :, :, None]
    p = (mean - h.astype(jnp.float32).reshape(bsz, t, POOL_GROUPS, POOL_GROUP_DIM)).astype(h.dtype)
    y = jnp.einsum('btgc,gce->btge', p, w_pool).reshape(bsz, t, D_MODEL) * scale
    return y, z[:, -POOL_BUF:]


def mla_attend(q_lat, q_rope, q_pos, k_c, k_r, k_pos):
    s = (jnp.einsum('bqhc,bkc->bhqk', q_lat, k_c) + jnp.einsum('bqhr,bkr->bhqk', q_rope, k_r)).astype(jnp.float32) * ATTN_SCALE
    s = jnp.where((k_pos[None, :] <= q_pos[:, None])[None, None], s, -1e30)
    p = jax.nn.softmax(s, axis=-1).astype(k_c.dtype)
    return jnp.einsum('bhqk,bkc->bqhc', p, k_c)


def mla_mix(h, pos, past_ckv, past_krope, w_dq, q_norm, w_uq, w_dkv, kv_norm, w_uk, w_uv, w_o):
    bsz, t, _ = h.shape
    c_q = rms_norm(h @ w_dq, q_norm)
    q = jnp.einsum('btq,qhd->bthd', c_q, w_uq)
    q_lat = jnp.einsum('bthd,chd->bthc', q[..., :D_NOPE], w_uk)
    q_rope = rope(q[..., D_NOPE:], pos)
    kv = h @ w_dkv
    ckv = rms_norm(kv[..., :KV_LORA], kv_norm)
    krope = rope(kv[..., KV_LORA:], pos)
    if past_ckv is None:
        blk = min(t, Q_BLOCK)
        nblk = t // blk

        def one_block(args):
            qb, rb, pb = args
            return mla_attend(qb, rb, pb, ckv, krope, pos)

        o = lax.map(one_block, (q_lat.reshape(bsz, nblk, blk, MLA_HEADS, KV_LORA).swapaxes(0, 1),
                                q_rope.reshape(bsz, nblk, blk, MLA_HEADS, D_ROPE).swapaxes(0, 1),
                                pos.reshape(nblk, blk)))
        o_lat = o.swapaxes(0, 1).reshape(bsz, t, MLA_HEADS, KV_LORA)
    else:
        keys_c = jnp.concatenate([past_ckv, ckv], axis=1)
        keys_r = jnp.concatenate([past_krope, krope], axis=1)
        k_pos = jnp.arange(keys_c.shape[1], dtype=jnp.int32)
        o_lat = mla_attend(q_lat, q_rope, pos, keys_c, keys_r, k_pos)
    o = jnp.einsum('bthc,chv->bthv', o_lat, w_uv)
    y = jnp.einsum('bthv,hvd->btd', o, w_o)
    return y, (ckv, krope)


def layer_block(x, c, mixer, w_ada, b_ada, ln_g, ln_b, w_gate, w_up, w_down):
    mod = (jax.nn.silu(c) @ w_ada + b_ada).reshape(c.shape[0], N_MOD, 1, D_MODEL)

    def modulate(x, k):
        return x * (1.0 + mod[:, 3 * k + 1]) + mod[:, 3 * k]

    def post(x, f, k, weight):
        return layer_norm(ALPHA * x + weight * (1.0 + mod[:, 3 * k + 2]) * f, ln_g[k], ln_b[k])

    x = post(x, swiglu(modulate(x, 0), w_gate[0], w_up[0], w_down[0]), 0, 0.5)
    y, st = mixer(modulate(x, 1))
    x = post(x, y, 1, 1.0)
    x = post(x, swiglu(modulate(x, 2), w_gate[1], w_up[1], w_down[1]), 2, 0.5)
    return x, st


def setup_inputs(seed: int = 0) -> dict:
    key = jax.random.key(seed)
    ks = iter(jax.random.split(key, 64))
    f32 = jnp.float32

    def nrm(shape, scale=1.0):
        return scale * jax.random.normal(next(ks), shape, f32)

    d = D_MODEL
    n_pages = PAST_LEN // PAGE_SIZE
    n_phys = (DEC_BATCH * n_pages * 5) // 4
    page_table = jax.random.permutation(next(ks), n_phys)[: DEC_BATCH * n_pages].reshape(DEC_BATCH, n_pages).astype(jnp.int32)
    G, P, C = S5_GROUPS, S5_STATE, S5_GROUP
    return {
        'x_prompt': nrm((BATCH, SEQ, d)),
        'x_sample': nrm((DEC_BATCH, DEC_SEQ, d)),
        'state_ssm_re': nrm((N_REP, DEC_BATCH, G, P), 0.1),
        'state_ssm_im': nrm((N_REP, DEC_BATCH, G, P), 0.1),
        'state_pool': nrm((N_REP, DEC_BATCH, POOL_BUF, d)),
        'cache_mla_ckv': nrm((N_REP, n_phys, PAGE_SIZE, KV_LORA)),
        'cache_mla_krope': nrm((N_REP, n_phys, PAGE_SIZE, D_ROPE)),
        'page_table': page_table,
        'c_prompt': nrm((BATCH, d)),
        'c_sample': nrm((DEC_BATCH, d)),
        'w_ada': nrm((DEPTH, d, N_MOD * d), 0.1 * d ** -0.5),
        'b_ada': nrm((DEPTH, N_MOD * d), 0.01),
        'ln_g': 1.0 + nrm((DEPTH, 3, d), 0.02),
        'ln_b': nrm((DEPTH, 3, d), 0.02),
        'ffn_w_gate': nrm((DEPTH, 2, d, D_FF), d ** -0.5),
        'ffn_w_up': nrm((DEPTH, 2, d, D_FF), BETA * d ** -0.5),
        'ffn_w_down': nrm((DEPTH, 2, D_FF, d), BETA * D_FF ** -0.5),
        's5_a_re': -0.5 * jnp.exp(nrm((N_REP, G, P), 0.01)),
        's5_a_im': math.pi * jnp.arange(P, dtype=f32) + nrm((N_REP, G, P), 0.01),
        's5_log_dt': jax.random.uniform(next(ks), (N_REP, G), f32, math.log(DT_MIN), math.log(DT_MAX)),
        's5_b_re': nrm((N_REP, G, P, C), (2.0 * C) ** -0.5),
        's5_b_im': nrm((N_REP, G, P, C), (2.0 * C) ** -0.5),
        's5_c_re': nrm((N_REP, G, C, P), (2.0 * P) ** -0.5 * 4.0),
        's5_c_im': nrm((N_REP, G, C, P), (2.0 * P) ** -0.5 * 4.0),
        's5_d': nrm((N_REP, d)),
        's5_w_out': nrm((N_REP, d, d), BETA * d ** -0.5),
        's5_w_gate': nrm((N_REP, d, d), d ** -0.5),
        'gm_w_in': nrm((N_REP, d, 2 * GM_WIDTH), d ** -0.5),
        'gm_ln_g': 1.0 + nrm((N_REP, GM_WIDTH), 0.02),
        'gm_ln_b': nrm((N_REP, GM_WIDTH), 0.02),
        'gm_w_s': nrm((N_REP, GM_HEADS, CHUNK, CHUNK), CHUNK ** -0.5),
        'gm_b_s': 1.0 + nrm((N_REP, GM_HEADS, CHUNK), 0.02),
        'gm_w_out': nrm((N_REP, GM_WIDTH, d), BETA * GM_WIDTH ** -0.5),
        'pool_w': nrm((N_REP, POOL_GROUPS, POOL_GROUP_DIM, POOL_GROUP_DIM), BETA * POOL_GROUP_DIM ** -0.5),
        'pool_scale': 1.0 + nrm((N_REP, d), 0.02),
        'mla_w_dq': nrm((N_REP, d, Q_LORA), d ** -0.5),
        'mla_q_norm': 1.0 + nrm((N_REP, Q_LORA), 0.02),
        'mla_w_uq': nrm((N_REP, Q_LORA, MLA_HEADS, D_NOPE + D_ROPE), Q_LORA ** -0.5),
        'mla_w_dkv': nrm((N_REP, d, KV_LORA + D_ROPE), d ** -0.5),
        'mla_kv_norm': 1.0 + nrm((N_REP, KV_LORA), 0.02),
        'mla_w_uk': nrm((N_REP, KV_LORA, MLA_HEADS, D_NOPE), KV_LORA ** -0.5),
        'mla_w_uv': nrm((N_REP, KV_LORA, MLA_HEADS, D_V), BETA * KV_LORA ** -0.5),
        'mla_w_o': nrm((N_REP, MLA_HEADS, D_V, d), BETA * (MLA_HEADS * D_V) ** -0.5),
    }


def reference(x_prompt, x_sample, state_ssm_re, state_ssm_im, state_pool, cache_mla_ckv, cache_mla_krope,
              page_table, c_prompt, c_sample, w_ada, b_ada, ln_g, ln_b, ffn_w_gate, ffn_w_up, ffn_w_down,
              s5_a_re, s5_a_im, s5_log_dt, s5_b_re, s5_b_im, s5_c_re, s5_c_im, s5_d, s5_w_out, s5_w_gate,
              gm_w_in, gm_ln_g, gm_ln_b, gm_w_s, gm_b_s, gm_w_out, pool_w, pool_scale,
              mla_w_dq, mla_q_norm, mla_w_uq, mla_w_dkv, mla_kv_norm, mla_w_uk, mla_w_uv, mla_w_o):
    n_seq, n_pages = page_table.shape
    past_len = n_pages * PAGE_SIZE
    pos_p = jnp.arange(x_prompt.shape[1], dtype=jnp.int32)
    pos_s = past_len + jnp.arange(x_sample.shape[1], dtype=jnp.int32)
    xp, xs = x_prompt, x_sample
    ssm_re_p, ssm_im_p, ssm_re_s, ssm_im_s = [], [], [], []
    gm_v_s, pool_p, pool_s = [], [], []
    ckv_p, kr_p, ckv_s, kr_s = [], [], [], []
    for i in range(DEPTH):
        kind, r = i % N_MIXERS, i // N_MIXERS
        if kind == 0:
            w = dict(a_re=s5_a_re[r], a_im=s5_a_im[r], log_dt=s5_log_dt[r], b_re=s5_b_re[r], b_im=s5_b_im[r],
                     c_re=s5_c_re[r], c_im=s5_c_im[r], d_skip=s5_d[r], w_out=s5_w_out[r], w_gate=s5_w_gate[r])
            zero_state = jnp.zeros((xp.shape[0], S5_GROUPS, S5_STATE), xp.dtype)
            mix_p = functools.partial(s5_mix, h0_re=zero_state, h0_im=zero_state, **w)
            mix_s = functools.partial(s5_mix, h0_re=state_ssm_re[r], h0_im=state_ssm_im[r], **w)
        elif kind == 1:
            w = dict(w_in=gm_w_in[r], ln_g=gm_ln_g[r], ln_b=gm_ln_b[r], w_s=gm_w_s[r], b_s=gm_b_s[r], w_out=gm_w_out[r])
            mix_p = functools.partial(gmlp_mix, **w)
            mix_s = functools.partial(gmlp_mix, **w)
        elif kind == 2:
            mix_p = functools.partial(pool_mix, buf=None, w_pool=pool_w[r], scale=pool_scale[r])
            mix_s = functools.partial(pool_mix, buf=state_pool[r], w_pool=pool_w[r], scale=pool_scale[r])
        else:
            w = dict(w_dq=mla_w_dq[r], q_norm=mla_q_norm[r], w_uq=mla_w_uq[r], w_dkv=mla_w_dkv[r],
                     kv_norm=mla_kv_norm[r], w_uk=mla_w_uk[r], w_uv=mla_w_uv[r], w_o=mla_w_o[r])
            past_c = cache_mla_ckv[r, page_table].reshape(n_seq, past_len, KV_LORA)
            past_r = cache_mla_krope[r, page_table].reshape(n_seq, past_len, D_ROPE)
            mix_p = functools.partial(mla_mix, pos=pos_p, past_ckv=None, past_krope=None, **w)
            mix_s = functools.partial(mla_mix, pos=pos_s, past_ckv=past_c, past_krope=past_r, **w)
        xp, st_p = layer_block(xp, c_prompt, mix_p, w_ada[i], b_ada[i], ln_g[i], ln_b[i],
                               ffn_w_gate[i], ffn_w_up[i], ffn_w_down[i])
        xs, st_s = layer_block(xs, c_sample, mix_s, w_ada[i], b_ada[i], ln_g[i], ln_b[i],
                               ffn_w_gate[i], ffn_w_up[i], ffn_w_down[i])
        if kind == 0:
            ssm_re_p.append(st_p[0]); ssm_im_p.append(st_p[1])
            ssm_re_s.append(st_s[0]); ssm_im_s.append(st_s[1])
        elif kind == 1:
            gm_v_s.append(st_s)
        elif kind == 2:
            pool_p.append(st_p); pool_s.append(st_s)
        else:
            ckv_p.append(st_p[0]); kr_p.append(st_p[1])
            ckv_s.append(st_s[0]); kr_s.append(st_s[1])
    return (xp, xs,
            jnp.stack(ssm_re_p), jnp.stack(ssm_im_p), jnp.stack(ssm_re_s), jnp.stack(ssm_im_s),
            jnp.stack(gm_v_s),
            jnp.stack(pool_p), jnp.stack(pool_s),
            jnp.stack(ckv_p), jnp.stack(kr_p), jnp.stack(ckv_s), jnp.stack(kr_s))
```

```python
import contextlib
import math
import numpy as np
import ml_dtypes
import concourse.bass as bass
import concourse.mybir as mybir
from concourse.bass_utils import run_bass_kernel_spmd

F32 = mybir.dt.float32
BF16 = mybir.dt.bfloat16
I32 = mybir.dt.int32
U32 = mybir.dt.uint32
AF = mybir.ActivationFunctionType
ALU = mybir.AluOpType
AX = mybir.AxisListType

D = 1024
DFF = 2816
NFC = DFF // 128
DEPTH = 4
SEQ = 2048
NPS = 2
NSS = 16
DEC = 8
T = 512
ALPHA = (2.0 * DEPTH) ** 0.25
LN_EPS = 1e-5
RMS_EPS = 1e-6
N_PAGES = 64
PAGE = 128
PAST = N_PAGES * PAGE
KVL = 256
DR = 64
QL = 384
NH = 8
ATTN_SCALE = (128 + 64) ** -0.5
SLOT = 4096


class Res:
    __slots__ = ("name", "w", "r")

    def __init__(self, name):
        self.name = name
        self.w = []
        self.r = []


class Pend:
    __slots__ = ("sid", "val")

    def __init__(self, sid):
        self.sid = sid
        self.val = None


class Q:
    def __init__(self, name, sid):
        self.name = name
        self.sid = sid
        self.ops = []
        self.count = 0
        self.known = {}
        self.pend = []
        self.dma_n = 0


NRING = 8


class Prog:
    def __init__(self):
        self.q = {}
        for i, n in enumerate(["pe", "act", "dve", "pool", "sp"]):
            self.q[n] = Q(n, i)
        self.nsem = 5
        self.ring = {}
        for n in ["sp", "pool", "act"]:
            self.ring[n] = [[self.nsem + i, 0] for i in range(NRING)]
            self.nsem += NRING
        self.dry = False
        self.no_self_wait = {"pe"}

    def _need(self, q, waits, ev):
        if ev is None:
            return
        if isinstance(ev, Pend):
            if ev.sid == q.sid and q.name in self.no_self_wait:
                return
            assert ev.val is not None, "dependency on unsignalled PE op"
            ev = (ev.sid, ev.val)
        sid, val = ev
        if sid == q.sid and q.name in self.no_self_wait:
            return
        if q.known.get(sid, 0) >= val:
            return
        if waits.get(sid, 0) < val:
            waits[sid] = val

    def _deps(self, q, reads, writes):
        waits = {}
        for r in reads:
            for ev in r.w:
                self._need(q, waits, ev)
        for w in writes:
            for ev in w.w:
                self._need(q, waits, ev)
            for ev in w.r:
                self._need(q, waits, ev)
        for sid, val in waits.items():
            q.known[sid] = val
        return waits

    def op(self, eng, fn, reads=(), writes=(), sig=True):
        if self.dry:
            return None
        q = self.q[eng]
        waits = self._deps(q, reads, writes)
        if sig:
            q.count += 1
            ev = (q.sid, q.count)
            for p in q.pend:
                p.val = q.count
            q.pend = []
        else:
            ev = Pend(q.sid)
            q.pend.append(ev)
        q.ops.append((sorted(waits.items()), fn, q.sid if sig else None))
        for r in reads:
            r.r.append(ev)
        for w in writes:
            w.w = [ev]
            w.r = []
        return ev

    def dma(self, eng, out, in_, reads=(), writes=(), **kw):
        return self.dma_group(eng, [(out, in_)], reads, writes, **kw)

    def dma_group(self, eng, parts, reads=(), writes=(), **kw):
        if self.dry:
            return []
        q = self.q[eng]
        waits = self._deps(q, reads, writes)
        evs = []
        for out, in_ in parts:
            slot = self.ring[eng][q.dma_n % NRING]
            q.dma_n += 1
            sid, prev = slot
            if prev > 0:
                self._need(q, waits, (sid, prev))
                q.known[sid] = max(q.known.get(sid, 0), prev)
            slot[1] = prev + 16
            ev = (sid, prev + 16)
            evs.append(ev)

            if callable(out):
                fn = out
            else:
                def fn(e, out=out, in_=in_, kw=kw):
                    return e.dma_start(out=out, in_=in_, **kw)
            q.ops.append((sorted(waits.items()), fn, ("dma", sid)))
            waits = {}
        for r in reads:
            r.r.extend(evs)
        for w in writes:
            w.w = list(evs)
            w.r = []
        return evs

    def wait_all(self, eng, evs):
        q = self.q[eng]
        waits = {}
        for ev in evs:
            self._need(q, waits, ev)
        for sid, val in waits.items():
            q.known[sid] = val
        q.ops.append((sorted(waits.items()), None, None))

    def check(self):
        sem = {}
        pos = {n: 0 for n in self.q}
        progress = True
        while progress:
            progress = False
            for n, q in self.q.items():
                while pos[n] < len(q.ops):
                    waits, fn, sig = q.ops[pos[n]]
                    if any(sem.get(sid, 0) < val for sid, val in waits):
                        break
                    if sig is not None:
                        if isinstance(sig, tuple):
                            sem[sig[1]] = sem.get(sig[1], 0) + 16
                        else:
                            sem[sig] = sem.get(sig, 0) + 1
                    pos[n] += 1
                    progress = True
        stuck = {n: (pos[n], len(q.ops), q.ops[pos[n]][0]) for n, q in self.q.items() if pos[n] < len(q.ops)}
        assert not stuck, f"deadlock: {stuck} sems={sem}"

    def emit(self, nc, es):
        sems = [es.enter_context(nc.semaphore(f"s{i}")) for i in range(self.nsem)]
        with nc.Block() as block:
            def mk(q):
                def body(e):
                    for waits, fn, sig in q.ops:
                        for sid, val in waits:
                            e.wait_ge(sems[sid], val)
                        if fn is None:
                            continue
                        inst = fn(e)
                        if sig is None:
                            continue
                        if isinstance(sig, tuple):
                            inst.then_inc(sems[sig[1]], 16)
                        else:
                            inst.then_inc(sems[sig], 1)
                return body
            block.tensor(mk(self.q["pe"]))
            block.scalar(mk(self.q["act"]))
            block.vector(mk(self.q["dve"]))
            block.gpsimd(mk(self.q["pool"]))
            block.sync(mk(self.q["sp"]))


class WStream:
    def __init__(self, P, nslots):
        self.P = P
        self.nslots = nslots
        self.plan = []
        self.i = 0
        self.issued = 0
        self.slots = None
        self.res = [Res(f"ws{i}") for i in range(nslots)]

    def reset(self):
        self.i = 0
        self.issued = 0

    def _issue(self, j):
        key, parts, reads = self.plan[j]
        s = j % self.nslots
        tile = self.slots[s]
        self.P.dma_group("pool", [(view_fn(tile), src) for view_fn, src in parts], reads=reads, writes=[self.res[s]])

    def get(self, key, parts, reads=()):
        if self.P.dry:
            self.plan.append((key, parts, list(reads)))
            return self.slots[0], self.res[0]
        assert self.plan[self.i][0] == key, (self.plan[self.i][0], key)
        while self.issued < min(len(self.plan), self.i + self.nslots):
            self._issue(self.issued)
            self.issued += 1
        s = self.i % self.nslots
        self.i += 1
        return self.slots[s], self.res[s]


WEIGHT_SPECS = [
    ("w_ada", [DEPTH, D, 9 * D]), ("b_ada", [DEPTH, 9 * D]), ("ln_g", [DEPTH, 3, D]), ("ln_b", [DEPTH, 3, D]),
    ("ffn_w_gate", [DEPTH, 2, D, DFF]), ("ffn_w_up", [DEPTH, 2, D, DFF]), ("ffn_w_down", [DEPTH, 2, DFF, D]),
    ("s5_a_re", [1, 64, 64]), ("s5_a_im", [1, 64, 64]), ("s5_log_dt", [1, 64]),
    ("s5_b_re", [1, 64, 64, 16]), ("s5_b_im", [1, 64, 64, 16]), ("s5_c_re", [1, 64, 16, 64]), ("s5_c_im", [1, 64, 16, 64]),
    ("s5_d", [1, D]), ("s5_w_out", [1, D, D]), ("s5_w_gate", [1, D, D]),
    ("gm_w_in", [1, D, 2 * D]), ("gm_ln_g", [1, D]), ("gm_ln_b", [1, D]), ("gm_w_s", [1, 8, 128, 128]),
    ("gm_b_s", [1, 8, 128]), ("gm_w_out", [1, D, D]), ("pool_w", [1, 4, 256, 256]), ("pool_scale", [1, D]),
    ("mla_w_dq", [1, D, QL]), ("mla_q_norm", [1, QL]), ("mla_w_uq", [1, QL, NH, 192]), ("mla_w_dkv", [1, D, KVL + DR]),
    ("mla_kv_norm", [1, KVL]), ("mla_w_uk", [1, KVL, NH, 128]), ("mla_w_uv", [1, KVL, NH, 128]), ("mla_w_o", [1, NH, 128, D]),
]


class KB:
    def __init__(self, cfg):
        self.cfg = cfg
        self.nc = bass.Bass("TRN2", target_bir_lowering=False)
        self.P = Prog()
        self.es = contextlib.ExitStack()
        self.out_evs = []
        self.dram = {}
        self._n = 0

    def din(self, name, shape, dt=F32):
        self.dram[name] = self.nc.dram_tensor(name, list(shape), dt, kind="ExternalInput").ap()
        return self.dram[name]

    def dout(self, name, shape, dt=F32):
        self.dram[name] = self.nc.dram_tensor(name, list(shape), dt, kind="ExternalOutput").ap()
        return self.dram[name]

    def dscr(self, name, shape, dt=F32):
        self.dram[name] = self.nc.dram_tensor(name, list(shape), dt, kind="Internal").ap()
        return self.dram[name]

    def sb(self, name, shape, dt=F32):
        return self.es.enter_context(self.nc.sbuf_tensor(name, list(shape), dt))

    def ps(self, name, shape, dt=F32):
        return self.es.enter_context(self.nc.psum_tensor(name, list(shape), dt))

    def mm(self, out, lhsT, rhs, start, stop, reads, writes, sig=False, **kw):
        return self.P.op("pe", lambda e: e.matmul(out, lhsT=lhsT, rhs=rhs, start=start, stop=stop, **kw),
                         reads=reads, writes=writes, sig=sig)

    def tr(self, out, in_, ident, reads, writes, sig=False):
        return self.P.op("pe", lambda e: e.transpose(out=out, in_=in_, identity=ident), reads=reads, writes=writes, sig=sig)

    def act(self, out, in_, func, reads, writes, **kw):
        return self.P.op("act", lambda e: e.activation(out=out, in_=in_, func=func, **kw), reads=reads, writes=writes)

    def tt(self, eng, out, in0, in1, op, reads, writes):
        return self.P.op(eng, lambda e: e.tensor_tensor(out=out, in0=in0, in1=in1, op=op), reads=reads, writes=writes)

    def ts(self, eng, out, in0, s1, s2, op0, op1, reads, writes):
        return self.P.op(eng, lambda e: e.tensor_scalar(out=out, in0=in0, scalar1=s1, scalar2=s2, op0=op0, op1=op1),
                         reads=reads, writes=writes)

    def stt(self, out, in0, scalar, in1, op0, op1, reads, writes):
        return self.P.op("dve", lambda e: e.scalar_tensor_tensor(out=out, in0=in0, scalar=scalar, in1=in1, op0=op0, op1=op1),
                         reads=reads, writes=writes)

    def cp(self, eng, out, in_, reads, writes):
        if eng == "act":
            return self.P.op("act", lambda e: e.copy(out=out, in_=in_), reads=reads, writes=writes)
        return self.P.op(eng, lambda e: e.tensor_copy(out=out, in_=in_), reads=reads, writes=writes)

    def sel(self, i):
        self.cur = i
        self.x, self.xR = self.X2[i], self.X2R[i]
        self.xmT, self.xmTR = self.XT2[i], self.XT2R[i]

    def memset(self, eng, ap, val, writes):
        return self.P.op(eng, lambda e: e.memset(ap, val), writes=writes)


class Tile:
    def __init__(self, kind, seq, t0, ntok):
        self.kind = kind
        self.seq = seq
        self.t0 = t0
        self.ntok = ntok
        self.NB = ntok // 128


def _declare(kb):
    nc, P = kb.nc, kb.P
    kb.din("xp", [NPS, SEQ, D]); kb.din("xs", [128, D])
    kb.din("st_re", [NSS, 64, 64]); kb.din("st_im", [NSS, 64, 64]); kb.din("st_pool", [NSS, 15, D])
    kb.din("cache_ckv", [kb.cfg["n_phys"], PAGE, KVL]); kb.din("cache_kr", [kb.cfg["n_phys"], PAGE, DR])
    kb.din("ptab", [NSS, N_PAGES], I32)
    kb.din("c_all", [NPS + NSS, D])
    for n, shp in WEIGHT_SPECS:
        kb.din(n, kb.cfg.get("wshapes", {}).get(n, shp))
    kb.din("k_ident", [128, 128])
    for n, shp in kb.cfg.get("consts", []):
        kb.din(n, shp)
    kb.dout("y_p", [NPS, SEQ, D]); kb.dout("y_s", [128, D])
    kb.dout("ssm_re_p", [NPS, 64, 64]); kb.dout("ssm_im_p", [NPS, 64, 64])
    kb.dout("ssm_re_s", [NSS, 64, 64]); kb.dout("ssm_im_s", [NSS, 64, 64])
    kb.dout("gm_v_s", [128, D])
    kb.dout("pool_p", [NPS, 15, D]); kb.dout("pool_s", [NSS, 15, D])
    kb.dout("ckv_p", [NPS, SEQ, KVL]); kb.dout("kr_p", [NPS, SEQ, DR])
    kb.dout("ckv_s", [128, KVL]); kb.dout("kr_s", [128, DR])
    kb.dscr("mod_scr", [DEPTH, NPS + NSS, 9 * D])
    for n, shp, dt in kb.cfg.get("dbg", []):
        kb.dout(n, shp, dt)

    NBM = T // 128
    kb.X2 = [kb.sb(f"x{s}", [128, NBM, D]) for s in range(2)]
    kb.X2R = [[Res(f"x{s}_{i}") for i in range(NBM)] for s in range(2)]
    kb.XT2 = [kb.sb(f"xmT{s}", [128, 8, T], BF16) for s in range(2)]
    kb.XT2R = [[Res(f"xmT{s}_{i}") for i in range(NBM)] for s in range(2)]
    kb.sel(0)
    kb.WS = kb.sb("WS", [128, 24 * 1024], BF16)
    kb.hR2 = [[Res(f"h{s}_{i}") for i in range(NFC)] for s in range(2)]
    kb.hR = kb.hR2[0] + kb.hR2[1]
    kb.ws = WStream(P, 3)
    kb.ws.slots = [kb.sb(f"ring{i}", [128, SLOT], BF16) for i in range(3)]
    names = ["S1", "Bm", "G1", "LNg", "LNb"]
    kb.bc = {n: kb.sb("bc_" + n, [128, D]) for n in names}
    kb.bcR = {n: Res("bc_" + n) for n in names}
    kb.tmpf = [kb.sb(f"tmpf{i}", [128, D]) for i in range(2)]; kb.tmpfR = [Res(f"tmpf{i}") for i in range(2)]
    kb.xmb = [kb.sb(f"xmb{i}", [128, D], BF16) for i in range(2)]; kb.xmbR = [Res(f"xmb{i}") for i in range(2)]
    kb.ysb = [kb.sb(f"ysb{i}", [128, T]) for i in range(2)]; kb.ysbR = [Res(f"ysb{i}") for i in range(2)]
    kb.t2 = [kb.sb(f"t2{i}", [128, NBM, 128]) for i in range(2)]; kb.t2R = [Res(f"t2{i}") for i in range(2)]
    kb.sgt = [kb.sb(f"sgt{i}", [128, T], BF16) for i in range(2)]; kb.sgtR = [Res(f"sgt{i}") for i in range(2)]
    kb.st6 = kb.sb("st6", [128, 2, 6]); kb.st6R = Res("st6")
    kb.mv = kb.sb("mv", [128, 8]); kb.mvR = Res("mv")
    kb.identf = kb.sb("identf", [128, 128]); kb.identb = kb.sb("identb", [128, 128], BF16)
    kb.cR = Res("consts")
    kb.pb = [kb.ps(f"pb{i}", [128, 512]) for i in range(7)]
    kb.pbR = [Res(f"pb{i}") for i in range(8)]
    kb.pt = kb.ps("pt", [128, 1024], BF16)
    kb.dummy = kb.sb("dummyt", [128, 8])
    kb.din("k_tmask", [128, 128])
    kb.tmask = kb.sb("tmask", [128, 128])
    P.dma("sp", kb.tmask[:, :], kb.dram["k_tmask"][:, :], writes=[kb.cR])
    _s5_declare(kb)
    _gm_declare(kb)
    _pool_declare(kb)
    _mla_declare(kb)
    P.dma("sp", kb.identf[:, :], kb.dram["k_ident"][:, :], writes=[kb.cR])
    P.dma("pool", kb.identb[:, :], kb.dram["k_ident"][:, :], writes=[kb.cR])


def _mod_phase(kb):
    P, d = kb.P, kb.dram
    NS = NPS + NSS
    csb = kb.tmpf[0]; csbR = kb.tmpfR[0]
    P.dma("sp", csb[0:NS, :], d["c_all"][:, :], writes=[csbR])
    cs = kb.xmb[0]; csR = kb.xmbR[0]
    kb.act(cs[0:NS, :], csb[0:NS, :], AF.Silu, reads=[csbR], writes=[csR])
    cT = kb.sgt[0]; cTR = kb.sgtR[0]
    cTv = cT[:, 0:8 * NS].rearrange("p (k s) -> p k s", k=8)
    for k in range(8):
        kb.tr(kb.pt[:, k * NS:(k + 1) * NS], cs[0:NS, k * 128:(k + 1) * 128], kb.identb[0:NS, 0:NS],
              reads=[csR, kb.cR], writes=[kb.pbR[7]], sig=(k == 7))
    kb.cp("act", cT[:, 0:8 * NS], kb.pt[:, 0:8 * NS], reads=[kb.pbR[7]], writes=[cTR])
    modR = kb.modR = [Res(f"mod{l}") for l in range(DEPTH)]
    bt = [kb.ysb[0], kb.ysb[1]]; btR = kb.ysbR
    ob = [kb.bc["S1"], kb.bc["Bm"]]; obR = [kb.bcR["S1"], kb.bcR["Bm"]]
    n = 0
    for l in kb.cfg["layers"]:
        for j in range(18):
            c0 = j * 512
            slot, sR = kb.ws.get(("ada", l, j), [(lambda t: t[:, 0:4096].rearrange("p (k f) -> p k f", k=8),
                                                   d["w_ada"][l, :, c0:c0 + 512].rearrange("(k p) f -> p k f", p=128))])
            sv = slot[:, 0:4096].rearrange("p (k f) -> p k f", k=8)
            b = n % 2
            P.dma("sp", bt[b][0:NS, :], d["b_ada"][l:l + 1, c0:c0 + 512].broadcast_to([NS, 512]), writes=[btR[b]])
            acc = kb.pb[b]
            for k in range(8):
                kb.mm(acc[0:NS, :], cTv[:, k, :], sv[:, k, :], k == 0, k == 7, reads=[cTR, sR], writes=[kb.pbR[b]], sig=(k == 7))
            kb.tt("dve", ob[b][0:NS, 0:512], acc[0:NS, :], bt[b][0:NS, :], ALU.add, reads=[kb.pbR[b], btR[b]], writes=[obR[b]])
            P.dma("sp", d["mod_scr"][l, :, c0:c0 + 512], ob[b][0:NS, 0:512], reads=[obR[b]], writes=[modR[l]])
            n += 1


def _bcast_load(kb, tl, name, src_fn):
    P = kb.P
    dst, R = kb.bc[name], kb.bcR[name]
    if tl.kind == "p":
        src, rd = src_fn(tl.seq)
        P.dma("sp", dst[:, :], src.broadcast_to([128, D]), reads=rd, writes=[R])
    else:
        parts = []
        rd = []
        for q in range(NSS):
            src, r1 = src_fn(NPS + q)
            rd = r1
            parts.append((dst[q * DEC:(q + 1) * DEC, :], src.broadcast_to([DEC, D])))
        P.dma_group("sp", parts, reads=rd, writes=[R])


def _mod_row(kb, l, col):
    d = kb.dram
    return lambda s: (d["mod_scr"][l, s:s + 1, col * D:(col + 1) * D], [kb.modR[l]])


def _load_pre(kb, tl, l, k):
    _bcast_load(kb, tl, "S1", _mod_row(kb, l, 3 * k + 1))
    _bcast_load(kb, tl, "Bm", _mod_row(kb, l, 3 * k))
    R = kb.bcR["S1"]
    kb.act(kb.bc["S1"][:, :], kb.bc["S1"][:, :], AF.Identity, reads=[R], writes=[R], scale=1.0, bias=1.0)


def _load_post(kb, tl, l, k, wgt):
    d = kb.dram
    _bcast_load(kb, tl, "G1", _mod_row(kb, l, 3 * k + 2))
    R = kb.bcR["G1"]
    kb.act(kb.bc["G1"][:, :], kb.bc["G1"][:, :], AF.Identity, reads=[R], writes=[R], scale=wgt / ALPHA, bias=wgt / ALPHA)
    kb.P.dma("sp", kb.bc["LNg"][:, :], d["ln_g"][l, k:k + 1, :].broadcast_to([128, D]), writes=[kb.bcR["LNg"]])
    kb.P.dma("sp", kb.bc["LNb"][:, :], d["ln_b"][l, k:k + 1, :].broadcast_to([128, D]), writes=[kb.bcR["LNb"]])


def _modulate(kb, tl, f32_keep=None):
    for nb in range(tl.NB):
        b = nb % 2
        tmp, tR = kb.tmpf[b], kb.tmpfR[b]
        kb.tt("dve", tmp[:, :], kb.x[:, nb, :], kb.bc["S1"][:, :], ALU.mult, reads=[kb.xR[nb], kb.bcR["S1"]], writes=[tR])
        xmb, xR_ = kb.xmb[b], kb.xmbR[b]
        kb.tt("dve", xmb[:, :], tmp[:, :], kb.bc["Bm"][:, :], ALU.add, reads=[tR, kb.bcR["Bm"]], writes=[xR_])
        if f32_keep is not None:
            f32_keep(nb, tmp, tR)
        for k in range(8):
            kb.tr(kb.pt[:, k * 128:(k + 1) * 128], xmb[:, k * 128:(k + 1) * 128], kb.identb[:, :],
                  reads=[xR_, kb.cR], writes=[kb.pbR[7]], sig=(k == 7))
        kb.cp("act", kb.xmT[:, :, nb * 128:(nb + 1) * 128], kb.pt[:, :].rearrange("p (k t) -> p k t", k=8),
              reads=[kb.pbR[7]], writes=[kb.xmTR[nb]])


def _ffn_up(kb, tls, l, j):
    d = kb.dram
    hTs = [kb.WS[:, i * NFC * T:(i + 1) * NFC * T].rearrange("p (f t) -> p f t", f=NFC) for i in range(2)]
    for pc in range(NFC // 2):
        c0 = pc * 256
        slot, sR = kb.ws.get(("ffn_up", l, j, pc, tls[0].kind, tls[0].seq, tls[0].t0), [
            (lambda t: t[:, 0:2048].rearrange("p (k f) -> p k f", k=8), d["ffn_w_gate"][l, j, :, c0:c0 + 256].rearrange("(k p) f -> p k f", p=128)),
            (lambda t: t[:, 2048:4096].rearrange("p (k f) -> p k f", k=8), d["ffn_w_up"][l, j, :, c0:c0 + 256].rearrange("(k p) f -> p k f", p=128))])
        gv = slot[:, 0:2048].rearrange("p (k f) -> p k f", k=8)
        uv = slot[:, 2048:4096].rearrange("p (k f) -> p k f", k=8)
        for f2 in range(2):
            fc = pc * 2 + f2
            for i, tl in enumerate(tls):
                NT = tl.ntok
                xmT, xr = kb.XT2[i], kb.XT2R[i][:tl.NB]
                pi = i * 2 if len(tls) == 2 else (fc % 2) * 2
                gps, ups = kb.pb[pi], kb.pb[pi + 1]
                for k in range(8):
                    kb.mm(gps[:, 0:NT], gv[:, k, f2 * 128:(f2 + 1) * 128], xmT[:, k, 0:NT], k == 0, k == 7,
                          reads=[sR] + xr, writes=[kb.pbR[pi]], sig=(k == 7))
                for k in range(8):
                    kb.mm(ups[:, 0:NT], uv[:, k, f2 * 128:(f2 + 1) * 128], xmT[:, k, 0:NT], k == 0, k == 7,
                          reads=[sR] + xr, writes=[kb.pbR[pi + 1]], sig=(k == 7))
                sb_ = (fc + i) % 2
                sg, sgR = kb.sgt[sb_], kb.sgtR[sb_]
                kb.act(sg[:, 0:NT], gps[:, 0:NT], AF.Silu, reads=[kb.pbR[pi]], writes=[sgR])
                kb.tt("dve", hTs[i][:, fc, 0:NT], ups[:, 0:NT], sg[:, 0:NT], ALU.mult, reads=[kb.pbR[pi + 1], sgR], writes=[kb.hR2[i][fc]])
    return hTs


def _down_proj(kb, tl, key, KC, part_fn, view_fn, in_fn, evac=None, pre_scaled=False):
    NT, NB = tl.ntok, tl.NB
    for dc in range(8):
        slot, sR = kb.ws.get(key + (dc,), part_fn(dc))
        wv = view_fn(slot)
        b = dc % 2
        acc, aR = kb.pb[4 + b], kb.pbR[4 + b]
        kcs = KC(dc) if callable(KC) else list(range(KC))
        for i, kc in enumerate(kcs):
            ap, rs = in_fn(kc)
            kb.mm(acc[:, 0:NT], wv(kc, dc), ap, i == 0, i == len(kcs) - 1, reads=[sR] + rs, writes=[aR], sig=(i == len(kcs) - 1))
        ysb, yR = kb.ysb[b], kb.ysbR[b]
        if evac is None:
            kb.cp("act", ysb[:, 0:NT], acc[:, 0:NT], reads=[aR], writes=[yR])
        else:
            evac(dc, ysb, yR, acc, aR)
        _down_tail(kb, tl, dc, ysb, yR)


def _down_tail(kb, tl, dc, ysb, yR, b=None):
    NB = tl.NB
    if b is None:
        b = dc % 2
    for nb in range(NB):
        kb.tr(kb.pb[6][:, nb * 128:(nb + 1) * 128], ysb[:, nb * 128:(nb + 1) * 128], kb.identf[:, :],
              reads=[yR, kb.cR], writes=[kb.pbR[6]], sig=(nb == NB - 1))
    t2, t2R = kb.t2[b], kb.t2R[b]
    kb.tt("dve", t2[:, 0:NB, :], kb.pb[6][:, 0:NB * 128].rearrange("p (n c) -> p n c", n=NB),
          kb.bc["G1"][:, dc * 128:(dc + 1) * 128].unsqueeze(1).broadcast_to([128, NB, 128]), ALU.mult,
          reads=[kb.pbR[6], kb.bcR["G1"]], writes=[t2R])
    xv = kb.x[:, 0:NB, dc * 128:(dc + 1) * 128]
    kb.tt("dve", xv, xv, t2[:, 0:NB, :], ALU.add, reads=[t2R] + kb.xR[:NB], writes=kb.xR[:NB])


def _layer_norm(kb, tl):
    eps = LN_EPS / (ALPHA * ALPHA)
    for nb in range(tl.NB):
        xr = kb.xR[nb]
        xb = kb.x[:, nb, :]
        for h in range(2):
            kb.P.op("dve", lambda e, h=h, nb=nb, x=kb.x: e.bn_stats(out=kb.st6[:, h, :], in_=x[:, nb, h * 512:(h + 1) * 512]),
                    reads=[xr], writes=[kb.st6R])
        kb.P.op("dve", lambda e: e.bn_aggr(out=kb.mv[:, 0:2], in_=kb.st6[:, :, :].rearrange("p a b -> p (a b)")),
                reads=[kb.st6R], writes=[kb.mvR])
        kb.act(kb.mv[:, 2:3], kb.mv[:, 1:2], AF.Sqrt, reads=[kb.mvR], writes=[kb.mvR], bias=eps, scale=1.0)
        kb.P.op("dve", lambda e: e.reciprocal(out=kb.mv[:, 3:4], in_=kb.mv[:, 2:3]), reads=[kb.mvR], writes=[kb.mvR])
        kb.stt(kb.mv[:, 4:5], kb.mv[:, 0:1], -1.0, kb.mv[:, 3:4], ALU.mult, ALU.mult, reads=[kb.mvR], writes=[kb.mvR])
        kb.act(xb, xb, AF.Identity, reads=[xr, kb.mvR], writes=[xr], scale=kb.mv[:, 3:4], bias=kb.mv[:, 4:5])
        kb.tt("dve", xb, xb, kb.bc["LNg"][:, :], ALU.mult, reads=[xr, kb.bcR["LNg"]], writes=[xr])
        kb.tt("dve", xb, xb, kb.bc["LNb"][:, :], ALU.add, reads=[xr, kb.bcR["LNb"]], writes=[xr])


def _ffn_sublayer(kb, tls, l, j):
    k = 0 if j == 0 else 2
    d = kb.dram
    _load_post(kb, tls[0], l, k, 0.5)
    for i, tl in enumerate(tls):
        kb.sel(i)
        _modulate(kb, tl)
    kb.next_pre_now()
    hTs = _ffn_up(kb, tls, l, j)
    for dc in range(8):
        slot, sR = kb.ws.get(("ffn_dn", l, j, dc, tls[0].kind, tls[0].seq, tls[0].t0), [
            (lambda t: t[:, 0:NFC * 128].rearrange("p (f c) -> p f c", f=NFC),
             d["ffn_w_down"][l, j, :, dc * 128:(dc + 1) * 128].rearrange("(f p) c -> p f c", p=128))])
        wv = slot[:, 0:NFC * 128].rearrange("p (f c) -> p f c", f=NFC)
        for i, tl in enumerate(tls):
            kb.sel(i)
            NT = tl.ntok
            b = i if len(tls) == 2 else dc % 2
            acc, aR = kb.pb[4 + b], kb.pbR[4 + b]
            for fc in range(NFC):
                kb.mm(acc[:, 0:NT], wv[:, fc, :], hTs[i][:, fc, 0:NT], fc == 0, fc == NFC - 1, reads=[sR, kb.hR2[i][fc]], writes=[aR], sig=(fc == NFC - 1))
            ysb, yR = kb.ysb[b], kb.ysbR[b]
            kb.cp("act", ysb[:, 0:NT], acc[:, 0:NT], reads=[aR], writes=[yR])
            _down_tail(kb, tl, dc, ysb, yR, b)
    for i, tl in enumerate(tls):
        kb.sel(i)
        _layer_norm(kb, tl)


def _load_x(kb, tl):
    d = kb.dram
    if tl.kind == "p":
        src = d["xp"][tl.seq, tl.t0:tl.t0 + tl.ntok, :].rearrange("(nb p) f -> p nb f", p=128)
        kb.P.dma("sp", kb.x[:, 0:tl.NB, :], src, writes=kb.xR[:tl.NB])
    else:
        kb.P.dma("sp", kb.x[:, 0, :], d["xs"][:, :], writes=[kb.xR[0]])


def _store_y(kb, tl):
    d = kb.dram
    if tl.kind == "p":
        dst = d["y_p"][tl.seq, tl.t0:tl.t0 + tl.ntok, :].rearrange("(nb p) f -> p nb f", p=128)
        kb.out_evs += kb.P.dma("sp", dst, kb.x[:, 0:tl.NB, :], reads=kb.xR[:tl.NB])
    else:
        kb.out_evs += kb.P.dma("sp", d["y_s"][:, :], kb.x[:, 0, :], reads=[kb.xR[0]])


def _program(kb):
    cfg = kb.cfg
    if (0 in cfg["layers"] and 1 in cfg.get("subs", [0, 1, 2])) or cfg.get("force_s5pre"):
        _s5_precompute(kb)
    if 1 in cfg["layers"]:
        _gm_setup(kb)
    if 2 in cfg["layers"]:
        _pool_setup(kb)
    if 3 in cfg["layers"]:
        _mla_setup(kb)
    _mod_phase(kb)
    tiles = cfg["tiles"]
    pairs = []
    i = 0
    while i < len(tiles):
        a = tiles[i]
        if (cfg.get("pair", True) and a.kind == "p" and i + 1 < len(tiles) and tiles[i + 1].kind == "p"
                and tiles[i + 1].seq == a.seq and tiles[i + 1].t0 == a.t0 + a.ntok):
            pairs.append([a, tiles[i + 1]])
            i += 2
        else:
            pairs.append([a])
            i += 1
    for tls in pairs:
        for i, tl in enumerate(tls):
            kb.sel(i)
            _load_x(kb, tl)
        subs = [(l, k) for l in cfg["layers"] for k in cfg.get("subs", [0, 1, 2])]
        _load_pre(kb, tls[0], *subs[0])
        for si, (l, k) in enumerate(subs):
            nxt = (lambda si=si: _load_pre(kb, tls[0], *subs[si + 1])) if si + 1 < len(subs) else (lambda: None)
            cnt = [0]

            def counted(nxt=nxt, cnt=cnt, n=len(tls)):
                cnt[0] += 1
                if cnt[0] == n:
                    nxt()
            kb.next_pre = counted
            kb.next_pre_now = nxt
            if k != 1:
                _ffn_sublayer(kb, tls, l, 0 if k == 0 else 1)
            else:
                for i, tl in enumerate(tls):
                    kb.sel(i)
                    MIXERS[l % 4](kb, tl, l)
        for i, tl in enumerate(tls):
            kb.sel(i)
            _store_y(kb, tl)


MIXERS = {}


def build(cfg):
    kb = KB(cfg)
    kb.P.no_self_wait = set(cfg.get("no_self_wait", ["pe"]))
    with kb.es:
        _declare(kb)
        kb.P.dry = True
        _program(kb)
        kb.P.dry = False
        kb.ws.reset()
        kb.out_evs = []
        _program(kb)
        kb.P.wait_all("sp", kb.out_evs)
        kb.P.check()
        kb.P.emit(kb.nc, kb.es)
    return kb


def _s5_declare(kb):
    kb.dscr("s5_xm", [T, D], BF16); kb.dscr("s5_z", [T, D], BF16)
    kb.dscr("s5_BtT", [128, 64, 128], BF16)
    kb.dscr("s5_TgT", [128, 64, 128], BF16)
    kb.dscr("s5_Ct", [128, 32, 2, 128], BF16)
    kb.s5A = kb.sb("s5A", [128, 14, 32])
    kb.s5Dt = kb.sb("s5Dt", [128, 64])
    kb.s5car = kb.sb("s5car", [128, 2, 32])
    kb.s5R = Res("s5consts"); kb.s5carR = Res("s5car"); kb.s5mR = Res("s5mats")
    kb.s5xmR = Res("s5xm"); kb.s5zR = Res("s5z")


def _s5_precompute(kb):
    P, d = kb.P, kb.dram
    W = kb.WS[:, :].bitcast(F32)
    sm = [kb.tmpf[0], kb.tmpf[1]]
    R = Res("s5pre")
    cnt = [0]

    def small():
        i = cnt[0]; cnt[0] += 1
        return sm[i // 32][:, (i % 32) * 32:(i % 32 + 1) * 32]

    def V(eng, fn):
        return P.op(eng, fn, reads=[R], writes=[R])

    def tt(o, a, b, op):
        V("dve", lambda e: e.tensor_tensor(out=o, in0=a, in1=b, op=op))

    def ts(o, a, s1, s2, op0, op1):
        V("dve", lambda e: e.tensor_scalar(out=o, in0=a, scalar1=s1, scalar2=s2, op0=op0, op1=op1))

    def cmul(ore, oim, are, aim, bre, bim, t1, t2):
        tt(t1, are, bre, ALU.mult); tt(t2, aim, bim, ALU.mult); tt(ore, t1, t2, ALU.subtract)
        tt(t1, are, bim, ALU.mult); tt(t2, aim, bre, ALU.mult); tt(oim, t1, t2, ALU.add)

    a_re, a_im, ldt = small(), small(), small()
    nat = kb.ysb[1]
    for j, (nm, dst) in enumerate((("s5_a_re", a_re), ("s5_a_im", a_im))):
        for gh in range(2):
            P.dma("sp", nat[0:32, j * 128 + gh * 64:j * 128 + (gh + 1) * 64], d[nm][0, gh * 32:(gh + 1) * 32, :], writes=[R])
    for gh in range(2):
        hs = slice(gh * 64, (gh + 1) * 64)
        P.dma("sp", ldt[hs, :], d["s5_log_dt"][0:1, gh * 32:(gh + 1) * 32].broadcast_to([64, 32]), writes=[R])
    P.dma("sp", nat[0:64, 256:384].rearrange("g (i c) -> g i c", i=8),
          d["s5_d"][0, :].rearrange("(g c) -> g c", c=16).unsqueeze(1).broadcast_to([64, 8, 16]), writes=[R])
    kb.tr(kb.pb[0][:, 0:32], nat[0:32, 0:128], kb.identf[0:32, 0:32], reads=[R, kb.cR], writes=[kb.pbR[0]])
    kb.tr(kb.pb[0][:, 32:64], nat[0:32, 128:256], kb.identf[0:32, 0:32], reads=[R, kb.cR], writes=[kb.pbR[0]])
    kb.tr(kb.pb[0][:, 64:128], nat[0:64, 256:384], kb.identf[0:64, 0:64], reads=[R, kb.cR], writes=[kb.pbR[0]], sig=True)
    P.op("dve", lambda e: e.tensor_copy(out=a_re, in_=kb.pb[0][:, 0:32]), reads=[kb.pbR[0], R], writes=[R])
    P.op("dve", lambda e: e.tensor_copy(out=a_im, in_=kb.pb[0][:, 32:64]), reads=[kb.pbR[0], R], writes=[R])
    P.op("dve", lambda e: e.tensor_copy(out=kb.s5Dt[:, :], in_=kb.pb[0][:, 64:128]), reads=[kb.pbR[0], R], writes=[R, kb.s5R])
    if kb.cfg.get("s5stop", 99) <= 1:
        return
    off = [0]

    def big(n):
        o = off[0]; off[0] += n
        return W[:, o:o + n]
    Bre, Bim, bbre, bbim, Cre, Cim = [big(512).rearrange("p (g c) -> p g c", c=16) for _ in range(6)]
    for gh in range(2):
        hs = slice(gh * 64, (gh + 1) * 64)
        P.dma("sp", Bre[hs], d["s5_b_re"][0, gh * 32:(gh + 1) * 32].rearrange("g p c -> p g c"), writes=[R])
        P.dma("sp", Bim[hs], d["s5_b_im"][0, gh * 32:(gh + 1) * 32].rearrange("g p c -> p g c"), writes=[R])
    for nm, Ct_ in (("s5_c_re", Cre), ("s5_c_im", Cim)):
        cn = kb.ysb[0][:, :].rearrange("p (t h q) -> p t h q", t=4, h=2)
        for t in range(4):
            for gh in range(2):
                P.dma("sp", cn[:, t, gh, :], d[nm][0, gh * 32 + t * 8:gh * 32 + t * 8 + 8].rearrange("g c p -> (g c) p"), writes=[R])
        for t in range(4):
            kb.tr(kb.pb[0][:, t * 128:(t + 1) * 128], kb.ysb[0][:, t * 128:(t + 1) * 128], kb.identf[:, :], reads=[R, kb.cR], writes=[kb.pbR[0]], sig=(t == 3))
        P.op("dve", lambda e, Ct_=Ct_: e.tensor_copy(out=Ct_, in_=kb.pb[0][:, :].rearrange("p (g c) -> p g c", c=16)), reads=[kb.pbR[0], R], writes=[R])
    if kb.cfg.get("s5stop", 99) <= 2:
        return
    dt, ar, th, rr, t1, t2, cs_, sn_, are, aim = [small() for _ in range(10)]
    V("act", lambda e: e.activation(out=dt, in_=ldt, func=AF.Exp))
    tt(ar, a_re, dt, ALU.mult); tt(th, a_im, dt, ALU.mult)
    V("act", lambda e: e.activation(out=rr, in_=ar, func=AF.Exp))

    def sin_of(o, ang, shift):
        u, k, ki = small(), small(), small()
        ts(u, ang, 1.0 / (2 * math.pi), shift, ALU.mult, ALU.add)
        kint = ki.bitcast(I32)
        V("dve", lambda e: e.tensor_copy(out=kint, in_=u))
        V("dve", lambda e: e.tensor_copy(out=k, in_=kint))
        tt(u, u, k, ALU.subtract)
        ts(u, u, -0.5, 0.5, ALU.max, ALU.min)
        V("act", lambda e: e.activation(out=o, in_=u, func=AF.Sin, scale=2 * math.pi))
    sin_of(sn_, th, 0.0)
    sin_of(cs_, th, 0.25)
    tt(are, rr, cs_, ALU.mult); tt(aim, rr, sn_, ALU.mult)
    if kb.cfg.get("s5stop", 99) <= 3:
        return
    nre, den, qre, qim, t3 = [small() for _ in range(5)]
    ts(nre, are, -1.0, None, ALU.add, ALU.bypass)
    tt(t1, a_re, a_re, ALU.mult); tt(t2, a_im, a_im, ALU.mult); tt(den, t1, t2, ALU.add)
    V("dve", lambda e: e.reciprocal(out=den, in_=den))
    tt(t1, nre, a_re, ALU.mult); tt(t2, aim, a_im, ALU.mult); tt(qre, t1, t2, ALU.add); tt(qre, qre, den, ALU.mult)
    tt(t1, aim, a_re, ALU.mult); tt(t2, nre, a_im, ALU.mult); tt(qim, t1, t2, ALU.subtract); tt(qim, qim, den, ALU.mult)
    T1, T2 = [big(512).rearrange("p (g c) -> p g c", c=16) for _ in range(2)]

    def bc(sm_ap, lo=0, n=32):
        return sm_ap[:, lo:lo + n].unsqueeze(2).broadcast_to([128, n, 16])
    cmul(bbre, bbim, Bre, Bim, bc(qre), bc(qim), T1, T2)
    if kb.cfg.get("s5stop", 99) <= 4:
        return
    pw = [(small(), small()) for _ in range(9)]
    ipw = [(small(), small()) for _ in range(8)]
    for (re_, im_) in (pw[0], ipw[0]):
        V("dve", lambda e, re_=re_: e.memset(re_, 1.0)); V("dve", lambda e, im_=im_: e.memset(im_, 0.0))
    ts3, ts4 = small(), small()
    for l in range(8):
        cmul(pw[l + 1][0], pw[l + 1][1], pw[l][0], pw[l][1], are, aim, ts3, ts4)
    ire, iim, m2 = small(), small(), small()
    tt(t1, are, are, ALU.mult); tt(t2, aim, aim, ALU.mult); tt(m2, t1, t2, ALU.add)
    V("dve", lambda e: e.reciprocal(out=m2, in_=m2))
    tt(ire, are, m2, ALU.mult); tt(iim, aim, m2, ALU.mult); ts(iim, iim, -1.0, None, ALU.mult, ALU.bypass)
    for l in range(7):
        cmul(ipw[l + 1][0], ipw[l + 1][1], ipw[l][0], ipw[l][1], ire, iim, ts3, ts4)
    A = kb.s5A
    P.op("dve", lambda e: e.tensor_copy(out=A[:, 0, :], in_=pw[8][0]), reads=[R], writes=[kb.s5R])
    P.op("dve", lambda e: e.tensor_copy(out=A[:, 1, :], in_=pw[8][1]), reads=[R], writes=[kb.s5R])
    P.op("dve", lambda e: e.tensor_copy(out=A[:, 2, :], in_=pw[8][0]), reads=[R], writes=[kb.s5R])
    P.op("dve", lambda e: e.tensor_copy(out=A[:, 3, :], in_=pw[8][1]), reads=[R], writes=[kb.s5R])
    for s in range(5):
        o = 2 + 2 * s
        P.op("dve", lambda e, o=o: e.tensor_tensor(out=ts3, in0=A[:, o, :], in1=A[:, o, :], op=ALU.mult), reads=[R, kb.s5R], writes=[R])
        P.op("dve", lambda e, o=o: e.tensor_tensor(out=ts4, in0=A[:, o + 1, :], in1=A[:, o + 1, :], op=ALU.mult), reads=[R, kb.s5R], writes=[R])
        P.op("dve", lambda e, o=o: e.tensor_tensor(out=A[:, o + 2, :], in0=ts3, in1=ts4, op=ALU.subtract), reads=[R, kb.s5R], writes=[R, kb.s5R])
        P.op("dve", lambda e, o=o: e.tensor_tensor(out=ts3, in0=A[:, o, :], in1=A[:, o + 1, :], op=ALU.mult), reads=[R, kb.s5R], writes=[R])
        P.op("dve", lambda e, o=o: e.tensor_scalar(out=A[:, o + 3, :], in0=ts3, scalar1=2.0, scalar2=None, op0=ALU.mult, op1=ALU.bypass), reads=[R, kb.s5R], writes=[R, kb.s5R])
    if kb.cfg.get("s5stop", 99) <= 5:
        return
    Cpr = big(8 * 9 * 16).rearrange("p (g l c) -> p g l c", g=8, l=9)
    Cpi = big(8 * 9 * 16).rearrange("p (g l c) -> p g l c", g=8, l=9)
    Bnr_f, Bni_f, Xr_f, Xi_f = [big(8 * 8 * 16) for _ in range(4)]
    Bnr, Bni, Xr, Xi = [a.rearrange("p (g l c) -> p g l c", g=8, l=8) for a in (Bnr_f, Bni_f, Xr_f, Xi_f)]
    U1 = big(128).rearrange("p (g c) -> p g c", c=16); U2 = big(128).rearrange("p (g c) -> p g c", c=16)
    Cbd_r, Cbd_i = big(256), big(256)
    V("dve", lambda e: e.memset(Cbd_r, 0.0)); V("dve", lambda e: e.memset(Cbd_i, 0.0))
    stg = kb.xmb[0]
    stgR = kb.xmbR[0]
    for qd in range(4):
        g0 = qd * 8
        gs = slice(g0, g0 + 8)
        for l in range(9):
            cmul(Cpr[:, :, l, :], Cpi[:, :, l, :], Cre[:, gs, :], Cim[:, gs, :], bc(pw[l][0], g0, 8), bc(pw[l][1], g0, 8), U1, U2)
            ts(Cpi[:, :, l, :], Cpi[:, :, l, :], -1.0, None, ALU.mult, ALU.bypass)
        for i in range(8):
            cmul(Bnr[:, :, i, :], Bni[:, :, i, :], bbre[:, gs, :], bbim[:, gs, :], bc(ipw[i][0], g0, 8), bc(ipw[i][1], g0, 8), U1, U2)
            cmul(Xr[:, :, i, :], Xi[:, :, i, :], bbre[:, gs, :], bbim[:, gs, :], bc(pw[7 - i][0], g0, 8), bc(pw[7 - i][1], g0, 8), U1, U2)
        if kb.cfg.get("s5stop", 99) <= 6:
            return
        for gl in range(8):
            glg = g0 + gl
            P.op("dve", lambda e, gl=gl: e.tensor_copy(out=stg[:, 0:128].rearrange("p (l c) -> p l c", c=16), in_=Cpr[:, gl, 1:9, :]), reads=[R], writes=[stgR])
            P.op("dve", lambda e, gl=gl: e.tensor_copy(out=stg[:, 128:256].rearrange("p (l c) -> p l c", c=16), in_=Cpi[:, gl, 1:9, :]), reads=[R], writes=[stgR])
            P.dma("sp", d["s5_Ct"][:, glg, :, :], stg[:, 0:256].rearrange("p (a b) -> p a b", a=2), reads=[stgR], writes=[kb.s5mR])
            if "bt" in kb.cfg.get("s5skip", ""):
                continue
            kb.tr(kb.pb[1][:, 0:128], Xr_f[:, gl * 128:(gl + 1) * 128], kb.identf[:, :], reads=[R, kb.cR], writes=[kb.pbR[1]], sig=False)
            kb.tr(kb.pb[1][:, 128:256], Xi_f[:, gl * 128:(gl + 1) * 128], kb.identf[:, :], reads=[R, kb.cR], writes=[kb.pbR[1]], sig=True)
            sv = stg[:, 256:512].rearrange("p (h r q) -> p h r q", h=2, r=2)
            P.op("dve", lambda e, sv=sv: e.tensor_copy(out=sv[:, :, 0, :], in_=kb.pb[1][:, 0:128].rearrange("p (h q) -> p h q", h=2)), reads=[kb.pbR[1]], writes=[stgR])
            P.op("dve", lambda e, sv=sv: e.tensor_copy(out=sv[:, :, 1, :], in_=kb.pb[1][:, 128:256].rearrange("p (h q) -> p h q", h=2)), reads=[kb.pbR[1]], writes=[stgR])
            for gh in range(2):
                P.dma("sp", d["s5_BtT"][:, gh * 32 + glg, :], stg[:, 256 + gh * 128:256 + (gh + 1) * 128], reads=[stgR], writes=[kb.s5mR])
            if "tg" in kb.cfg.get("s5skip", ""):
                continue
            for gh in range(2):
                hs = slice(gh * 64, (gh + 1) * 64)
                tt(Cbd_r[hs, gh * 128:(gh + 1) * 128].rearrange("p (l c) -> p l c", c=16), Cpr[hs, gl, 0:8, :], Cpr[hs, gl, 0:8, :], ALU.bypass)
                tt(Cbd_i[hs, gh * 128:(gh + 1) * 128].rearrange("p (l c) -> p l c", c=16), Cpi[hs, gl, 0:8, :], Cpi[hs, gl, 0:8, :], ALU.bypass)
            kb.mm(kb.pb[2][:, 0:256], Bnr_f[:, gl * 128:(gl + 1) * 128], Cbd_r[:, :], True, False, reads=[R], writes=[kb.pbR[2]])
            kb.mm(kb.pb[2][:, 0:256], Bni_f[:, gl * 128:(gl + 1) * 128], Cbd_i[:, :], False, True, reads=[R], writes=[kb.pbR[2]], sig=True)
            for gh in range(2):
                P.op("dve", lambda e, gh=gh: e.tensor_tensor(out=stg[:, 512 + gh * 128:512 + (gh + 1) * 128], in0=kb.pb[2][:, gh * 128:(gh + 1) * 128],
                                                             in1=kb.tmask[:, :], op=ALU.mult), reads=[kb.pbR[2], kb.cR], writes=[stgR])
                P.dma("sp", d["s5_TgT"][:, gh * 32 + glg, :], stg[:, 512 + gh * 128:512 + (gh + 1) * 128], reads=[stgR], writes=[kb.s5mR])
    kb.barrier("dve", [R], kb.hR + kb.tmpfR + kb.xmbR + kb.ysbR + kb.sgtR + kb.xmTR)


def _barrier(kb, eng, src, dst):
    kb.P.op(eng, lambda e: e.memset(kb.dummy[:, 0:2], 0.0), reads=[], writes=list(src) + list(dst))


KB.barrier = _barrier


def _s5_mixer(kb, tl, l):
    P, d = kb.P, kb.dram
    NT, NB = tl.ntok, tl.NB
    NCH = NT // 8
    _load_post(kb, tl, l, 1, 1.0)
    W = kb.WS
    XC = W[:, 0:8192].rearrange("p (i f) -> p i f", i=8)
    Uall = W[:, 8192:12288].rearrange("p (g j) -> p g j", g=64)
    Wf = W[:, :].bitcast(F32)
    Vre = Wf[:, 6144:8192].rearrange("p (g j) -> p g j", g=32)
    Vim = Wf[:, 8192:10240].rearrange("p (g j) -> p g j", g=32)
    tB = Wf[:, 10240:12288].rearrange("p (g j) -> p g j", g=32)
    tA = kb.xmT[:, :, :].rearrange("p k t -> p (k t)").bitcast(F32).rearrange("p (g j) -> p g j", g=32)
    XCp = W[:, 12288:20480]
    XCpR = Res("XCp")
    XCR, UR, VreR, VimR, tAR, tBR, ZR = [Res(n) for n in ("XC", "Uall", "Vre", "Vim", "tA", "tB", "Zall")]
    kb.barrier("dve", kb.hR + kb.xmTR, [XCR, UR, VreR, VimR, tAR, tBR, ZR, XCpR])
    for nb in range(NB):
        b = nb % 2
        tmp, tR = kb.tmpf[b], kb.tmpfR[b]
        kb.tt("dve", tmp[:, :], kb.x[:, nb, :], kb.bc["S1"][:, :], ALU.mult, reads=[kb.xR[nb], kb.bcR["S1"]], writes=[tR])
        xmb, xR_ = kb.xmb[b], kb.xmbR[b]
        kb.tt("dve", xmb[:, :], tmp[:, :], kb.bc["Bm"][:, :], ALU.add, reads=[tR, kb.bcR["Bm"]], writes=[xR_])
        P.dma("sp", d["s5_xm"][nb * 128:(nb + 1) * 128, :], xmb[:, :], reads=[xR_], writes=[kb.s5xmR])
    kb.next_pre()
    P.dma("sp", XC[0:NCH, :, :], d["s5_xm"][0:NT, :].rearrange("(j i) f -> j i f", i=8), reads=[kb.s5xmR], writes=[XCR])
    for hh in range(2):
        kb.cp("dve" if hh == 0 else "act", XCp[0:NCH, :].rearrange("p (g i c) -> p i g c", g=64, i=8)[:, :, hh * 32:(hh + 1) * 32, :],
              XC[0:NCH, :, :].rearrange("p i (g c) -> p i g c", c=16)[:, :, hh * 32:(hh + 1) * 32, :], reads=[XCR], writes=[XCpR])
    for gb in range(4):
        for gg in range(16):
            g = gb * 16 + gg
            kb.tr(kb.pt[:, gg * NCH:(gg + 1) * NCH], XCp[0:NCH, g * 128:(g + 1) * 128], kb.identb[0:NCH, 0:NCH],
                  reads=[XCpR, kb.cR], writes=[kb.pbR[7]], sig=(gg == 15))
        kb.cp("act", Uall[:, gb * 16:(gb + 1) * 16, 0:NCH], kb.pt[:, 0:16 * NCH].rearrange("p (g j) -> p g j", g=16),
              reads=[kb.pbR[7]], writes=[UR])
    kb.barrier("dve", [XCpR], [VreR, VimR])
    for gb in range(4):
        parts = []
        for gh in range(2):
            g0 = gh * 32 + gb * 8
            parts.append((lambda t, gh=gh: t[:, gh * 1024:(gh + 1) * 1024].rearrange("p (g m) -> p g m", g=8), d["s5_BtT"][:, g0:g0 + 8, :]))
        slot, sR = kb.ws.get(("s5B", tl.kind, tl.seq, tl.t0, gb), parts, reads=[kb.s5mR])
        if not P.dry and gb == 0:
            pass
        for ri in range(2):
            bank = kb.pb[ri]
            for gl8 in range(8):
                for gh in range(2):
                    bt = slot[:, gh * 1024 + gl8 * 128 + ri * 64: gh * 1024 + gl8 * 128 + ri * 64 + 64]
                    g = gh * 32 + gb * 8 + gl8
                    kb.mm(bank[gh * 64:(gh + 1) * 64, gl8 * NCH:(gl8 + 1) * NCH], bt, Uall[:, g, 0:NCH], True, True,
                          reads=[sR, UR, kb.s5mR], writes=[kb.pbR[ri]], sig=(gl8 == 7 and gh == 1))
            dst, dR = (Vre, VreR) if ri == 0 else (Vim, VimR)
            kb.cp("act", dst[:, gb * 8:(gb + 1) * 8, 0:NCH], bank[:, 0:8 * NCH].rearrange("p (g j) -> p g j", g=8),
                  reads=[kb.pbR[ri]], writes=[dR])
    A = kb.s5A

    def Ab(idx, n):
        return A[:, idx, :].unsqueeze(2).broadcast_to([128, 32, n])
    Sp = tB.rearrange("p g j -> p (g j)").bitcast(BF16)
    Spre = Sp[:, 0:2048].rearrange("p (g j) -> p g j", g=32)
    Spim = Sp[:, 2048:4096].rearrange("p (g j) -> p g j", g=32)
    SpR = Res("Sp")
    if tl.kind == "p":
        car = kb.s5car
        if tl.t0 == 0:
            kb.memset("dve", car[:, :, :], 0.0, writes=[kb.s5carR])
        c_re, c_im = car[:, 0, :].unsqueeze(2), car[:, 1, :].unsqueeze(2)
        a1, a2 = tA[:, :, 0:1], tA[:, :, 1:2]
        kb.tt("dve", a1, c_re, Ab(0, 1), ALU.mult, reads=[kb.s5carR, kb.s5R], writes=[tAR])
        kb.tt("dve", a2, c_im, Ab(1, 1), ALU.mult, reads=[kb.s5carR, kb.s5R], writes=[tAR])
        kb.tt("dve", a1, a1, a2, ALU.subtract, reads=[tAR], writes=[tAR])
        kb.tt("dve", Vre[:, :, 0:1], Vre[:, :, 0:1], a1, ALU.add, reads=[tAR, VreR], writes=[VreR])
        kb.tt("dve", a1, c_re, Ab(1, 1), ALU.mult, reads=[kb.s5carR, kb.s5R], writes=[tAR])
        kb.tt("dve", a2, c_im, Ab(0, 1), ALU.mult, reads=[kb.s5carR, kb.s5R], writes=[tAR])
        kb.tt("dve", a1, a1, a2, ALU.add, reads=[tAR], writes=[tAR])
        kb.tt("dve", Vim[:, :, 0:1], Vim[:, :, 0:1], a1, ALU.add, reads=[tAR, VimR], writes=[VimR])
        s = 1
        k = 0
        while s < NCH:
            n = NCH - s
            ar_, ai_ = Ab(2 + 2 * k, n), Ab(3 + 2 * k, n)
            re_sh, im_sh = Vre[:, :, 0:n], Vim[:, :, 0:n]
            TA, TB = tA[:, :, 0:n], tB[:, :, 0:n]
            kb.tt("dve", TA, re_sh, ar_, ALU.mult, reads=[VreR, kb.s5R], writes=[tAR])
            kb.tt("dve", TB, im_sh, ai_, ALU.mult, reads=[VimR, kb.s5R], writes=[tBR])
            kb.tt("dve", TA, TA, TB, ALU.subtract, reads=[tAR, tBR], writes=[tAR])
            kb.tt("dve", TB, re_sh, ai_, ALU.mult, reads=[VreR, kb.s5R, tAR], writes=[tBR])
            kb.tt("dve", Vre[:, :, s:NCH], Vre[:, :, s:NCH], TA, ALU.add, reads=[tAR, tBR], writes=[VreR])
            kb.tt("dve", TA, im_sh, ar_, ALU.mult, reads=[VimR, kb.s5R], writes=[tAR])
            kb.tt("dve", TB, TB, TA, ALU.add, reads=[tAR], writes=[tBR])
            kb.tt("dve", Vim[:, :, s:NCH], Vim[:, :, s:NCH], TB, ALU.add, reads=[tBR], writes=[VimR])
            s *= 2
            k += 1
        kb.barrier("dve", [tBR], [SpR])
        kb.cp("dve", Spre[:, :, 0:1], c_re, reads=[kb.s5carR, tBR], writes=[SpR])
        kb.cp("dve", Spim[:, :, 0:1], c_im, reads=[kb.s5carR], writes=[SpR])
        kb.cp("dve", Spre[:, :, 1:NCH], Vre[:, :, 0:NCH - 1], reads=[VreR], writes=[SpR])
        kb.cp("act", Spim[:, :, 1:NCH], Vim[:, :, 0:NCH - 1], reads=[VimR], writes=[SpR])
        kb.cp("dve", car[:, 0, :].unsqueeze(2), Vre[:, :, NCH - 1:NCH], reads=[VreR, SpR], writes=[kb.s5carR])
        kb.cp("dve", car[:, 1, :].unsqueeze(2), Vim[:, :, NCH - 1:NCH], reads=[VimR], writes=[kb.s5carR])
        if tl.t0 + NT == SEQ:
            for ri, nm in ((0, "ssm_re_p"), (1, "ssm_im_p")):
                kb.tr(kb.pb[2][0:32, ri * 128:(ri + 1) * 128], car[:, ri, :], kb.identf[:, :], reads=[kb.s5carR, kb.cR], writes=[kb.pbR[2]], sig=(ri == 1))
            ot = kb.tmpf[0]; otR = kb.tmpfR[0]
            kb.cp("act", ot[0:32, 0:256], kb.pb[2][0:32, 0:256], reads=[kb.pbR[2]], writes=[otR])
            for ri, nm in ((0, "ssm_re_p"), (1, "ssm_im_p")):
                for gh in range(2):
                    kb.out_evs += P.dma("sp", d[nm][tl.seq, gh * 32:(gh + 1) * 32, :],
                                        ot[0:32, ri * 128 + gh * 64:ri * 128 + (gh + 1) * 64], reads=[otR])
    else:
        sin = kb.ysb[0]; sinR = kb.ysbR[0]
        Sin_re = tA[:, :, 0:NCH]; Sin_im = tA[:, :, NCH:2 * NCH]
        for ri, nm in ((0, "st_re"), (1, "st_im")):
            dstS = Sin_re if ri == 0 else Sin_im
            nat = kb.tmpf[ri]; natR = kb.tmpfR[ri]
            for rb in range(4):
                nv = nat[0:NSS, :].rearrange("q (g h p) -> q h g p", g=8, h=2)
                P.dma_group("sp", [(nv[:, gh, :, :], d[nm][:, gh * 32 + rb * 8:gh * 32 + rb * 8 + 8, :]) for gh in range(2)], writes=[natR])
                for gl8 in range(8):
                    kb.tr(kb.pb[2][:, gl8 * NSS:(gl8 + 1) * NSS], nat[0:NSS, gl8 * 128:(gl8 + 1) * 128], kb.identf[0:NSS, 0:NSS],
                          reads=[natR, kb.cR], writes=[kb.pbR[2]], sig=(gl8 == 7))
                kb.cp("act", dstS[:, rb * 8:(rb + 1) * 8, :], kb.pb[2][:, 0:8 * NSS].rearrange("p (g q) -> p g q", g=8),
                      reads=[kb.pbR[2]], writes=[tAR])
        kb.barrier("dve", [tBR], [SpR])
        kb.cp("dve", Spre[:, :, 0:NCH], Sin_re, reads=[tAR, tBR], writes=[SpR])
        kb.cp("dve", Spim[:, :, 0:NCH], Sin_im, reads=[tAR], writes=[SpR])
        X1, X2 = tA[:, :, 2 * NCH:3 * NCH], tA[:, :, 3 * NCH:4 * NCH]
        ar_, ai_ = Ab(0, NCH), Ab(1, NCH)
        kb.tt("dve", X1, Sin_re, ar_, ALU.mult, reads=[tAR, kb.s5R], writes=[tAR])
        kb.tt("dve", X2, Sin_im, ai_, ALU.mult, reads=[tAR, kb.s5R], writes=[tAR])
        kb.tt("dve", X1, X1, X2, ALU.subtract, reads=[tAR], writes=[tAR])
        kb.tt("dve", Vre[:, :, 0:NCH], Vre[:, :, 0:NCH], X1, ALU.add, reads=[tAR, VreR], writes=[VreR])
        kb.tt("dve", X1, Sin_re, ai_, ALU.mult, reads=[tAR, kb.s5R], writes=[tAR])
        kb.tt("dve", X2, Sin_im, ar_, ALU.mult, reads=[tAR, kb.s5R], writes=[tAR])
        kb.tt("dve", X1, X1, X2, ALU.add, reads=[tAR], writes=[tAR])
        kb.tt("dve", Vim[:, :, 0:NCH], Vim[:, :, 0:NCH], X1, ALU.add, reads=[tAR, VimR], writes=[VimR])
        for ri, nm in ((0, "ssm_re_s"), (1, "ssm_im_s")):
            srcS, sR_ = (Vre, VreR) if ri == 0 else (Vim, VimR)
            ot = kb.tmpf[ri]; otR = kb.tmpfR[ri]
            for r4 in range(8):
                for g4 in range(4):
                    kb.tr(kb.pb[3][0:NSS, g4 * 128:(g4 + 1) * 128], srcS[:, r4 * 4 + g4, 0:NCH], kb.identf[:, :],
                          reads=[sR_, kb.cR], writes=[kb.pbR[3]], sig=(g4 == 3))
                kb.cp("act", ot[0:NSS, 0:512].rearrange("q (h g p) -> q g h p", h=2, g=4),
                      kb.pb[3][0:NSS, :].rearrange("q (g h p) -> q g h p", g=4, h=2), reads=[kb.pbR[3]], writes=[otR])
                for gh in range(2):
                    kb.out_evs += P.dma("sp", d[nm][:, gh * 32 + r4 * 4:gh * 32 + r4 * 4 + 4, :].rearrange("q g p -> q (g p)"),
                                        ot[0:NSS, gh * 256:(gh + 1) * 256], reads=[otR])
    for gb in range(8):
        g0 = gb * 8
        gh = g0 // 32
        gl0 = g0 % 32
        hs = slice(gh * 64, (gh + 1) * 64)
        slot, sR = kb.ws.get(("s5T", tl.kind, tl.seq, tl.t0, gb), [
            (lambda t: t[:, 0:1024].rearrange("p (g m) -> p g m", g=8), d["s5_TgT"][:, g0:g0 + 8, :]),
            (lambda t, hs=hs: t[hs, 1024:3072].rearrange("p (g a m) -> p g a m", g=8, a=2), d["s5_Ct"][hs, gl0:gl0 + 8, :, :])], reads=[kb.s5mR])
        bank = kb.pb[gb % 2]
        bR = kb.pbR[gb % 2]
        for g8 in range(8):
            g = g0 + g8
            gl = gl0 + g8
            o = bank[:, g8 * NCH:(g8 + 1) * NCH]
            kb.mm(o, slot[:, g8 * 128:(g8 + 1) * 128], Uall[:, g, 0:NCH], True, False, reads=[sR, UR, kb.s5mR], writes=[bR])
            kb.mm(o, slot[hs, 1024 + g8 * 256:1024 + g8 * 256 + 128], Spre[hs, gl, 0:NCH], False, False, reads=[sR, SpR], writes=[bR])
            kb.mm(o, slot[hs, 1024 + g8 * 256 + 128:1024 + g8 * 256 + 256], Spim[hs, gl, 0:NCH], False, True, reads=[sR, SpR], writes=[bR], sig=(g8 == 7))
        tmp = kb.ysb[gb % 2][:, 0:8 * NCH].rearrange("p (g j) -> p g j", g=8)
        tR = kb.ysbR[gb % 2]
        Ug = Uall[:, g0:g0 + 8, 0:NCH]
        kb.tt("dve", tmp, Ug, kb.s5Dt[:, g0:g0 + 8].unsqueeze(2).broadcast_to([128, 8, NCH]), ALU.mult, reads=[UR, kb.s5R], writes=[tR])
        kb.tt("dve", tmp, tmp, bank[:, 0:8 * NCH].rearrange("p (g j) -> p g j", g=8), ALU.add, reads=[tR, bR], writes=[tR])
        kb.act(Ug, tmp, AF.Gelu_apprx_tanh, reads=[tR, UR], writes=[UR])
    for gb in range(8):
        for g8 in range(8):
            g = gb * 8 + g8
            kb.tr(kb.pt[0:NCH, g8 * 128:(g8 + 1) * 128], Uall[:, g, 0:NCH], kb.identb[:, :], reads=[UR, kb.cR], writes=[kb.pbR[7]], sig=(g8 == 7))
        kb.cp("act", XCp[0:NCH, gb * 1024:(gb + 1) * 1024], kb.pt[0:NCH, :], reads=[kb.pbR[7], VreR, VimR], writes=[XCpR])
    for hh in range(2):
        kb.cp("dve" if hh == 0 else "act", XC[0:NCH, :, :].rearrange("p i (g c) -> p i g c", c=16)[:, :, hh * 32:(hh + 1) * 32, :],
              XCp[0:NCH, :].rearrange("p (g i c) -> p i g c", g=64, i=8)[:, :, hh * 32:(hh + 1) * 32, :], reads=[XCpR], writes=[XCR])
    P.dma("sp", d["s5_z"][0:NT, :].rearrange("(j i) f -> j i f", i=8), XC[0:NCH, :, :], reads=[XCR], writes=[kb.s5zR])
    for nb in range(NB):
        b = nb % 2
        zb, zR = kb.xmb[b], kb.xmbR[b]
        P.dma("sp", zb[:, :], d["s5_z"][nb * 128:(nb + 1) * 128, :], reads=[kb.s5zR], writes=[zR])
        for k in range(8):
            kb.tr(kb.pt[:, k * 128:(k + 1) * 128], zb[:, k * 128:(k + 1) * 128], kb.identb[:, :], reads=[zR, kb.cR, tAR, SpR], writes=[kb.pbR[7]], sig=(k == 7))
        kb.cp("act", kb.xmT[:, :, nb * 128:(nb + 1) * 128], kb.pt[:, :].rearrange("p (k t) -> p k t", k=8),
              reads=[kb.pbR[7], tAR], writes=[kb.xmTR[nb]])
    xr = kb.xmTR[:NB]

    def evac(dc, ysb, yR, acc, aR):
        pass
    for dc in range(8):
        slot, sR = kb.ws.get(("s5o", tl.kind, tl.seq, tl.t0, dc), [
            (lambda t: t[:, 0:1024].rearrange("p (k c) -> p k c", k=8), d["s5_w_out"][0, :, dc * 128:(dc + 1) * 128].rearrange("(k p) c -> p k c", p=128)),
            (lambda t: t[:, 1024:2048].rearrange("p (k c) -> p k c", k=8), d["s5_w_gate"][0, :, dc * 128:(dc + 1) * 128].rearrange("(k p) c -> p k c", p=128))])
        b = dc % 2
        acc, aR = kb.pb[4 + b], kb.pbR[4 + b]
        acg, agR = kb.pb[2 + b], kb.pbR[2 + b]
        for k in range(8):
            kb.mm(acc[:, 0:NT], slot[:, k * 128:(k + 1) * 128], kb.xmT[:, k, 0:NT], k == 0, k == 7, reads=[sR] + xr, writes=[aR], sig=(k == 7))
        for k in range(8):
            kb.mm(acg[:, 0:NT], slot[:, 1024 + k * 128:1024 + (k + 1) * 128], kb.xmT[:, k, 0:NT], k == 0, k == 7, reads=[sR] + xr, writes=[agR], sig=(k == 7))
        sg, sgR = kb.tmpf[b], kb.tmpfR[b]
        kb.act(sg[:, 0:NT], acg[:, 0:NT], AF.Sigmoid, reads=[agR], writes=[sgR])
        ysb, yR = kb.ysb[b], kb.ysbR[b]
        kb.tt("dve", ysb[:, 0:NT], acc[:, 0:NT], sg[:, 0:NT], ALU.mult, reads=[aR, sgR], writes=[yR])
        _down_tail(kb, tl, dc, ysb, yR)
    _layer_norm(kb, tl)
    kb.barrier("dve", [XCR, UR, VreR, VimR, tAR, tBR, ZR, SpR, XCpR], kb.hR + kb.xmTR)


MIXERS[0] = _s5_mixer


def _ln_stats(kb, src_fn, nb_res, eps):
    for h in range(2):
        kb.P.op("dve", lambda e, h=h: e.bn_stats(out=kb.st6[:, h, :], in_=src_fn(h)), reads=[nb_res], writes=[kb.st6R])
    kb.P.op("dve", lambda e: e.bn_aggr(out=kb.mv[:, 0:2], in_=kb.st6[:, :, :].rearrange("p a b -> p (a b)")),
            reads=[kb.st6R], writes=[kb.mvR])
    kb.act(kb.mv[:, 2:3], kb.mv[:, 1:2], AF.Sqrt, reads=[kb.mvR], writes=[kb.mvR], bias=eps, scale=1.0)
    kb.P.op("dve", lambda e: e.reciprocal(out=kb.mv[:, 3:4], in_=kb.mv[:, 2:3]), reads=[kb.mvR], writes=[kb.mvR])
    kb.stt(kb.mv[:, 4:5], kb.mv[:, 0:1], -1.0, kb.mv[:, 3:4], ALU.mult, ALU.mult, reads=[kb.mvR], writes=[kb.mvR])


def _to_featT(kb, tl, nb, src, srcR):
    for k in range(8):
        kb.tr(kb.pt[:, k * 128:(k + 1) * 128], src[:, k * 128:(k + 1) * 128], kb.identb[:, :],
              reads=[srcR, kb.cR], writes=[kb.pbR[7]], sig=(k == 7))
    kb.cp("act", kb.xmT[:, :, nb * 128:(nb + 1) * 128], kb.pt[:, :].rearrange("p (k t) -> p k t", k=8),
          reads=[kb.pbR[7]], writes=[kb.xmTR[nb]])


def _gm_declare(kb):
    kb.din("k_tri", [128, 128])
    kb.gm_wsT = kb.sb("gm_wsT", [128, 8, 128], BF16)
    kb.gm_wsTs = kb.sb("gm_wsTs", [128, 8, 128], BF16)
    kb.gm_bs = kb.sb("gm_bs", [128, 16])
    kb.gmR = Res("gm_consts")


def _gm_setup(kb):
    P, d = kb.P, kb.dram
    tri = kb.tmpf[1][:, 0:128]; triR = kb.tmpfR[1]
    P.dma("sp", tri, d["k_tri"][:, :], writes=[triR])
    nat = kb.tmpf[0]; natR = kb.tmpfR[0]
    for var in range(2):
        natv = nat[:, :].rearrange("p (h s) -> p h s", h=8)
        if var == 0:
            P.dma("sp", natv, d["gm_w_s"][0].rearrange("h t s -> t h s"), writes=[natR])
        else:
            kb.memset("dve", nat[:, :], 0.0, writes=[natR])
            P.dma_group("sp", [(natv[q * 8:(q + 1) * 8, :, q * 8:(q + 1) * 8], d["gm_w_s"][0, :, 0:8, 0:8].rearrange("h i j -> i h j"))
                               for q in range(NSS)], writes=[natR])
        dst = kb.gm_wsT if var == 0 else kb.gm_wsTs
        for hb in range(2):
            for h4 in range(4):
                h = hb * 4 + h4
                kb.tr(kb.pb[0][:, h4 * 128:(h4 + 1) * 128], nat[:, h * 128:(h + 1) * 128], kb.identf[:, :],
                      reads=[natR, kb.cR], writes=[kb.pbR[0]], sig=(h4 == 3))
            kb.tt("dve", dst[:, hb * 4:(hb + 1) * 4, :], kb.pb[0][:, :].rearrange("p (h t) -> p h t", h=4),
                  tri.unsqueeze(1).broadcast_to([128, 4, 128]), ALU.mult, reads=[kb.pbR[0], triR], writes=[kb.gmR])
    bn = kb.ysb[0]; bnR = kb.ysbR[0]
    P.dma("sp", bn[0:8, 0:128], d["gm_b_s"][0, :, :], writes=[bnR])
    P.dma("sp", bn[0:8, 128:256].rearrange("h (q i) -> h q i", q=NSS), d["gm_b_s"][0, :, 0:8].unsqueeze(1).broadcast_to([8, NSS, 8]), writes=[bnR])
    for v in range(2):
        kb.tr(kb.pb[1][:, v * 8:(v + 1) * 8], bn[0:8, v * 128:(v + 1) * 128], kb.identf[0:8, 0:8], reads=[bnR, kb.cR], writes=[kb.pbR[1]], sig=(v == 1))
    kb.cp("dve", kb.gm_bs[:, 0:16], kb.pb[1][:, 0:16], reads=[kb.pbR[1]], writes=[kb.gmR])
    kb.barrier("dve", [natR, triR, bnR], [])


def _gm_mixer(kb, tl, l):
    P, d = kb.P, kb.dram
    NT, NB = tl.ntok, tl.NB
    NBM = T // 128
    _load_post(kb, tl, l, 1, 1.0)
    _modulate(kb, tl)
    kb.next_pre()
    W = kb.WS
    Wf = W[:, :].bitcast(F32)
    U = W[:, 0:NBM * D].rearrange("p (n f) -> p n f", n=NBM)
    Vf = Wf[:, 2048:2048 + NBM * D].rearrange("p (n f) -> p n f", n=NBM)
    Vn = W[:, 12288:12288 + NBM * D].rearrange("p (n f) -> p n f", n=NBM)
    Gg = Wf[:, 8192:9216]; Gb = Wf[:, 9216:10240]
    UR = [Res(f"gmU{i}") for i in range(NB)]; VR = [Res(f"gmV{i}") for i in range(NB)]; VnR = [Res(f"gmVn{i}") for i in range(NB)]
    GR = Res("gmG")
    kb.barrier("dve", kb.hR, UR + VR + VnR + [GR])
    P.dma("sp", Gg, d["gm_ln_g"][0:1, :].broadcast_to([128, D]), writes=[GR])
    P.dma("sp", Gb, d["gm_ln_b"][0:1, :].broadcast_to([128, D]), writes=[GR])
    for cb in range(4):
        slot, sR = kb.ws.get(("gm_in", tl.kind, tl.seq, tl.t0, cb), [
            (lambda t: t[:, 0:4096].rearrange("p (k f) -> p k f", k=8), d["gm_w_in"][0, :, cb * 512:(cb + 1) * 512].rearrange("(k p) f -> p k f", p=128))])
        sv = slot[:, 0:4096].rearrange("p (k f) -> p k f", k=8)
        for nb in range(NB):
            bank, bR = kb.pb[(cb * NB + nb) % 4], kb.pbR[(cb * NB + nb) % 4]
            for k in range(8):
                kb.mm(bank[:, :], kb.xmT[:, k, nb * 128:(nb + 1) * 128], sv[:, k, :], k == 0, k == 7, reads=[sR, kb.xmTR[nb]], writes=[bR], sig=(k == 7))
            if cb < 2:
                kb.act(U[:, nb, cb * 512:(cb + 1) * 512], bank[:, :], AF.Gelu_apprx_tanh, reads=[bR], writes=[UR[nb]])
            else:
                kb.act(Vf[:, nb, (cb - 2) * 512:(cb - 1) * 512], bank[:, :], AF.Gelu_apprx_tanh, reads=[bR], writes=[VR[nb]])
    for nb in range(NB):
        vb = Vf[:, nb, :]
        _ln_stats(kb, lambda h, nb=nb: Vf[:, nb, h * 512:(h + 1) * 512], VR[nb], LN_EPS)
        kb.act(vb, vb, AF.Identity, reads=[VR[nb], kb.mvR], writes=[VR[nb]], scale=kb.mv[:, 3:4], bias=kb.mv[:, 4:5])
        kb.tt("dve", vb, vb, Gg, ALU.mult, reads=[VR[nb], GR], writes=[VR[nb]])
        kb.tt("dve", vb, vb, Gb, ALU.add, reads=[VR[nb], GR], writes=[VR[nb]])
        kb.cp("act", Vn[:, nb, :], vb, reads=[VR[nb]], writes=[VnR[nb]])
        if tl.kind == "s":
            kb.out_evs += P.dma("sp", d["gm_v_s"][:, :], vb, reads=[VR[nb]])
    wsT = kb.gm_wsT if tl.kind == "p" else kb.gm_wsTs
    bo = 0 if tl.kind == "p" else 8
    for nb in range(NB):
        g, gR = kb.xmb[nb % 2], kb.xmbR[nb % 2]
        for hb in range(2):
            bank, bR = kb.pb[4 + hb], kb.pbR[4 + hb]
            for h4 in range(4):
                h = hb * 4 + h4
                kb.mm(bank[:, h4 * 128:(h4 + 1) * 128], wsT[:, h, :], Vn[:, nb, h * 128:(h + 1) * 128], True, True,
                      reads=[kb.gmR, VnR[nb]], writes=[bR], sig=(h4 == 3))
            for h4 in range(4):
                h = hb * 4 + h4
                kb.stt(g[:, h * 128:(h + 1) * 128], bank[:, h4 * 128:(h4 + 1) * 128], kb.gm_bs[:, bo + h:bo + h + 1],
                       U[:, nb, h * 128:(h + 1) * 128], ALU.add, ALU.mult, reads=[bR, kb.gmR, UR[nb]], writes=[gR])
        _to_featT(kb, tl, nb, g, gR)
    _down_proj(kb, tl, ("gm_out", tl.kind, tl.seq, tl.t0), 8,
               lambda dc: [(lambda t: t[:, 0:1024].rearrange("p (k c) -> p k c", k=8),
                            d["gm_w_out"][0, :, dc * 128:(dc + 1) * 128].rearrange("(k p) c -> p k c", p=128))],
               lambda slot: (lambda kc, dc, v=slot[:, 0:1024].rearrange("p (k c) -> p k c", k=8): v[:, kc, :]),
               lambda kc: (kb.xmT[:, kc, 0:NT], kb.xmTR[:NB]))
    _layer_norm(kb, tl)
    kb.barrier("dve", UR + VR + VnR + [GR], kb.hR)


MIXERS[1] = _gm_mixer


def _pool_declare(kb):
    kb.din("k_pm", [5, 128, 4, 128])
    kb.din("k_prc", [128, 8])
    kb.pm = kb.sb("pm", [128, 5, 512], BF16)
    kb.prc = kb.sb("prc", [128, 8])
    kb.pscT = kb.sb("pscT", [128, 8])
    kb.pprev = kb.sb("pprev", [128, D], BF16)
    kb.pprevR = Res("pprev"); kb.plR = Res("pool_consts")


def _pool_setup(kb):
    P, d = kb.P, kb.dram
    P.dma("pool", kb.pm[:, :, :].rearrange("p a (g t) -> p a g t", g=4), d["k_pm"].rearrange("a s g t -> s a g t"), writes=[kb.plR])
    P.dma("sp", kb.prc[:, :], d["k_prc"][:, :], writes=[kb.plR])
    nat = kb.ysb[0]; natR = kb.ysbR[0]
    P.dma("sp", nat[0:8, 0:128], d["pool_scale"][0, :].rearrange("(k p) -> k p", p=128), writes=[natR])
    kb.tr(kb.pb[1][:, 0:8], nat[0:8, 0:128], kb.identf[0:8, 0:8], reads=[natR, kb.cR], writes=[kb.pbR[1]], sig=True)
    kb.cp("dve", kb.pscT[:, :], kb.pb[1][:, 0:8], reads=[kb.pbR[1]], writes=[kb.plR])
    kb.barrier("dve", [natR], [])


def _pool_mixer(kb, tl, l):
    P, d = kb.P, kb.dram
    NT, NB = tl.ntok, tl.NB
    NBM = T // 128
    _load_post(kb, tl, l, 1, 1.0)
    W = kb.WS
    XB = W[:, 0:NBM * D].rearrange("p (n f) -> p n f", n=NBM)
    XBR = [Res(f"plX{i}") for i in range(NB)]
    SPt = [W[:, 4096 + i * 1024:4096 + (i + 1) * 1024] for i in range(2)]
    SPR = Res("plSP")
    Wf = W[:, :].bitcast(F32)
    SPf = [Wf[:, 4096 + i * 1024:4096 + (i + 1) * 1024] for i in range(2)]
    kb.barrier("dve", kb.hR, XBR + [SPR])
    last_seq_blk = (tl.kind == "p" and tl.t0 + NT == SEQ)
    for nb in range(NB):
        b = nb % 2
        tmp, tR = kb.tmpf[b], kb.tmpfR[b]
        kb.tt("dve", tmp[:, :], kb.x[:, nb, :], kb.bc["S1"][:, :], ALU.mult, reads=[kb.xR[nb], kb.bcR["S1"]], writes=[tR])
        kb.tt("dve", XB[:, nb, :], tmp[:, :], kb.bc["Bm"][:, :], ALU.add, reads=[tR, kb.bcR["Bm"]], writes=[XBR[nb]])
        if (last_seq_blk and nb == NB - 1) or tl.kind == "s":
            kb.tt("dve", tmp[:, :], tmp[:, :], kb.bc["Bm"][:, :], ALU.add, reads=[tR, kb.bcR["Bm"]], writes=[tR])
            if tl.kind == "p":
                kb.out_evs += P.dma("sp", d["pool_p"][tl.seq, :, :], tmp[113:128, :], reads=[tR])
            else:
                kb.out_evs += P.dma_group("sp", [(d["pool_s"][q, 7:15, :], tmp[q * 8:(q + 1) * 8, :]) for q in range(NSS)], reads=[tR])
    kb.next_pre()
    if tl.kind == "s":
        for i in range(2):
            kb.memset("dve", SPf[i], 0.0, writes=[SPR])
        P.dma_group("sp", [(SPf[q // 8][(q % 8) * 16:(q % 8) * 16 + 15, :], d["st_pool"][q, :, :]) for q in range(NSS)], writes=[SPR])
        kb.out_evs += P.dma_group("sp", [(d["pool_s"][q, 0:7, :], SPf[q // 8][(q % 8) * 16 + 8:(q % 8) * 16 + 15, :]) for q in range(NSS)], reads=[SPR])
        SPb = [kb.xmb[i] for i in range(2)]
        for i in range(2):
            kb.cp("dve", SPb[i][:, :], SPf[i], reads=[SPR], writes=[kb.xmbR[i]])
    for nb in range(NB):
        pp, ppR = kb.sgt[nb % 2], kb.sgtR[nb % 2]
        first = (tl.kind == "p" and tl.t0 == 0 and nb == 0)
        pbuf, pR = kb.tmpf[nb % 2][:, :].bitcast(BF16)[:, 0:D], kb.tmpfR[nb % 2]
        for gq in range(4):
            bank, bR = kb.pb[gq % 2], kb.pbR[gq % 2]
            cs = slice(gq * 256, (gq + 1) * 256)
            if tl.kind == "p":
                kb.mm(bank[:, 0:256], kb.pm[:, 0, gq * 128:(gq + 1) * 128], XB[:, nb, cs], True, first, reads=[kb.plR, XBR[nb]], writes=[bR], sig=first)
                if not first:
                    prev, prevR = (XB[:, nb - 1, cs], XBR[nb - 1]) if nb > 0 else (kb.pprev[:, cs], kb.pprevR)
                    kb.mm(bank[:, 0:256], kb.pm[:, 1, gq * 128:(gq + 1) * 128], prev, False, True, reads=[kb.plR, prevR], writes=[bR], sig=True)
            else:
                kb.mm(bank[:, 0:256], kb.pm[:, 2, gq * 128:(gq + 1) * 128], XB[:, nb, cs], True, False, reads=[kb.plR, XBR[nb]], writes=[bR])
                kb.mm(bank[:, 0:256], kb.pm[:, 3, gq * 128:(gq + 1) * 128], SPb[0][:, cs], False, False, reads=[kb.plR, kb.xmbR[0]], writes=[bR])
                kb.mm(bank[:, 0:256], kb.pm[:, 4, gq * 128:(gq + 1) * 128], SPb[1][:, cs], False, True, reads=[kb.plR, kb.xmbR[1]], writes=[bR], sig=True)
            ro = 0 if first else 4
            kb.stt(pbuf[:, cs], bank[:, 0:256], kb.prc[:, ro + gq:ro + gq + 1], XB[:, nb, cs], ALU.mult, ALU.subtract,
                   reads=[bR, kb.plR, XBR[nb]], writes=[pR])
        _to_featT(kb, tl, nb, pbuf, pR)
    if tl.kind == "p":
        kb.cp("act", kb.pprev[:, :], XB[:, NB - 1, :], reads=[XBR[NB - 1]], writes=[kb.pprevR])

    def evac(dc, ysb, yR, acc, aR):
        kb.act(ysb[:, 0:NT], acc[:, 0:NT], AF.Identity, reads=[aR, kb.plR], writes=[yR], scale=kb.pscT[:, dc:dc + 1])
    _down_proj(kb, tl, ("pool_w", tl.kind, tl.seq, tl.t0), lambda dc: [2 * (dc // 2), 2 * (dc // 2) + 1],
               lambda dc: [(lambda t: t[:, 0:256].rearrange("p (k c) -> p k c", k=2),
                            d["pool_w"][0, dc // 2, :, (dc % 2) * 128:(dc % 2 + 1) * 128].rearrange("(k p) c -> p k c", p=128))],
               lambda slot: (lambda kc, dc, v=slot[:, 0:256].rearrange("p (k c) -> p k c", k=2): v[:, kc % 2, :]),
               lambda kc: (kb.xmT[:, kc, 0:NT], kb.xmTR[:NB]), evac=evac)
    _layer_norm(kb, tl)
    kb.barrier("dve", XBR + [SPR], kb.hR)


MIXERS[2] = _pool_mixer


def make_consts():
    c = {}
    c["k_ident"] = np.eye(128, dtype=np.float32)
    ri = np.arange(128)[:, None]
    ci = np.arange(128)[None, :]
    c["k_tmask"] = ((ci // 16) >= (ri // 16)).astype(np.float32)
    c["k_tri"] = (ci >= ri).astype(np.float32)
    wins = [2, 4, 8, 16]
    pm = np.zeros((5, 128, 4, 128), np.float32)
    prc = np.zeros((128, 8), np.float32)
    for g, w in enumerate(wins):
        s = np.arange(128)[:, None]
        t = np.arange(128)[None, :]
        pm[0, :, g, :] = ((s <= t) & (s > t - w))
        pm[1, :, g, :] = ((s - 128) > (t - w))
        qs, is_ = s // 8, s % 8
        qt, it = t // 8, t % 8
        pm[2, :, g, :] = ((qs == qt) & (is_ <= it) & (is_ > it - w))
        for h in range(2):
            q8, j = s // 16, s % 16
            pm[3 + h, :, g, :] = ((q8 + 8 * h == qt) & (j < 15) & ((j - 15) >= (it - w + 1)))
        prc[:, g] = 1.0 / np.minimum(w, np.arange(128) + 1)
        prc[:, 4 + g] = 1.0 / w
    c["k_pm"] = pm
    c["k_prc"] = prc
    half = 32
    inv_freq = np.power(np.float32(10000.0), -np.arange(half, dtype=np.float32) * np.float32(2.0 / 64)).astype(np.float32)

    def rope_tab(pos):
        ang = pos.astype(np.float32)[:, None] * inv_freq[None, :]
        return np.concatenate([np.cos(ang), np.sin(ang)], 1).astype(np.float32)
    c["k_rope_p"] = rope_tab(np.arange(SEQ))
    c["k_rope_s"] = rope_tab(PAST + (np.arange(128) % 8))
    c["k_iota32"] = (np.arange(128) % 32).astype(np.float32)[:, None]
    key = np.arange(128)[:, None]
    qi = np.arange(128)[None, :]
    c["k_smask"] = ((key // 8 == qi // 8) & (key % 8 <= qi % 8)).astype(np.float32)
    return c


def _mla_declare(kb):
    kb.din("k_rope_p", [SEQ, 64]); kb.din("k_rope_s", [128, 64])
    kb.din("k_iota32", [128, 1]); kb.din("k_smask", [128, 128])
    kb.ml_ukT = kb.sb("ml_ukT", [128, 8, 256], BF16)
    kb.ml_qn = kb.sb("ml_qn", [128, QL]); kb.ml_kvn = kb.sb("ml_kvn", [128, KVL])
    kb.ml_tri = kb.sb("ml_tri", [128, 128], BF16); kb.ml_smask = kb.sb("ml_smask", [128, 128], BF16)
    kb.ml_zero = kb.sb("ml_zero", [128, 512], BF16); kb.ml_one = kb.sb("ml_one", [128, 2], BF16)
    kb.ml_ckvT = kb.sb("ml_ckvT", [128, 2, SEQ], BF16); kb.ml_krT = kb.sb("ml_krT", [64, SEQ], BF16)
    kb.ml_Vx = kb.sb("ml_Vx", [128, SEQ // 128, KVL], BF16)
    kb.ml_idx = kb.X2[1][:, :, :].rearrange("p n f -> p (n f)").bitcast(U32)[:, 0:NSS * N_PAGES // 4]
    kb.ml_rt = kb.sb("ml_rt", [128, 4, 64])
    kb.ml_sm = kb.sb("ml_sm", [128, 16])
    kb.mlR = Res("mla_consts"); kb.kvR = Res("mla_kv"); kb.rtR = Res("mla_rt"); kb.smR = Res("mla_sm")


def _mla_setup(kb):
    P, d = kb.P, kb.dram
    nat = kb.WS[:, :].bitcast(F32)[:, 0:2048].rearrange("p (a f) -> p a f", a=2)
    natR = Res("mla_nat")
    kb.barrier("dve", kb.hR, [natR])
    P.dma("sp", nat, d["mla_w_uk"][0].rearrange("(a p) h e -> p a (h e)", p=128), writes=[natR])
    for cc in range(2):
        for hb in range(2):
            for h4 in range(4):
                h = hb * 4 + h4
                kb.tr(kb.pb[0][:, h4 * 128:(h4 + 1) * 128], nat[:, cc, h * 128:(h + 1) * 128], kb.identf[:, :],
                      reads=[natR, kb.cR], writes=[kb.pbR[0]], sig=(h4 == 3))
            kb.cp("dve", kb.ml_ukT[:, hb * 4:(hb + 1) * 4, cc * 128:(cc + 1) * 128], kb.pb[0][:, :].rearrange("p (h c) -> p h c", h=4),
                  reads=[kb.pbR[0]], writes=[kb.mlR])
    P.dma("sp", kb.ml_qn[:, :], d["mla_q_norm"][0:1, :].broadcast_to([128, QL]), writes=[kb.mlR])
    P.dma("sp", kb.ml_kvn[:, :], d["mla_kv_norm"][0:1, :].broadcast_to([128, KVL]), writes=[kb.mlR])
    P.dma("pool", kb.ml_tri[:, :], d["k_tri"][:, :], writes=[kb.mlR])
    P.dma("pool", kb.ml_smask[:, :], d["k_smask"][:, :], writes=[kb.mlR])
    kb.memset("dve", kb.ml_zero[:, :], 0.0, writes=[kb.mlR])
    kb.memset("dve", kb.ml_one[:, :], 1.0, writes=[kb.mlR])
    kb.barrier("dve", [natR], kb.hR)


def _mla_build_idx(kb):
    P, d = kb.P, kb.dram
    pti = kb.tmpf[0][:, :].bitcast(I32); ptf = kb.tmpf[1]; iot = kb.mv[:, 7:8]
    sel = kb.tmpf[0][:, 0:256]
    P.dma("sp", pti, d["ptab"].rearrange("q g -> (q g)").unsqueeze(0).broadcast_to([128, NSS * N_PAGES]), writes=[kb.tmpfR[0]])
    P.dma("sp", iot, d["k_iota32"][:, :], writes=[kb.mvR])
    kb.cp("dve", ptf[:, :], pti, reads=[kb.tmpfR[0]], writes=[kb.tmpfR[1]])
    pv = ptf[:, :].rearrange("p (q g r) -> p q g r", q=NSS, r=4)
    for r in range(4):
        kb.cp("dve", sel[32 * r:32 * (r + 1), :].rearrange("p (q g) -> p q g", q=NSS), pv[32 * r:32 * (r + 1), :, :, r],
              reads=[kb.tmpfR[1]], writes=[kb.tmpfR[0]])
    kb.ts("dve", sel, sel, 32.0, iot, ALU.mult, ALU.add, reads=[kb.tmpfR[0], kb.mvR], writes=[kb.tmpfR[0]])
    kb.cp("dve", kb.ml_idx, sel, reads=[kb.tmpfR[0]], writes=[kb.mlR] + kb.X2R[1])


def _rms_rstd(kb, bank, bR, n, col):
    junk = kb.tmpf[1]
    kb.P.op("act", lambda e: e.activation(out=junk[:, 0:n], in_=bank[:, 0:n], func=AF.Square, accum_out=kb.ml_sm[:, col:col + 1]),
            reads=[bR], writes=[kb.tmpfR[1], kb.smR])
    kb.act(kb.ml_sm[:, col + 1:col + 2], kb.ml_sm[:, col:col + 1], AF.Sqrt, reads=[kb.smR], writes=[kb.smR], bias=RMS_EPS, scale=1.0 / n)
    kb.P.op("dve", lambda e: e.reciprocal(out=kb.ml_sm[:, col + 2:col + 3], in_=kb.ml_sm[:, col + 1:col + 2]), reads=[kb.smR], writes=[kb.smR])
    return kb.ml_sm[:, col + 2:col + 3]


def _rope_tok(kb, X, XR, rt, outv, outR, nh, scratch):
    A, Bv = scratch
    cosb = rt[:, 0:32].unsqueeze(1).unsqueeze(1).broadcast_to([128, nh, 2, 32])
    sinb = rt[:, 32:64].unsqueeze(1).unsqueeze(1).broadcast_to([128, nh, 2, 32])
    kb.tt("dve", A, X, cosb, ALU.mult, reads=XR + [kb.rtR], writes=[kb.tmpfR[0]])
    kb.tt("dve", Bv, X, sinb, ALU.mult, reads=XR + [kb.rtR], writes=[kb.tmpfR[0]])
    kb.tt("dve", outv[:, :, 0, :], A[:, :, 0, :], Bv[:, :, 1, :], ALU.subtract, reads=[kb.tmpfR[0]], writes=[outR])
    kb.tt("dve", outv[:, :, 1, :], Bv[:, :, 0, :], A[:, :, 1, :], ALU.add, reads=[kb.tmpfR[0]], writes=[outR])


def _mla_mixer(kb, tl, l):
    P, d = kb.P, kb.dram
    NT, NB = tl.ntok, tl.NB
    samp = tl.kind == "s"
    if samp:
        _mla_build_idx(kb)
    _load_post(kb, tl, l, 1, 1.0)
    _modulate(kb, tl)
    kb.next_pre()
    W = kb.WS
    cqT = W[:, 0:1536].rearrange("p (a t) -> p a t", a=3)
    qnT = W[:, 1536:5632].rearrange("p (h t) -> p h t", h=8)
    olat = W[:, 1536:3584].rearrange("p (h c) -> p h c", h=8)
    olT = W[:, 3584:5632].rearrange("p (a h t) -> p a h t", a=2, h=8)
    qrT = W[:, 5632:9728].rearrange("p (h t) -> p h t", h=8)
    qlT = W[:, 9728:17920].rearrange("p (h a t) -> p h a t", h=8, a=2)
    PT = [W[:, 17920 + i * 1024:17920 + (i + 1) * 1024] for i in range(2)]
    KP = [W[:, 19968 + i * 1280:19968 + (i + 1) * 1280] for i in range(2)]
    KT = [W[:, 22528 + i * 384:22528 + (i + 1) * 384] for i in range(2)]
    cqR, qnR, qrR, qlR, olR, oltR = [Res(n) for n in ("cqT", "qnT", "qrT", "qlT", "olat", "olT")]
    PTR = [Res("PT0"), Res("PT1")]; KPR = [Res(f"KP{i}") for i in range(2)]; KTR = [Res(f"KT{i}") for i in range(2)]
    kb.barrier("dve", kb.hR, [cqR, qnR, qrR, qlR, olR, oltR] + PTR + KPR + KTR)
    for nb in range(NB):
        src = d["k_rope_s"][:, :] if samp else d["k_rope_p"][tl.t0 + nb * 128:tl.t0 + (nb + 1) * 128, :]
        P.dma("sp", kb.ml_rt[:, nb, :], src, writes=[kb.rtR])
    fA = kb.tmpf[0][:, 0:512]; fB = kb.tmpf[0][:, 512:1024]
    kb0 = 0 if samp else tl.t0 // 128
    ckvT, krT, Vx = kb.ml_ckvT, kb.ml_krT, kb.ml_Vx
    slot, sR = kb.ws.get(("ml_dkv", tl.kind, tl.seq, tl.t0), [
        (lambda t: t[:, 0:2560].rearrange("p (k f) -> p k f", k=8), d["mla_w_dkv"][0].rearrange("(k p) f -> p k f", p=128))])
    sv = slot[:, 0:2560].rearrange("p (k f) -> p k f", k=8)
    for nb in range(NB):
        bank, bR = kb.pb[nb % 2], kb.pbR[nb % 2]
        for k in range(8):
            kb.mm(bank[:, 0:320], kb.xmT[:, k, nb * 128:(nb + 1) * 128], sv[:, k, :], k == 0, k == 7, reads=[sR, kb.xmTR[nb]], writes=[bR], sig=(k == 7))
        rstd = _rms_rstd(kb, bank, bR, KVL, 0)
        cf, cfR = kb.ysb[nb % 2], kb.ysbR[nb % 2]
        kb.stt(cf[:, 0:KVL], bank[:, 0:KVL], rstd, kb.ml_kvn[:, :], ALU.mult, ALU.mult, reads=[bR, kb.smR, kb.mlR], writes=[cfR])
        A = fA[:, 0:64].rearrange("p (h a f) -> p h a f", h=1, a=2); Bv = fB[:, 0:64].rearrange("p (h a f) -> p h a f", h=1, a=2)
        _rope_tok(kb, bank[:, 256:320].rearrange("p (h a f) -> p h a f", h=1, a=2), [bR], kb.ml_rt[:, nb, :],
                  cf[:, 256:320].rearrange("p (h a f) -> p h a f", h=1, a=2), cfR, 1, (A, Bv))
        if samp:
            kb.out_evs += P.dma("sp", d["ckv_s"][:, :], cf[:, 0:KVL], reads=[cfR])
            kb.out_evs += P.dma("sp", d["kr_s"][:, :], cf[:, 256:320], reads=[cfR])
        else:
            r0 = tl.t0 + nb * 128
            kb.out_evs += P.dma("sp", d["ckv_p"][tl.seq, r0:r0 + 128, :], cf[:, 0:KVL], reads=[cfR])
            kb.out_evs += P.dma("sp", d["kr_p"][tl.seq, r0:r0 + 128, :], cf[:, 256:320], reads=[cfR])
        kblk = kb0 + nb
        kbf, kbfR = kb.xmb[nb % 2], kb.xmbR[nb % 2]
        kb.cp("act", kbf[:, 0:320], cf[:, 0:320], reads=[cfR], writes=[kbfR])
        kb.cp("dve", Vx[:, kblk, :], kbf[:, 0:KVL], reads=[kbfR], writes=[kb.kvR])
        for cc in range(2):
            kb.tr(kb.pt[:, cc * 128:(cc + 1) * 128], kbf[:, cc * 128:(cc + 1) * 128], kb.identb[:, :], reads=[kbfR, kb.cR], writes=[kb.pbR[7]])
        kb.tr(kb.pt[0:64, 256:384], kbf[:, 256:320], kb.identb[:, :], reads=[kbfR, kb.cR], writes=[kb.pbR[7]], sig=True)
        kb.cp("act", ckvT[:, :, kblk * 128:(kblk + 1) * 128], kb.pt[:, 0:256].rearrange("p (a t) -> p a t", a=2), reads=[kb.pbR[7]], writes=[kb.kvR])
        kb.cp("act", krT[:, kblk * 128:(kblk + 1) * 128], kb.pt[0:64, 256:384], reads=[kb.pbR[7]], writes=[kb.kvR])
    slot, sR = kb.ws.get(("ml_dq", tl.kind, tl.seq, tl.t0), [
        (lambda t: t[:, 0:3072].rearrange("p (k f) -> p k f", k=8), d["mla_w_dq"][0].rearrange("(k p) f -> p k f", p=128))])
    sv = slot[:, 0:3072].rearrange("p (k f) -> p k f", k=8)
    for nb in range(NB):
        bank, bR = kb.pb[2 + nb % 2], kb.pbR[2 + nb % 2]
        for k in range(8):
            kb.mm(bank[:, 0:QL], kb.xmT[:, k, nb * 128:(nb + 1) * 128], sv[:, k, :], k == 0, k == 7, reads=[sR, kb.xmTR[nb]], writes=[bR], sig=(k == 7))
        rstd = _rms_rstd(kb, bank, bR, QL, 4)
        cqb, cqbR = kb.xmb[nb % 2], kb.xmbR[nb % 2]
        kb.stt(cqb[:, 0:QL], bank[:, 0:QL], rstd, kb.ml_qn[:, :], ALU.mult, ALU.mult, reads=[bR, kb.smR, kb.mlR], writes=[cqbR])
        for a in range(3):
            kb.tr(kb.pt[:, a * 128:(a + 1) * 128], cqb[:, a * 128:(a + 1) * 128], kb.identb[:, :], reads=[cqbR, kb.cR], writes=[kb.pbR[7]], sig=(a == 2))
        kb.cp("act", cqT[:, :, nb * 128:(nb + 1) * 128], kb.pt[:, 0:384].rearrange("p (a t) -> p a t", a=3), reads=[kb.pbR[7]], writes=[cqR])
    for hg in range(2):
        slot, sR = kb.ws.get(("ml_uq", tl.kind, tl.seq, tl.t0, hg), [
            (lambda t: t[:, 0:2304].rearrange("p (a f) -> p a f", a=3),
             d["mla_w_uq"][0, :, hg * 4:(hg + 1) * 4, :].rearrange("(a p) h e -> p a (h e)", p=128))])
        sv = slot[:, 0:2304].rearrange("p (a f) -> p a f", a=3)
        for hh in range(4):
            h = hg * 4 + hh
            bank, bR = kb.pb[hh % 2], kb.pbR[hh % 2]
            for a in range(3):
                kb.mm(bank[:, 0:NT], sv[:, a, hh * 192:hh * 192 + 128], cqT[:, a, 0:NT], a == 0, a == 2, reads=[sR, cqR], writes=[bR], sig=(a == 2))
            kb.cp("act", qnT[:, h, 0:NT], bank[:, 0:NT], reads=[bR], writes=[qnR])
        for nb in range(NB):
            bank, bR = kb.pb[2 + nb % 2], kb.pbR[2 + nb % 2]
            for hh in range(4):
                for a in range(3):
                    kb.mm(bank[:, hh * 64:(hh + 1) * 64], cqT[:, a, nb * 128:(nb + 1) * 128], sv[:, a, hh * 192 + 128:hh * 192 + 192], a == 0, a == 2,
                          reads=[sR, cqR], writes=[bR], sig=(hh == 3 and a == 2))
            qrb, qrbR = kb.xmb[nb % 2], kb.xmbR[nb % 2]
            A = fA[:, 0:256].rearrange("p (h a f) -> p h a f", h=4, a=2); Bv = fB[:, 0:256].rearrange("p (h a f) -> p h a f", h=4, a=2)
            _rope_tok(kb, bank[:, 0:256].rearrange("p (h a f) -> p h a f", h=4, a=2), [bR], kb.ml_rt[:, nb, :],
                      qrb[:, 0:256].rearrange("p (h a f) -> p h a f", h=4, a=2), qrbR, 4, (A, Bv))
            for hh in range(4):
                kb.tr(kb.pt[0:64, hh * 128:(hh + 1) * 128], qrb[:, hh * 64:(hh + 1) * 64], kb.identb[:, :], reads=[qrbR, kb.cR], writes=[kb.pbR[7]], sig=(hh == 3))
            kb.act(qrT[0:64, hg * 4:(hg + 1) * 4, nb * 128:(nb + 1) * 128], kb.pt[0:64, 0:512].rearrange("p (h t) -> p h t", h=4), AF.Identity,
                   reads=[kb.pbR[7]], writes=[qrR], scale=ATTN_SCALE)
    for h in range(8):
        for cc in range(2):
            bank, bR = kb.pb[(h * 2 + cc) % 4], kb.pbR[(h * 2 + cc) % 4]
            kb.mm(bank[:, 0:NT], kb.ml_ukT[:, h, cc * 128:(cc + 1) * 128], qnT[:, h, 0:NT], True, True, reads=[kb.mlR, qnR], writes=[bR], sig=True)
            kb.act(qlT[:, h, cc, 0:NT], bank[:, 0:NT], AF.Identity, reads=[bR], writes=[qlR], scale=ATTN_SCALE)
    rsm = kb.ml_sm
    if not samp:
        for nb in range(NB):
            qb = kb0 + nb
            qs = slice(nb * 128, (nb + 1) * 128)
            for bk in range(4):
                kb.mm(kb.pb[2 + bk][:, :], kb.ml_zero[0:1, 0:128], kb.ml_zero[0:1, :], True, False, reads=[kb.mlR], writes=[kb.pbR[2 + bk]])
            kb.mm(kb.pb[6][:, 0:8], kb.ml_zero[0:1, 0:128], kb.ml_zero[0:1, 0:8], True, False, reads=[kb.mlR], writes=[kb.pbR[6]])
            for kk in range(qb + 1):
                ks = slice(kk * 128, (kk + 1) * 128)
                pt_, ptR = PT[kk % 2], PTR[kk % 2]
                for hf in range(2):
                    bank, bR = kb.pb[hf], kb.pbR[hf]
                    hs4 = slice(hf * 4, (hf + 1) * 4)
                    kb.mm(bank[:, :], ckvT[:, 0, ks], qlT[:, hs4, 0, qs], True, False, reads=[kb.kvR, qlR], writes=[bR])
                    kb.mm(bank[:, :], ckvT[:, 1, ks], qlT[:, hs4, 1, qs], False, False, reads=[kb.kvR, qlR], writes=[bR])
                    kb.mm(bank[:, :], krT[0:64, ks], qrT[0:64, hs4, qs], False, True, reads=[kb.kvR, qrR], writes=[bR], sig=True)
                    kb.act(pt_[:, hf * 512:(hf + 1) * 512], bank[:, :], AF.Exp, reads=[bR], writes=[ptR])
                if kk == qb:
                    pv = pt_[:, :].rearrange("p (h q) -> p h q", h=8)
                    kb.tt("dve", pv, pv, kb.ml_tri[:, :].unsqueeze(1).broadcast_to([128, 8, 128]), ALU.mult, reads=[ptR, kb.mlR], writes=[ptR])
                last = kk == qb
                for h in range(8):
                    kb.mm(kb.pb[2 + h // 2][:, (h % 2) * 256:(h % 2 + 1) * 256], pt_[:, h * 128:(h + 1) * 128], Vx[:, kk, :], False, last,
                          reads=[ptR, kb.kvR], writes=[kb.pbR[2 + h // 2]], sig=(last and h % 2 == 1))
                for h in range(8):
                    kb.mm(kb.pb[6][:, h:h + 1], pt_[:, h * 128:(h + 1) * 128], kb.ml_one[:, 0:1], False, last,
                          reads=[ptR, kb.mlR], writes=[kb.pbR[6]], sig=(last and h == 7))
            kb.P.op("dve", lambda e: e.reciprocal(out=rsm[:, 8:16], in_=kb.pb[6][:, 0:8]), reads=[kb.pbR[6]], writes=[kb.smR])
            for h in range(8):
                kb.act(olat[:, h, :], kb.pb[2 + h // 2][:, (h % 2) * 256:(h % 2 + 1) * 256], AF.Identity, reads=[kb.pbR[2 + h // 2], kb.smR],
                       writes=[olR], scale=rsm[:, 8 + h:9 + h])
            for cc in range(2):
                for h in range(8):
                    kb.tr(kb.pt[:, h * 128:(h + 1) * 128], olat[:, h, cc * 128:(cc + 1) * 128], kb.identb[:, :], reads=[olR, kb.cR], writes=[kb.pbR[7]], sig=(h == 7))
                kb.cp("act", olT[:, cc, :, :], kb.pt[:, :].rearrange("p (h t) -> p h t", h=8), reads=[kb.pbR[7]], writes=[oltR])
            _mla_uv(kb, tl, nb, olT, oltR)
    else:
        rows_c = d["cache_ckv"].rearrange("n (a b) c -> (n a) (b c)", b=4)
        rows_r = d["cache_kr"].rearrange("n (a b) c -> (n a) (b c)", b=4)
        n = 0
        for q in range(NSS):
            kb.mm(kb.pb[2][0:64, 0:256], kb.ml_zero[0:1, 0:64], kb.ml_zero[0:1, 0:256], True, False, reads=[kb.mlR], writes=[kb.pbR[2]])
            kb.mm(kb.pb[3][0:64, 0:2], kb.ml_zero[0:1, 0:64], kb.ml_zero[0:1, 0:2], True, False, reads=[kb.mlR], writes=[kb.pbR[3]])
            qsl = slice(q * 8, (q + 1) * 8)
            for pg in range(N_PAGES + 1):
                last = pg == N_PAGES
                pt_, ptR = PT[n % 2][:, 0:64], PTR[n % 2]
                if not last:
                    g4, j4 = pg // 4, pg % 4
                    kpi = (q * (N_PAGES // 4) + g4) % 2
                    kpf, kpR = KP[kpi], KPR[kpi]
                    kt, ktR = KT[n % 2], KTR[n % 2]
                    if j4 == 0:
                        col = q * (N_PAGES // 4) + g4
                        parts = []
                        for (dst, rows) in ((kpf[:, 0:1024], rows_c), (kpf[:, 1024:1280], rows_r)):
                            parts.append((lambda e, dst=dst, rows=rows, col=col: e.indirect_dma_start(
                                out=dst, out_offset=None, in_=rows, in_offset=bass.IndirectOffsetOnAxis(ap=kb.ml_idx[:, col:col + 1], axis=0)), None))
                        P.dma_group("pool", parts, reads=[kb.mlR], writes=[kpR])
                    kp = kpf[:, j4 * 256:(j4 + 1) * 256]
                    kpr_ = kpf[:, 1024 + j4 * 64:1024 + (j4 + 1) * 64]
                    for cc in range(2):
                        kb.tr(kb.pt[:, cc * 128:(cc + 1) * 128], kp[:, cc * 128:(cc + 1) * 128], kb.identb[:, :], reads=[kpR, kb.cR], writes=[kb.pbR[7]])
                    kb.tr(kb.pt[0:64, 256:384], kpr_, kb.identb[:, :], reads=[kpR, kb.cR], writes=[kb.pbR[7]], sig=True)
                    kb.cp("dve", kt[:, 0:256], kb.pt[:, 0:256], reads=[kb.pbR[7]], writes=[ktR])
                    kb.cp("dve", kt[0:64, 256:384], kb.pt[0:64, 256:384], reads=[kb.pbR[7]], writes=[ktR])
                    kc0, kc1, kr_, vv, rds = kt[:, 0:128], kt[:, 128:256], kt[0:64, 256:384], kp[:, 0:256], [ktR, kpR]
                else:
                    kc0, kc1, kr_, vv, rds = ckvT[:, 0, 0:128], ckvT[:, 1, 0:128], krT[0:64, 0:128], Vx[:, 0, :], [kb.kvR]
                bank, bR = kb.pb[n % 2], kb.pbR[n % 2]
                kb.mm(bank[:, 0:64], kc0, qlT[:, :, 0, qsl], True, False, reads=rds + [qlR], writes=[bR])
                kb.mm(bank[:, 0:64], kc1, qlT[:, :, 1, qsl], False, False, reads=rds + [qlR], writes=[bR])
                kb.mm(bank[:, 0:64], kr_, qrT[0:64, :, qsl], False, True, reads=rds + [qrR], writes=[bR], sig=True)
                kb.act(pt_, bank[:, 0:64], AF.Exp, reads=[bR], writes=[ptR])
                if last:
                    pv = pt_.rearrange("p (h i) -> p h i", h=8)
                    kb.tt("dve", pv, pv, kb.ml_smask[:, q * 8:(q + 1) * 8].unsqueeze(1).broadcast_to([128, 8, 8]), ALU.mult, reads=[ptR, kb.mlR], writes=[ptR])
                kb.mm(kb.pb[2][0:64, 0:256], pt_, vv, False, last, reads=[ptR] + rds, writes=[kb.pbR[2]], sig=last)
                kb.mm(kb.pb[3][0:64, 0:1], pt_, kb.ml_one[:, 0:1], False, last, reads=[ptR, kb.mlR], writes=[kb.pbR[3]], sig=last)
                n += 1
            kb.P.op("dve", lambda e: e.reciprocal(out=rsm[0:64, 8:9], in_=kb.pb[3][0:64, 0:1]), reads=[kb.pbR[3]], writes=[kb.smR])
            ob, obR = kb.sgt[q % 2], kb.sgtR[q % 2]
            kb.act(ob[0:64, 0:256], kb.pb[2][0:64, 0:256], AF.Identity, reads=[kb.pbR[2], kb.smR], writes=[obR], scale=rsm[0:64, 8:9])
            for cc in range(2):
                kb.tr(kb.pt[:, cc * 64:(cc + 1) * 64], ob[0:64, cc * 128:(cc + 1) * 128], kb.identb[0:64, 0:64], reads=[obR, kb.cR], writes=[kb.pbR[7]], sig=(cc == 1))
            kb.cp("act", olT[:, :, :, q * 8:(q + 1) * 8], kb.pt[:, 0:128].rearrange("p (a h i) -> p a h i", a=2, h=8), reads=[kb.pbR[7]], writes=[oltR])
        _mla_uv(kb, tl, 0, olT, oltR)
    _down_proj(kb, tl, ("ml_o", tl.kind, tl.seq, tl.t0), 8,
               lambda dc: [(lambda t: t[:, 0:1024].rearrange("p (h c) -> p h c", h=8),
                            d["mla_w_o"][0, :, :, dc * 128:(dc + 1) * 128].rearrange("h p c -> p h c"))],
               lambda slot: (lambda kc, dc, v=slot[:, 0:1024].rearrange("p (h c) -> p h c", h=8): v[:, kc, :]),
               lambda kc: (kb.xmT[:, kc, 0:NT], kb.xmTR[:NB]))
    _layer_norm(kb, tl)
    kb.barrier("dve", [cqR, qnR, qrR, qlR, olR, oltR] + PTR + KPR + KTR, kb.hR)


def _mla_uv(kb, tl, nb, olT, oltR):
    d = kb.dram
    slot, sR = kb.ws.get(("ml_uv", tl.kind, tl.seq, tl.t0, nb), [
        (lambda t: t[:, 0:2048].rearrange("p (a f) -> p a f", a=2), d["mla_w_uv"][0].rearrange("(a p) h e -> p a (h e)", p=128))])
    sv = slot[:, 0:2048].rearrange("p (a f) -> p a f", a=2)
    for hb in range(2):
        bank, bR = kb.pb[hb], kb.pbR[hb]
        for h4 in range(4):
            h = hb * 4 + h4
            for cc in range(2):
                kb.mm(bank[:, h4 * 128:(h4 + 1) * 128], sv[:, cc, h * 128:(h + 1) * 128], olT[:, cc, h, :], cc == 0, cc == 1,
                      reads=[sR, oltR], writes=[bR], sig=(h4 == 3 and cc == 1))
        kb.cp("act", kb.xmT[:, hb * 4:(hb + 1) * 4, nb * 128:(nb + 1) * 128], bank[:, :].rearrange("p (h t) -> p h t", h=4),
              reads=[bR], writes=[kb.xmTR[nb]])


MIXERS[3] = _mla_mixer


N_CORES = 8
_W_NAMES = [n for n, _ in WEIGHT_SPECS]


def kernel(**inputs):
    f32 = np.float32
    a = {k: np.asarray(v) for k, v in inputs.items()}
    n_phys = a["cache_mla_ckv"].shape[1]
    tiles = [Tile("p", s, t0, T) for s in range(NPS) for t0 in range(0, SEQ, T)] + [Tile("s", 0, 0, 128)]
    cfg = dict(tiles=tiles, layers=[0, 1, 2, 3], n_phys=n_phys)
    kb = build(cfg)
    consts = make_consts()
    shared = {n: np.ascontiguousarray(a[n], dtype=f32) for n in _W_NAMES}
    shared.update(consts)
    shared["cache_ckv"] = np.ascontiguousarray(a["cache_mla_ckv"][0], dtype=f32)
    shared["cache_kr"] = np.ascontiguousarray(a["cache_mla_krope"][0], dtype=f32)
    in_maps = []
    for c in range(N_CORES):
        ps, ss = slice(NPS * c, NPS * (c + 1)), slice(NSS * c, NSS * (c + 1))
        m = dict(shared)
        m["xp"] = np.ascontiguousarray(a["x_prompt"][ps], dtype=f32)
        m["xs"] = np.ascontiguousarray(a["x_sample"][ss], dtype=f32).reshape(128, D)
        m["st_re"] = np.ascontiguousarray(a["state_ssm_re"][0, ss], dtype=f32)
        m["st_im"] = np.ascontiguousarray(a["state_ssm_im"][0, ss], dtype=f32)
        m["st_pool"] = np.ascontiguousarray(a["state_pool"][0, ss], dtype=f32)
        m["ptab"] = np.ascontiguousarray(a["page_table"][ss]).astype(np.int32)
        m["c_all"] = np.ascontiguousarray(np.concatenate([a["c_prompt"][ps], a["c_sample"][ss]], 0), dtype=f32)
        in_maps.append(m)
    res = run_bass_kernel_spmd(kb.nc, in_maps, core_ids=list(range(N_CORES)))
    r = res.results

    def cat(name):
        return np.concatenate([np.asarray(r[c][name]) for c in range(N_CORES)], 0)
    y_p = cat("y_p")
    y_s = cat("y_s").reshape(N_CORES * NSS, DEC, D)
    outs = (
        y_p, y_s,
        cat("ssm_re_p")[None], cat("ssm_im_p")[None], cat("ssm_re_s")[None], cat("ssm_im_s")[None],
        cat("gm_v_s").reshape(1, N_CORES * NSS, DEC, D),
        cat("pool_p")[None], cat("pool_s")[None],
        cat("ckv_p")[None], cat("kr_p")[None],
        cat("ckv_s").reshape(1, N_CORES * NSS, DEC, KVL), cat("kr_s").reshape(1, N_CORES * NSS, DEC, DR),
    )
    return tuple(np.ascontiguousarray(o, dtype=f32) for o in outs)
```

```python
import contextlib
import math
import numpy as np
import ml_dtypes
import concourse.bass as bass
import concourse.mybir as mybir
from concourse.bass_utils import run_bass_kernel_spmd

F32 = mybir.dt.float32
BF16 = mybir.dt.bfloat16
I32 = mybir.dt.int32
U32 = mybir.dt.uint32
AF = mybir.ActivationFunctionType
ALU = mybir.AluOpType
AX = mybir.AxisListType

D = 1024
DFF = 2816
NFC = DFF // 128
DEPTH = 4
SEQ = 2048
NPS = 2
NSS = 16
DEC = 8
T = 512
ALPHA = (2.0 * DEPTH) ** 0.25
LN_EPS = 1e-5
RMS_EPS = 1e-6
N_PAGES = 64
PAGE = 128
PAST = N_PAGES * PAGE
KVL = 256
DR = 64
QL = 384
NH = 8
ATTN_SCALE = (128 + 64) ** -0.5
SLOT = 4096


class Res:
    __slots__ = ("name", "w", "r")

    def __init__(self, name):
        self.name = name
        self.w = []
        self.r = []


class Pend:
    __slots__ = ("sid", "val")

    def __init__(self, sid):
        self.sid = sid
        self.val = None


class Q:
    def __init__(self, name, sid):
        self.name = name
        self.sid = sid
        self.ops = []
        self.count = 0
        self.known = {}
        self.pend = []
        self.dma_n = 0


NRING = 8


class Prog:
    def __init__(self):
        self.q = {}
        for i, n in enumerate(["pe", "act", "dve", "pool", "sp"]):
            self.q[n] = Q(n, i)
        self.nsem = 5
        self.ring = {}
        for n in ["sp", "pool", "act"]:
            self.ring[n] = [[self.nsem + i, 0] for i in range(NRING)]
            self.nsem += NRING
        self.dry = False
        self.no_self_wait = {"pe"}

    def _need(self, q, waits, ev):
        if ev is None:
            return
        if isinstance(ev, Pend):
            if ev.sid == q.sid and q.name in self.no_self_wait:
                return
            assert ev.val is not None, "dependency on unsignalled PE op"
            ev = (ev.sid, ev.val)
        sid, val = ev
        if sid == q.sid and q.name in self.no_self_wait:
            return
        if q.known.get(sid, 0) >= val:
            return
        if waits.get(sid, 0) < val:
            waits[sid] = val

    def _deps(self, q, reads, writes):
        waits = {}
        for r in reads:
            for ev in r.w:
                self._need(q, waits, ev)
        for w in writes:
            for ev in w.w:
                self._need(q, waits, ev)
            for ev in w.r:
                self._need(q, waits, ev)
        for sid, val in waits.items():
            q.known[sid] = val
        return waits

    def op(self, eng, fn, reads=(), writes=(), sig=True):
        if self.dry:
            return None
        q = self.q[eng]
        waits = self._deps(q, reads, writes)
        if sig:
            q.count += 1
            ev = (q.sid, q.count)
            for p in q.pend:
                p.val = q.count
            q.pend = []
        else:
            ev = Pend(q.sid)
            q.pend.append(ev)
        q.ops.append((sorted(waits.items()), fn, q.sid if sig else None))
        for r in reads:
            r.r.append(ev)
        for w in writes:
            w.w = [ev]
            w.r = []
        return ev

    def dma(self, eng, out, in_, reads=(), writes=(), **kw):
        return self.dma_group(eng, [(out, in_)], reads, writes, **kw)

    def dma_group(self, eng, parts, reads=(), writes=(), **kw):
        if self.dry:
            return []
        q = self.q[eng]
        waits = self._deps(q, reads, writes)
        evs = []
        for out, in_ in parts:
            slot = self.ring[eng][q.dma_n % NRING]
            q.dma_n += 1
            sid, prev = slot
            if prev > 0:
                self._need(q, waits, (sid, prev))
                q.known[sid] = max(q.known.get(sid, 0), prev)
            slot[1] = prev + 16
            ev = (sid, prev + 16)
            evs.append(ev)

            if callable(out):
                fn = out
            else:
                def fn(e, out=out, in_=in_, kw=kw):
                    return e.dma_start(out=out, in_=in_, **kw)
            q.ops.append((sorted(waits.items()), fn, ("dma", sid)))
            waits = {}
        for r in reads:
            r.r.extend(evs)
        for w in writes:
            w.w = list(evs)
            w.r = []
        return evs

    def wait_all(self, eng, evs):
        q = self.q[eng]
        waits = {}
        for ev in evs:
            self._need(q, waits, ev)
        for sid, val in waits.items():
            q.known[sid] = val
        q.ops.append((sorted(waits.items()), None, None))

    def check(self):
        sem = {}
        pos = {n: 0 for n in self.q}
        progress = True
        while progress:
            progress = False
            for n, q in self.q.items():
                while pos[n] < len(q.ops):
                    waits, fn, sig = q.ops[pos[n]]
                    if any(sem.get(sid, 0) < val for sid, val in waits):
                        break
                    if sig is not None:
                        if isinstance(sig, tuple):
                            sem[sig[1]] = sem.get(sig[1], 0) + 16
                        else:
                            sem[sig] = sem.get(sig, 0) + 1
                    pos[n] += 1
                    progress = True
        stuck = {n: (pos[n], len(q.ops), q.ops[pos[n]][0]) for n, q in self.q.items() if pos[n] < len(q.ops)}
        assert not stuck, f"deadlock: {stuck} sems={sem}"

    def emit(self, nc, es):
        sems = [es.enter_context(nc.semaphore(f"s{i}")) for i in range(self.nsem)]
        with nc.Block() as block:
            def mk(q):
                def body(e):
                    for waits, fn, sig in q.ops:
                        for sid, val in waits:
                            e.wait_ge(sems[sid], val)
                        if fn is None:
                            continue
                        inst = fn(e)
                        if sig is None:
                            continue
                        if isinstance(sig, tuple):
                            inst.then_inc(sems[sig[1]], 16)
                        else:
                            inst.then_inc(sems[sig], 1)
                return body
            block.tensor(mk(self.q["pe"]))
            block.scalar(mk(self.q["act"]))
            block.vector(mk(self.q["dve"]))
            block.gpsimd(mk(self.q["pool"]))
            block.sync(mk(self.q["sp"]))


class WStream:
    def __init__(self, P, nslots):
        self.P = P
        self.nslots = nslots
        self.plan = []
        self.i = 0
        self.issued = 0
        self.slots = None
        self.res = [Res(f"ws{i}") for i in range(nslots)]

    def reset(self):
        self.i = 0
        self.issued = 0

    def _issue(self, j):
        key, parts, reads = self.plan[j]
        s = j % self.nslots
        tile = self.slots[s]
        self.P.dma_group("pool", [(view_fn(tile), src) for view_fn, src in parts], reads=reads, writes=[self.res[s]])

    def get(self, key, parts, reads=()):
        if self.P.dry:
            self.plan.append((key, parts, list(reads)))
            return self.slots[0], self.res[0]
        assert self.plan[self.i][0] == key, (self.plan[self.i][0], key)
        while self.issued < min(len(self.plan), self.i + self.nslots):
            self._issue(self.issued)
            self.issued += 1
        s = self.i % self.nslots
        self.i += 1
        return self.slots[s], self.res[s]


WEIGHT_SPECS = [
    ("w_ada", [DEPTH, D, 9 * D]), ("b_ada", [DEPTH, 9 * D]), ("ln_g", [DEPTH, 3, D]), ("ln_b", [DEPTH, 3, D]),
    ("ffn_w_gate", [DEPTH, 2, D, DFF]), ("ffn_w_up", [DEPTH, 2, D, DFF]), ("ffn_w_down", [DEPTH, 2, DFF, D]),
    ("s5_a_re", [1, 64, 64]), ("s5_a_im", [1, 64, 64]), ("s5_log_dt", [1, 64]),
    ("s5_b_re", [1, 64, 64, 16]), ("s5_b_im", [1, 64, 64, 16]), ("s5_c_re", [1, 64, 16, 64]), ("s5_c_im", [1, 64, 16, 64]),
    ("s5_d", [1, D]), ("s5_w_out", [1, D, D]), ("s5_w_gate", [1, D, D]),
    ("gm_w_in", [1, D, 2 * D]), ("gm_ln_g", [1, D]), ("gm_ln_b", [1, D]), ("gm_w_s", [1, 8, 128, 128]),
    ("gm_b_s", [1, 8, 128]), ("gm_w_out", [1, D, D]), ("pool_w", [1, 4, 256, 256]), ("pool_scale", [1, D]),
    ("mla_w_dq", [1, D, QL]), ("mla_q_norm", [1, QL]), ("mla_w_uq", [1, QL, NH, 192]), ("mla_w_dkv", [1, D, KVL + DR]),
    ("mla_kv_norm", [1, KVL]), ("mla_w_uk", [1, KVL, NH, 128]), ("mla_w_uv", [1, KVL, NH, 128]), ("mla_w_o", [1, NH, 128, D]),
]


class KB:
    def __init__(self, cfg):
        self.cfg = cfg
        self.nc = bass.Bass("TRN2", target_bir_lowering=False)
        self.P = Prog()
        self.es = contextlib.ExitStack()
        self.out_evs = []
        self.dram = {}
        self._n = 0

    def din(self, name, shape, dt=F32):
        self.dram[name] = self.nc.dram_tensor(name, list(shape), dt, kind="ExternalInput").ap()
        return self.dram[name]

    def dout(self, name, shape, dt=F32):
        self.dram[name] = self.nc.dram_tensor(name, list(shape), dt, kind="ExternalOutput").ap()
        return self.dram[name]

    def dscr(self, name, shape, dt=F32):
        self.dram[name] = self.nc.dram_tensor(name, list(shape), dt, kind="Internal").ap()
        return self.dram[name]

    def sb(self, name, shape, dt=F32):
        return self.es.enter_context(self.nc.sbuf_tensor(name, list(shape), dt))

    def ps(self, name, shape, dt=F32):
        return self.es.enter_context(self.nc.psum_tensor(name, list(shape), dt))

    def mm(self, out, lhsT, rhs, start, stop, reads, writes, sig=False, **kw):
        return self.P.op("pe", lambda e: e.matmul(out, lhsT=lhsT, rhs=rhs, start=start, stop=stop, **kw),
                         reads=reads, writes=writes, sig=sig)

    def tr(self, out, in_, ident, reads, writes, sig=False):
        return self.P.op("pe", lambda e: e.transpose(out=out, in_=in_, identity=ident), reads=reads, writes=writes, sig=sig)

    def act(self, out, in_, func, reads, writes, **kw):
        return self.P.op("act", lambda e: e.activation(out=out, in_=in_, func=func, **kw), reads=reads, writes=writes)

    def tt(self, eng, out, in0, in1, op, reads, writes):
        return self.P.op(eng, lambda e: e.tensor_tensor(out=out, in0=in0, in1=in1, op=op), reads=reads, writes=writes)

    def ts(self, eng, out, in0, s1, s2, op0, op1, reads, writes):
        return self.P.op(eng, lambda e: e.tensor_scalar(out=out, in0=in0, scalar1=s1, scalar2=s2, op0=op0, op1=op1),
                         reads=reads, writes=writes)

    def stt(self, out, in0, scalar, in1, op0, op1, reads, writes):
        return self.P.op("dve", lambda e: e.scalar_tensor_tensor(out=out, in0=in0, scalar=scalar, in1=in1, op0=op0, op1=op1),
                         reads=reads, writes=writes)

    def cp(self, eng, out, in_, reads, writes):
        if eng == "act":
            return self.P.op("act", lambda e: e.copy(out=out, in_=in_), reads=reads, writes=writes)
        return self.P.op(eng, lambda e: e.tensor_copy(out=out, in_=in_), reads=reads, writes=writes)

    def rot_ln(self):
        i = self.lnrot % 4
        self.lnrot += 1
        self.st6, self.st6R, self.mv, self.mvR = self.st6s[i], self.st6Rs[i], self.mvs[i], self.mvRs[i]

    def sel(self, i):
        self.cur = i
        self.x, self.xR = self.X2[i], self.X2R[i]
        self.xmT, self.xmTR = self.XT2[i], self.XT2R[i]

    def memset(self, eng, ap, val, writes):
        return self.P.op(eng, lambda e: e.memset(ap, val), writes=writes)


class Tile:
    def __init__(self, kind, seq, t0, ntok):
        self.kind = kind
        self.seq = seq
        self.t0 = t0
        self.ntok = ntok
        self.NB = ntok // 128


def _declare(kb):
    nc, P = kb.nc, kb.P
    kb.din("xp", [NPS, SEQ, D]); kb.din("xs", [128, D])
    kb.din("st_re", [NSS, 64, 64]); kb.din("st_im", [NSS, 64, 64]); kb.din("st_pool", [NSS, 15, D])
    kb.din("cache_ckv", [kb.cfg["n_phys"], PAGE, KVL]); kb.din("cache_kr", [kb.cfg["n_phys"], PAGE, DR])
    kb.din("ptab", [NSS, N_PAGES], I32)
    kb.din("c_all", [NPS + NSS, D])
    for n, shp in WEIGHT_SPECS:
        kb.din(n, kb.cfg.get("wshapes", {}).get(n, shp))
    kb.din("k_ident", [128, 128])
    for n, shp in kb.cfg.get("consts", []):
        kb.din(n, shp)
    kb.dout("y_p", [NPS, SEQ, D]); kb.dout("y_s", [128, D])
    kb.dout("ssm_re_p", [NPS, 64, 64]); kb.dout("ssm_im_p", [NPS, 64, 64])
    kb.dout("ssm_re_s", [NSS, 64, 64]); kb.dout("ssm_im_s", [NSS, 64, 64])
    kb.dout("gm_v_s", [128, D])
    kb.dout("pool_p", [NPS, 15, D]); kb.dout("pool_s", [NSS, 15, D])
    kb.dout("ckv_p", [NPS, SEQ, KVL]); kb.dout("kr_p", [NPS, SEQ, DR])
    kb.dout("ckv_s", [128, KVL]); kb.dout("kr_s", [128, DR])
    kb.dscr("mod_scr", [DEPTH, NPS + NSS, 9 * D])
    for n, shp, dt in kb.cfg.get("dbg", []):
        kb.dout(n, shp, dt)

    NBM = T // 128
    kb.X2 = [kb.sb(f"x{s}", [128, NBM, D]) for s in range(2)]
    kb.X2R = [[Res(f"x{s}_{i}") for i in range(NBM)] for s in range(2)]
    kb.XT2 = [kb.sb(f"xmT{s}", [128, 8, T], BF16) for s in range(2)]
    kb.XT2R = [[Res(f"xmT{s}_{i}") for i in range(NBM)] for s in range(2)]
    kb.sel(0)
    kb.WS = kb.sb("WS", [128, 24 * 1024], BF16)
    kb.hR2 = [[Res(f"h{s}_{i}") for i in range(NFC)] for s in range(2)]
    kb.hR = kb.hR2[0] + kb.hR2[1]
    kb.ws = WStream(P, 3)
    kb.ws.slots = [kb.sb(f"ring{i}", [128, SLOT], BF16) for i in range(3)]
    names = ["S1", "Bm", "G1", "LNg", "LNb"]
    kb.bc = {n: kb.sb("bc_" + n, [128, D]) for n in names}
    kb.bcR = {n: Res("bc_" + n) for n in names}
    kb.tmpf = [kb.sb(f"tmpf{i}", [128, D]) for i in range(2)]; kb.tmpfR = [Res(f"tmpf{i}") for i in range(2)]
    kb.xmb = [kb.sb(f"xmb{i}", [128, D], BF16) for i in range(2)]; kb.xmbR = [Res(f"xmb{i}") for i in range(2)]
    kb.ysb = [kb.sb(f"ysb{i}", [128, T]) for i in range(2)]; kb.ysbR = [Res(f"ysb{i}") for i in range(2)]
    kb.t2 = [kb.sb(f"t2{i}", [128, NBM, 128]) for i in range(2)]; kb.t2R = [Res(f"t2{i}") for i in range(2)]
    kb.sgt = [kb.sb(f"sgt{i}", [128, T], BF16) for i in range(2)]; kb.sgtR = [Res(f"sgt{i}") for i in range(2)]
    kb.st6s = [kb.sb(f"st6_{i}", [128, 2, 6]) for i in range(4)]; kb.st6Rs = [Res(f"st6_{i}") for i in range(4)]
    kb.mvs = [kb.sb(f"mv_{i}", [128, 8]) for i in range(4)]; kb.mvRs = [Res(f"mv_{i}") for i in range(4)]
    kb.lnrot = 0
    kb.rot_ln()
    kb.identf = kb.sb("identf", [128, 128]); kb.identb = kb.sb("identb", [128, 128], BF16)
    kb.cR = Res("consts")
    kb.pb = [kb.ps(f"pb{i}", [128, 512]) for i in range(7)]
    kb.pbR = [Res(f"pb{i}") for i in range(8)]
    kb.pt = kb.ps("pt", [128, 1024], BF16)
    kb.dummy = kb.sb("dummyt", [128, 8])
    kb.din("k_tmask", [128, 128])
    kb.tmask = kb.sb("tmask", [128, 128])
    P.dma("sp", kb.tmask[:, :], kb.dram["k_tmask"][:, :], writes=[kb.cR])
    _s5_declare(kb)
    _gm_declare(kb)
    _pool_declare(kb)
    _mla_declare(kb)
    P.dma("sp", kb.identf[:, :], kb.dram["k_ident"][:, :], writes=[kb.cR])
    P.dma("pool", kb.identb[:, :], kb.dram["k_ident"][:, :], writes=[kb.cR])


def _mod_phase(kb):
    P, d = kb.P, kb.dram
    NS = NPS + NSS
    csb = kb.tmpf[0]; csbR = kb.tmpfR[0]
    P.dma("sp", csb[0:NS, :], d["c_all"][:, :], writes=[csbR])
    cs = kb.xmb[0]; csR = kb.xmbR[0]
    kb.act(cs[0:NS, :], csb[0:NS, :], AF.Silu, reads=[csbR], writes=[csR])
    cT = kb.sgt[0]; cTR = kb.sgtR[0]
    cTv = cT[:, 0:8 * NS].rearrange("p (k s) -> p k s", k=8)
    for k in range(8):
        kb.tr(kb.pt[:, k * NS:(k + 1) * NS], cs[0:NS, k * 128:(k + 1) * 128], kb.identb[0:NS, 0:NS],
              reads=[csR, kb.cR], writes=[kb.pbR[7]], sig=(k == 7))
    kb.cp("act", cT[:, 0:8 * NS], kb.pt[:, 0:8 * NS], reads=[kb.pbR[7]], writes=[cTR])
    modR = kb.modR = [Res(f"mod{l}") for l in range(DEPTH)]
    bt = [kb.ysb[0], kb.ysb[1]]; btR = kb.ysbR
    ob = [kb.bc["S1"], kb.bc["Bm"]]; obR = [kb.bcR["S1"], kb.bcR["Bm"]]
    n = 0
    for l in kb.cfg["layers"]:
        for j in range(18):
            c0 = j * 512
            slot, sR = kb.ws.get(("ada", l, j), [(lambda t: t[:, 0:4096].rearrange("p (k f) -> p k f", k=8),
                                                   d["w_ada"][l, :, c0:c0 + 512].rearrange("(k p) f -> p k f", p=128))])
            sv = slot[:, 0:4096].rearrange("p (k f) -> p k f", k=8)
            b = n % 2
            P.dma("sp", bt[b][0:NS, :], d["b_ada"][l:l + 1, c0:c0 + 512].broadcast_to([NS, 512]), writes=[btR[b]])
            acc = kb.pb[b]
            for k in range(8):
                kb.mm(acc[0:NS, :], cTv[:, k, :], sv[:, k, :], k == 0, k == 7, reads=[cTR, sR], writes=[kb.pbR[b]], sig=(k == 7))
            kb.tt("dve", ob[b][0:NS, 0:512], acc[0:NS, :], bt[b][0:NS, :], ALU.add, reads=[kb.pbR[b], btR[b]], writes=[obR[b]])
            P.dma("sp", d["mod_scr"][l, :, c0:c0 + 512], ob[b][0:NS, 0:512], reads=[obR[b]], writes=[modR[l]])
            n += 1


def _bcast_load(kb, tl, name, src_fn):
    P = kb.P
    dst, R = kb.bc[name], kb.bcR[name]
    if tl.kind == "p":
        src, rd = src_fn(tl.seq)
        P.dma("sp", dst[:, :], src.broadcast_to([128, D]), reads=rd, writes=[R])
    else:
        parts = []
        rd = []
        for q in range(NSS):
            src, r1 = src_fn(NPS + q)
            rd = r1
            parts.append((dst[q * DEC:(q + 1) * DEC, :], src.broadcast_to([DEC, D])))
        P.dma_group("sp", parts, reads=rd, writes=[R])


def _mod_row(kb, l, col):
    d = kb.dram
    return lambda s: (d["mod_scr"][l, s:s + 1, col * D:(col + 1) * D], [kb.modR[l]])


def _load_pre(kb, tl, l, k):
    _bcast_load(kb, tl, "S1", _mod_row(kb, l, 3 * k + 1))
    _bcast_load(kb, tl, "Bm", _mod_row(kb, l, 3 * k))
    R = kb.bcR["S1"]
    kb.act(kb.bc["S1"][:, :], kb.bc["S1"][:, :], AF.Identity, reads=[R], writes=[R], scale=1.0, bias=1.0)


def _load_post(kb, tl, l, k, wgt):
    d = kb.dram
    _bcast_load(kb, tl, "G1", _mod_row(kb, l, 3 * k + 2))
    R = kb.bcR["G1"]
    kb.act(kb.bc["G1"][:, :], kb.bc["G1"][:, :], AF.Identity, reads=[R], writes=[R], scale=wgt / ALPHA, bias=wgt / ALPHA)
    kb.P.dma("sp", kb.bc["LNg"][:, :], d["ln_g"][l, k:k + 1, :].broadcast_to([128, D]), writes=[kb.bcR["LNg"]])
    kb.P.dma("sp", kb.bc["LNb"][:, :], d["ln_b"][l, k:k + 1, :].broadcast_to([128, D]), writes=[kb.bcR["LNb"]])


def _modulate(kb, tl, f32_keep=None):
    for nb in range(tl.NB):
        b = nb % 2
        tmp, tR = kb.tmpf[b], kb.tmpfR[b]
        kb.tt("dve", tmp[:, :], kb.x[:, nb, :], kb.bc["S1"][:, :], ALU.mult, reads=[kb.xR[nb], kb.bcR["S1"]], writes=[tR])
        xmb, xR_ = kb.xmb[b], kb.xmbR[b]
        kb.tt("dve", xmb[:, :], tmp[:, :], kb.bc["Bm"][:, :], ALU.add, reads=[tR, kb.bcR["Bm"]], writes=[xR_])
        if f32_keep is not None:
            f32_keep(nb, tmp, tR)
        for k in range(8):
            kb.tr(kb.pt[:, k * 128:(k + 1) * 128], xmb[:, k * 128:(k + 1) * 128], kb.identb[:, :],
                  reads=[xR_, kb.cR], writes=[kb.pbR[7]], sig=(k == 7))
        kb.cp("act", kb.xmT[:, :, nb * 128:(nb + 1) * 128], kb.pt[:, :].rearrange("p (k t) -> p k t", k=8),
              reads=[kb.pbR[7]], writes=[kb.xmTR[nb]])


def _ffn_up(kb, tls, l, j):
    d = kb.dram
    hTs = [kb.WS[:, i * NFC * T:(i + 1) * NFC * T].rearrange("p (f t) -> p f t", f=NFC) for i in range(2)]
    for pc in range(NFC // 2):
        c0 = pc * 256
        slot, sR = kb.ws.get(("ffn_up", l, j, pc, tls[0].kind, tls[0].seq, tls[0].t0), [
            (lambda t: t[:, 0:2048].rearrange("p (k f) -> p k f", k=8), d["ffn_w_gate"][l, j, :, c0:c0 + 256].rearrange("(k p) f -> p k f", p=128)),
            (lambda t: t[:, 2048:4096].rearrange("p (k f) -> p k f", k=8), d["ffn_w_up"][l, j, :, c0:c0 + 256].rearrange("(k p) f -> p k f", p=128))])
        gv = slot[:, 0:2048].rearrange("p (k f) -> p k f", k=8)
        uv = slot[:, 2048:4096].rearrange("p (k f) -> p k f", k=8)
        for f2 in range(2):
            fc = pc * 2 + f2
            for i, tl in enumerate(tls):
                NT = tl.ntok
                xmT, xr = kb.XT2[i], kb.XT2R[i][:tl.NB]
                pi = i * 2 if len(tls) == 2 else (fc % 2) * 2
                gps, ups = kb.pb[pi], kb.pb[pi + 1]
                for k in range(8):
                    kb.mm(gps[:, 0:NT], gv[:, k, f2 * 128:(f2 + 1) * 128], xmT[:, k, 0:NT], k == 0, k == 7,
                          reads=[sR] + xr, writes=[kb.pbR[pi]], sig=(k == 7))
                for k in range(8):
                    kb.mm(ups[:, 0:NT], uv[:, k, f2 * 128:(f2 + 1) * 128], xmT[:, k, 0:NT], k == 0, k == 7,
                          reads=[sR] + xr, writes=[kb.pbR[pi + 1]], sig=(k == 7))
                sb_ = (fc + i) % 2
                sg, sgR = kb.sgt[sb_], kb.sgtR[sb_]
                kb.act(sg[:, 0:NT], gps[:, 0:NT], AF.Silu, reads=[kb.pbR[pi]], writes=[sgR])
                kb.tt("dve", hTs[i][:, fc, 0:NT], ups[:, 0:NT], sg[:, 0:NT], ALU.mult, reads=[kb.pbR[pi + 1], sgR], writes=[kb.hR2[i][fc]])
    return hTs


def _down_proj(kb, tl, key, KC, part_fn, view_fn, in_fn, evac=None, pre_scaled=False):
    NT, NB = tl.ntok, tl.NB
    for dc in range(8):
        slot, sR = kb.ws.get(key + (dc,), part_fn(dc))
        wv = view_fn(slot)
        b = dc % 2
        acc, aR = kb.pb[4 + b], kb.pbR[4 + b]
        kcs = KC(dc) if callable(KC) else list(range(KC))
        for i, kc in enumerate(kcs):
            ap, rs = in_fn(kc)
            kb.mm(acc[:, 0:NT], wv(kc, dc), ap, i == 0, i == len(kcs) - 1, reads=[sR] + rs, writes=[aR], sig=(i == len(kcs) - 1))
        ysb, yR = kb.ysb[b], kb.ysbR[b]
        if evac is None:
            kb.cp("act", ysb[:, 0:NT], acc[:, 0:NT], reads=[aR], writes=[yR])
        else:
            evac(dc, ysb, yR, acc, aR)
        _down_tail(kb, tl, dc, ysb, yR)


def _down_tail(kb, tl, dc, ysb, yR, b=None):
    NB = tl.NB
    if b is None:
        b = dc % 2
    for nb in range(NB):
        kb.tr(kb.pb[6][:, nb * 128:(nb + 1) * 128], ysb[:, nb * 128:(nb + 1) * 128], kb.identf[:, :],
              reads=[yR, kb.cR], writes=[kb.pbR[6]], sig=(nb == NB - 1))
    t2, t2R = kb.t2[b], kb.t2R[b]
    kb.tt("dve", t2[:, 0:NB, :], kb.pb[6][:, 0:NB * 128].rearrange("p (n c) -> p n c", n=NB),
          kb.bc["G1"][:, dc * 128:(dc + 1) * 128].unsqueeze(1).broadcast_to([128, NB, 128]), ALU.mult,
          reads=[kb.pbR[6], kb.bcR["G1"]], writes=[t2R])
    xv = kb.x[:, 0:NB, dc * 128:(dc + 1) * 128]
    kb.tt("dve", xv, xv, t2[:, 0:NB, :], ALU.add, reads=[t2R] + kb.xR[:NB], writes=kb.xR[:NB])


def _layer_norm(kb, tl):
    eps = LN_EPS / (ALPHA * ALPHA)
    for nb in range(tl.NB):
        kb.rot_ln()
        xr = kb.xR[nb]
        xb = kb.x[:, nb, :]
        _ln_stats(kb, lambda h, nb=nb, x=kb.x: x[:, nb, h * 512:(h + 1) * 512], xr, eps)
        kb.act(xb, xb, AF.Identity, reads=[xr, kb.mvR], writes=[xr], scale=kb.mv[:, 3:4], bias=kb.mv[:, 4:5])
        kb.tt("dve", xb, xb, kb.bc["LNg"][:, :], ALU.mult, reads=[xr, kb.bcR["LNg"]], writes=[xr])
        kb.tt("dve", xb, xb, kb.bc["LNb"][:, :], ALU.add, reads=[xr, kb.bcR["LNb"]], writes=[xr])


def _ffn_sublayer(kb, tls, l, j):
    k = 0 if j == 0 else 2
    d = kb.dram
    _load_post(kb, tls[0], l, k, 0.5)
    for i, tl in enumerate(tls):
        kb.sel(i)
        _modulate(kb, tl)
    kb.next_pre_now()
    hTs = _ffn_up(kb, tls, l, j)
    for dc in range(8):
        slot, sR = kb.ws.get(("ffn_dn", l, j, dc, tls[0].kind, tls[0].seq, tls[0].t0), [
            (lambda t: t[:, 0:NFC * 128].rearrange("p (f c) -> p f c", f=NFC),
             d["ffn_w_down"][l, j, :, dc * 128:(dc + 1) * 128].rearrange("(f p) c -> p f c", p=128))])
        wv = slot[:, 0:NFC * 128].rearrange("p (f c) -> p f c", f=NFC)
        for i, tl in enumerate(tls):
            kb.sel(i)
            NT = tl.ntok
            b = i if len(tls) == 2 else dc % 2
            acc, aR = kb.pb[4 + b], kb.pbR[4 + b]
            for fc in range(NFC):
                kb.mm(acc[:, 0:NT], wv[:, fc, :], hTs[i][:, fc, 0:NT], fc == 0, fc == NFC - 1, reads=[sR, kb.hR2[i][fc]], writes=[aR], sig=(fc == NFC - 1))
            ysb, yR = kb.ysb[b], kb.ysbR[b]
            kb.cp("act", ysb[:, 0:NT], acc[:, 0:NT], reads=[aR], writes=[yR])
            _down_tail(kb, tl, dc, ysb, yR, b)
    for i, tl in enumerate(tls):
        kb.sel(i)
        _layer_norm(kb, tl)


def _load_x(kb, tl):
    d = kb.dram
    if tl.kind == "p":
        src = d["xp"][tl.seq, tl.t0:tl.t0 + tl.ntok, :].rearrange("(nb p) f -> p nb f", p=128)
        kb.P.dma("sp", kb.x[:, 0:tl.NB, :], src, writes=kb.xR[:tl.NB])
    else:
        kb.P.dma("sp", kb.x[:, 0, :], d["xs"][:, :], writes=[kb.xR[0]])


def _store_y(kb, tl):
    d = kb.dram
    if tl.kind == "p":
        dst = d["y_p"][tl.seq, tl.t0:tl.t0 + tl.ntok, :].rearrange("(nb p) f -> p nb f", p=128)
        kb.out_evs += kb.P.dma("sp", dst, kb.x[:, 0:tl.NB, :], reads=kb.xR[:tl.NB])
    else:
        kb.out_evs += kb.P.dma("sp", d["y_s"][:, :], kb.x[:, 0, :], reads=[kb.xR[0]])


def _program(kb):
    cfg = kb.cfg
    if (0 in cfg["layers"] and 1 in cfg.get("subs", [0, 1, 2])) or cfg.get("force_s5pre"):
        _s5_precompute(kb)
    if 1 in cfg["layers"]:
        _gm_setup(kb)
    if 2 in cfg["layers"]:
        _pool_setup(kb)
    if 3 in cfg["layers"]:
        _mla_setup(kb)
    _mod_phase(kb)
    tiles = cfg["tiles"]
    pairs = []
    i = 0
    while i < len(tiles):
        a = tiles[i]
        if (cfg.get("pair", True) and a.kind == "p" and i + 1 < len(tiles) and tiles[i + 1].kind == "p"
                and tiles[i + 1].seq == a.seq and tiles[i + 1].t0 == a.t0 + a.ntok):
            pairs.append([a, tiles[i + 1]])
            i += 2
        else:
            pairs.append([a])
            i += 1
    for tls in pairs:
        for i, tl in enumerate(tls):
            kb.sel(i)
            _load_x(kb, tl)
        subs = [(l, k) for l in cfg["layers"] for k in cfg.get("subs", [0, 1, 2])]
        _load_pre(kb, tls[0], *subs[0])
        for si, (l, k) in enumerate(subs):
            nxt = (lambda si=si: _load_pre(kb, tls[0], *subs[si + 1])) if si + 1 < len(subs) else (lambda: None)
            cnt = [0]

            def counted(nxt=nxt, cnt=cnt, n=len(tls)):
                cnt[0] += 1
                if cnt[0] == n:
                    nxt()
            kb.next_pre = counted
            kb.next_pre_now = nxt
            if k != 1:
                _ffn_sublayer(kb, tls, l, 0 if k == 0 else 1)
            else:
                for i, tl in enumerate(tls):
                    kb.sel(i)
                    MIXERS[l % 4](kb, tl, l)
        for i, tl in enumerate(tls):
            kb.sel(i)
            _store_y(kb, tl)


MIXERS = {}


def build(cfg):
    kb = KB(cfg)
    kb.P.no_self_wait = set(cfg.get("no_self_wait", ["pe"]))
    with kb.es:
        _declare(kb)
        kb.P.dry = True
        _program(kb)
        kb.P.dry = False
        kb.ws.reset()
        kb.out_evs = []
        _program(kb)
        kb.P.wait_all("sp", kb.out_evs)
        kb.P.check()
        kb.P.emit(kb.nc, kb.es)
    return kb


def _s5_declare(kb):
    kb.dscr("s5_xm", [T, D], BF16); kb.dscr("s5_z", [T, D], BF16)
    kb.dscr("s5_BtT", [128, 64, 128], BF16)
    kb.dscr("s5_TgT", [128, 64, 128], BF16)
    kb.dscr("s5_Ct", [128, 32, 2, 128], BF16)
    kb.s5A = kb.sb("s5A", [128, 14, 32])
    kb.s5Dt = kb.sb("s5Dt", [128, 64])
    kb.s5car = kb.sb("s5car", [128, 2, 32])
    kb.s5R = Res("s5consts"); kb.s5carR = Res("s5car"); kb.s5mR = Res("s5mats")
    kb.s5xmR = Res("s5xm"); kb.s5zR = Res("s5z")


def _s5_precompute(kb):
    P, d = kb.P, kb.dram
    W = kb.WS[:, :].bitcast(F32)
    sm = [kb.tmpf[0], kb.tmpf[1]]
    R = Res("s5pre")
    cnt = [0]

    def small():
        i = cnt[0]; cnt[0] += 1
        return sm[i // 32][:, (i % 32) * 32:(i % 32 + 1) * 32]

    def V(eng, fn):
        return P.op(eng, fn, reads=[R], writes=[R])

    def tt(o, a, b, op):
        V("dve", lambda e: e.tensor_tensor(out=o, in0=a, in1=b, op=op))

    def ts(o, a, s1, s2, op0, op1):
        V("dve", lambda e: e.tensor_scalar(out=o, in0=a, scalar1=s1, scalar2=s2, op0=op0, op1=op1))

    def cmul(ore, oim, are, aim, bre, bim, t1, t2):
        tt(t1, are, bre, ALU.mult); tt(t2, aim, bim, ALU.mult); tt(ore, t1, t2, ALU.subtract)
        tt(t1, are, bim, ALU.mult); tt(t2, aim, bre, ALU.mult); tt(oim, t1, t2, ALU.add)

    a_re, a_im, ldt = small(), small(), small()
    nat = kb.ysb[1]
    for j, (nm, dst) in enumerate((("s5_a_re", a_re), ("s5_a_im", a_im))):
        for gh in range(2):
            P.dma("sp", nat[0:32, j * 128 + gh * 64:j * 128 + (gh + 1) * 64], d[nm][0, gh * 32:(gh + 1) * 32, :], writes=[R])
    for gh in range(2):
        hs = slice(gh * 64, (gh + 1) * 64)
        P.dma("sp", ldt[hs, :], d["s5_log_dt"][0:1, gh * 32:(gh + 1) * 32].broadcast_to([64, 32]), writes=[R])
    P.dma("sp", nat[0:64, 256:384].rearrange("g (i c) -> g i c", i=8),
          d["s5_d"][0, :].rearrange("(g c) -> g c", c=16).unsqueeze(1).broadcast_to([64, 8, 16]), writes=[R])
    kb.tr(kb.pb[0][:, 0:32], nat[0:32, 0:128], kb.identf[0:32, 0:32], reads=[R, kb.cR], writes=[kb.pbR[0]])
    kb.tr(kb.pb[0][:, 32:64], nat[0:32, 128:256], kb.identf[0:32, 0:32], reads=[R, kb.cR], writes=[kb.pbR[0]])
    kb.tr(kb.pb[0][:, 64:128], nat[0:64, 256:384], kb.identf[0:64, 0:64], reads=[R, kb.cR], writes=[kb.pbR[0]], sig=True)
    P.op("dve", lambda e: e.tensor_copy(out=a_re, in_=kb.pb[0][:, 0:32]), reads=[kb.pbR[0], R], writes=[R])
    P.op("dve", lambda e: e.tensor_copy(out=a_im, in_=kb.pb[0][:, 32:64]), reads=[kb.pbR[0], R], writes=[R])
    P.op("dve", lambda e: e.tensor_copy(out=kb.s5Dt[:, :], in_=kb.pb[0][:, 64:128]), reads=[kb.pbR[0], R], writes=[R, kb.s5R])
    if kb.cfg.get("s5stop", 99) <= 1:
        return
    off = [0]

    def big(n):
        o = off[0]; off[0] += n
        return W[:, o:o + n]
    Bre, Bim, bbre, bbim, Cre, Cim = [big(512).rearrange("p (g c) -> p g c", c=16) for _ in range(6)]
    for gh in range(2):
        hs = slice(gh * 64, (gh + 1) * 64)
        P.dma("sp", Bre[hs], d["s5_b_re"][0, gh * 32:(gh + 1) * 32].rearrange("g p c -> p g c"), writes=[R])
        P.dma("sp", Bim[hs], d["s5_b_im"][0, gh * 32:(gh + 1) * 32].rearrange("g p c -> p g c"), writes=[R])
    for nm, Ct_ in (("s5_c_re", Cre), ("s5_c_im", Cim)):
        cn = kb.ysb[0][:, :].rearrange("p (t h q) -> p t h q", t=4, h=2)
        for t in range(4):
            for gh in range(2):
                P.dma("sp", cn[:, t, gh, :], d[nm][0, gh * 32 + t * 8:gh * 32 + t * 8 + 8].rearrange("g c p -> (g c) p"), writes=[R])
        for t in range(4):
            kb.tr(kb.pb[0][:, t * 128:(t + 1) * 128], kb.ysb[0][:, t * 128:(t + 1) * 128], kb.identf[:, :], reads=[R, kb.cR], writes=[kb.pbR[0]], sig=(t == 3))
        P.op("dve", lambda e, Ct_=Ct_: e.tensor_copy(out=Ct_, in_=kb.pb[0][:, :].rearrange("p (g c) -> p g c", c=16)), reads=[kb.pbR[0], R], writes=[R])
    if kb.cfg.get("s5stop", 99) <= 2:
        return
    dt, ar, th, rr, t1, t2, cs_, sn_, are, aim = [small() for _ in range(10)]
    V("act", lambda e: e.activation(out=dt, in_=ldt, func=AF.Exp))
    tt(ar, a_re, dt, ALU.mult); tt(th, a_im, dt, ALU.mult)
    V("act", lambda e: e.activation(out=rr, in_=ar, func=AF.Exp))

    def sin_of(o, ang, shift):
        u, k, ki = small(), small(), small()
        ts(u, ang, 1.0 / (2 * math.pi), shift, ALU.mult, ALU.add)
        kint = ki.bitcast(I32)
        V("dve", lambda e: e.tensor_copy(out=kint, in_=u))
        V("dve", lambda e: e.tensor_copy(out=k, in_=kint))
        tt(u, u, k, ALU.subtract)
        ts(u, u, -0.5, 0.5, ALU.max, ALU.min)
        V("act", lambda e: e.activation(out=o, in_=u, func=AF.Sin, scale=2 * math.pi))
    sin_of(sn_, th, 0.0)
    sin_of(cs_, th, 0.25)
    tt(are, rr, cs_, ALU.mult); tt(aim, rr, sn_, ALU.mult)
    if kb.cfg.get("s5stop", 99) <= 3:
        return
    nre, den, qre, qim, t3 = [small() for _ in range(5)]
    ts(nre, are, -1.0, None, ALU.add, ALU.bypass)
    tt(t1, a_re, a_re, ALU.mult); tt(t2, a_im, a_im, ALU.mult); tt(den, t1, t2, ALU.add)
    V("dve", lambda e: e.reciprocal(out=den, in_=den))
    tt(t1, nre, a_re, ALU.mult); tt(t2, aim, a_im, ALU.mult); tt(qre, t1, t2, ALU.add); tt(qre, qre, den, ALU.mult)
    tt(t1, aim, a_re, ALU.mult); tt(t2, nre, a_im, ALU.mult); tt(qim, t1, t2, ALU.subtract); tt(qim, qim, den, ALU.mult)
    T1, T2 = [big(512).rearrange("p (g c) -> p g c", c=16) for _ in range(2)]

    def bc(sm_ap, lo=0, n=32):
        return sm_ap[:, lo:lo + n].unsqueeze(2).broadcast_to([128, n, 16])
    cmul(bbre, bbim, Bre, Bim, bc(qre), bc(qim), T1, T2)
    if kb.cfg.get("s5stop", 99) <= 4:
        return
    pw = [(small(), small()) for _ in range(9)]
    ipw = [(small(), small()) for _ in range(8)]
    for (re_, im_) in (pw[0], ipw[0]):
        V("dve", lambda e, re_=re_: e.memset(re_, 1.0)); V("dve", lambda e, im_=im_: e.memset(im_, 0.0))
    ts3, ts4 = small(), small()
    for l in range(8):
        cmul(pw[l + 1][0], pw[l + 1][1], pw[l][0], pw[l][1], are, aim, ts3, ts4)
    ire, iim, m2 = small(), small(), small()
    tt(t1, are, are, ALU.mult); tt(t2, aim, aim, ALU.mult); tt(m2, t1, t2, ALU.add)
    V("dve", lambda e: e.reciprocal(out=m2, in_=m2))
    tt(ire, are, m2, ALU.mult); tt(iim, aim, m2, ALU.mult); ts(iim, iim, -1.0, None, ALU.mult, ALU.bypass)
    for l in range(7):
        cmul(ipw[l + 1][0], ipw[l + 1][1], ipw[l][0], ipw[l][1], ire, iim, ts3, ts4)
    A = kb.s5A
    P.op("dve", lambda e: e.tensor_copy(out=A[:, 0, :], in_=pw[8][0]), reads=[R], writes=[kb.s5R])
    P.op("dve", lambda e: e.tensor_copy(out=A[:, 1, :], in_=pw[8][1]), reads=[R], writes=[kb.s5R])
    P.op("dve", lambda e: e.tensor_copy(out=A[:, 2, :], in_=pw[8][0]), reads=[R], writes=[kb.s5R])
    P.op("dve", lambda e: e.tensor_copy(out=A[:, 3, :], in_=pw[8][1]), reads=[R], writes=[kb.s5R])
    for s in range(5):
        o = 2 + 2 * s
        P.op("dve", lambda e, o=o: e.tensor_tensor(out=ts3, in0=A[:, o, :], in1=A[:, o, :], op=ALU.mult), reads=[R, kb.s5R], writes=[R])
        P.op("dve", lambda e, o=o: e.tensor_tensor(out=ts4, in0=A[:, o + 1, :], in1=A[:, o + 1, :], op=ALU.mult), reads=[R, kb.s5R], writes=[R])
        P.op("dve", lambda e, o=o: e.tensor_tensor(out=A[:, o + 2, :], in0=ts3, in1=ts4, op=ALU.subtract), reads=[R, kb.s5R], writes=[R, kb.s5R])
        P.op("dve", lambda e, o=o: e.tensor_tensor(out=ts3, in0=A[:, o, :], in1=A[:, o + 1, :], op=ALU.mult), reads=[R, kb.s5R], writes=[R])
        P.op("dve", lambda e, o=o: e.tensor_scalar(out=A[:, o + 3, :], in0=ts3, scalar1=2.0, scalar2=None, op0=ALU.mult, op1=ALU.bypass), reads=[R, kb.s5R], writes=[R, kb.s5R])
    if kb.cfg.get("s5stop", 99) <= 5:
        return
    Cpr = big(8 * 9 * 16).rearrange("p (g l c) -> p g l c", g=8, l=9)
    Cpi = big(8 * 9 * 16).rearrange("p (g l c) -> p g l c", g=8, l=9)
    Bnr_f, Bni_f, Xr_f, Xi_f = [big(8 * 8 * 16) for _ in range(4)]
    Bnr, Bni, Xr, Xi = [a.rearrange("p (g l c) -> p g l c", g=8, l=8) for a in (Bnr_f, Bni_f, Xr_f, Xi_f)]
    U1 = big(128).rearrange("p (g c) -> p g c", c=16); U2 = big(128).rearrange("p (g c) -> p g c", c=16)
    Cbd_r, Cbd_i = big(256), big(256)
    V("dve", lambda e: e.memset(Cbd_r, 0.0)); V("dve", lambda e: e.memset(Cbd_i, 0.0))
    stg = kb.xmb[0]
    stgR = kb.xmbR[0]
    for qd in range(4):
        g0 = qd * 8
        gs = slice(g0, g0 + 8)
        for l in range(9):
            cmul(Cpr[:, :, l, :], Cpi[:, :, l, :], Cre[:, gs, :], Cim[:, gs, :], bc(pw[l][0], g0, 8), bc(pw[l][1], g0, 8), U1, U2)
            ts(Cpi[:, :, l, :], Cpi[:, :, l, :], -1.0, None, ALU.mult, ALU.bypass)
        for i in range(8):
            cmul(Bnr[:, :, i, :], Bni[:, :, i, :], bbre[:, gs, :], bbim[:, gs, :], bc(ipw[i][0], g0, 8), bc(ipw[i][1], g0, 8), U1, U2)
            cmul(Xr[:, :, i, :], Xi[:, :, i, :], bbre[:, gs, :], bbim[:, gs, :], bc(pw[7 - i][0], g0, 8), bc(pw[7 - i][1], g0, 8), U1, U2)
        if kb.cfg.get("s5stop", 99) <= 6:
            return
        for gl in range(8):
            glg = g0 + gl
            P.op("dve", lambda e, gl=gl: e.tensor_copy(out=stg[:, 0:128].rearrange("p (l c) -> p l c", c=16), in_=Cpr[:, gl, 1:9, :]), reads=[R], writes=[stgR])
            P.op("dve", lambda e, gl=gl: e.tensor_copy(out=stg[:, 128:256].rearrange("p (l c) -> p l c", c=16), in_=Cpi[:, gl, 1:9, :]), reads=[R], writes=[stgR])
            P.dma("sp", d["s5_Ct"][:, glg, :, :], stg[:, 0:256].rearrange("p (a b) -> p a b", a=2), reads=[stgR], writes=[kb.s5mR])
            if "bt" in kb.cfg.get("s5skip", ""):
                continue
            kb.tr(kb.pb[1][:, 0:128], Xr_f[:, gl * 128:(gl + 1) * 128], kb.identf[:, :], reads=[R, kb.cR], writes=[kb.pbR[1]], sig=False)
            kb.tr(kb.pb[1][:, 128:256], Xi_f[:, gl * 128:(gl + 1) * 128], kb.identf[:, :], reads=[R, kb.cR], writes=[kb.pbR[1]], sig=True)
            sv = stg[:, 256:512].rearrange("p (h r q) -> p h r q", h=2, r=2)
            P.op("dve", lambda e, sv=sv: e.tensor_copy(out=sv[:, :, 0, :], in_=kb.pb[1][:, 0:128].rearrange("p (h q) -> p h q", h=2)), reads=[kb.pbR[1]], writes=[stgR])
            P.op("dve", lambda e, sv=sv: e.tensor_copy(out=sv[:, :, 1, :], in_=kb.pb[1][:, 128:256].rearrange("p (h q) -> p h q", h=2)), reads=[kb.pbR[1]], writes=[stgR])
            for gh in range(2):
                P.dma("sp", d["s5_BtT"][:, gh * 32 + glg, :], stg[:, 256 + gh * 128:256 + (gh + 1) * 128], reads=[stgR], writes=[kb.s5mR])
            if "tg" in kb.cfg.get("s5skip", ""):
                continue
            for gh in range(2):
                hs = slice(gh * 64, (gh + 1) * 64)
                tt(Cbd_r[hs, gh * 128:(gh + 1) * 128].rearrange("p (l c) -> p l c", c=16), Cpr[hs, gl, 0:8, :], Cpr[hs, gl, 0:8, :], ALU.bypass)
                tt(Cbd_i[hs, gh * 128:(gh + 1) * 128].rearrange("p (l c) -> p l c", c=16), Cpi[hs, gl, 0:8, :], Cpi[hs, gl, 0:8, :], ALU.bypass)
            kb.mm(kb.pb[2][:, 0:256], Bnr_f[:, gl * 128:(gl + 1) * 128], Cbd_r[:, :], True, False, reads=[R], writes=[kb.pbR[2]])
            kb.mm(kb.pb[2][:, 0:256], Bni_f[:, gl * 128:(gl + 1) * 128], Cbd_i[:, :], False, True, reads=[R], writes=[kb.pbR[2]], sig=True)
            for gh in range(2):
                P.op("dve", lambda e, gh=gh: e.tensor_tensor(out=stg[:, 512 + gh * 128:512 + (gh + 1) * 128], in0=kb.pb[2][:, gh * 128:(gh + 1) * 128],
                                                             in1=kb.tmask[:, :], op=ALU.mult), reads=[kb.pbR[2], kb.cR], writes=[stgR])
                P.dma("sp", d["s5_TgT"][:, gh * 32 + glg, :], stg[:, 512 + gh * 128:512 + (gh + 1) * 128], reads=[stgR], writes=[kb.s5mR])
    kb.barrier("dve", [R], kb.hR + kb.tmpfR + kb.xmbR + kb.ysbR + kb.sgtR + kb.xmTR)


def _barrier(kb, eng, src, dst):
    kb.P.op(eng, lambda e: e.memset(kb.dummy[:, 0:2], 0.0), reads=[], writes=list(src) + list(dst))


KB.barrier = _barrier


def _s5_mixer(kb, tl, l):
    P, d = kb.P, kb.dram
    NT, NB = tl.ntok, tl.NB
    NCH = NT // 8
    _load_post(kb, tl, l, 1, 1.0)
    W = kb.WS
    XC = W[:, 0:8192].rearrange("p (i f) -> p i f", i=8)
    Uall = W[:, 8192:12288].rearrange("p (g j) -> p g j", g=64)
    Wf = W[:, :].bitcast(F32)
    Vre = Wf[:, 6144:8192].rearrange("p (g j) -> p g j", g=32)
    Vim = Wf[:, 8192:10240].rearrange("p (g j) -> p g j", g=32)
    tB = Wf[:, 10240:12288].rearrange("p (g j) -> p g j", g=32)
    tA = kb.xmT[:, :, :].rearrange("p k t -> p (k t)").bitcast(F32).rearrange("p (g j) -> p g j", g=32)
    XCp = W[:, 12288:20480]
    XCpR = Res("XCp")
    XCR, UR, VreR, VimR, tAR, tBR, ZR = [Res(n) for n in ("XC", "Uall", "Vre", "Vim", "tA", "tB", "Zall")]
    kb.barrier("dve", kb.hR + kb.xmTR, [XCR, UR, VreR, VimR, tAR, tBR, ZR, XCpR])
    for nb in range(NB):
        b = nb % 2
        tmp, tR = kb.tmpf[b], kb.tmpfR[b]
        kb.tt("dve", tmp[:, :], kb.x[:, nb, :], kb.bc["S1"][:, :], ALU.mult, reads=[kb.xR[nb], kb.bcR["S1"]], writes=[tR])
        xmb, xR_ = kb.xmb[b], kb.xmbR[b]
        kb.tt("dve", xmb[:, :], tmp[:, :], kb.bc["Bm"][:, :], ALU.add, reads=[tR, kb.bcR["Bm"]], writes=[xR_])
        P.dma("sp", d["s5_xm"][nb * 128:(nb + 1) * 128, :], xmb[:, :], reads=[xR_], writes=[kb.s5xmR])
    kb.next_pre()
    P.dma("sp", XC[0:NCH, :, :], d["s5_xm"][0:NT, :].rearrange("(j i) f -> j i f", i=8), reads=[kb.s5xmR], writes=[XCR])
    for hh in range(2):
        kb.cp("dve" if hh == 0 else "act", XCp[0:NCH, :].rearrange("p (g i c) -> p i g c", g=64, i=8)[:, :, hh * 32:(hh + 1) * 32, :],
              XC[0:NCH, :, :].rearrange("p i (g c) -> p i g c", c=16)[:, :, hh * 32:(hh + 1) * 32, :], reads=[XCR], writes=[XCpR])
    for gb in range(4):
        for gg in range(16):
            g = gb * 16 + gg
            kb.tr(kb.pt[:, gg * NCH:(gg + 1) * NCH], XCp[0:NCH, g * 128:(g + 1) * 128], kb.identb[0:NCH, 0:NCH],
                  reads=[XCpR, kb.cR], writes=[kb.pbR[7]], sig=(gg == 15))
        kb.cp("act", Uall[:, gb * 16:(gb + 1) * 16, 0:NCH], kb.pt[:, 0:16 * NCH].rearrange("p (g j) -> p g j", g=16),
              reads=[kb.pbR[7]], writes=[UR])
    kb.barrier("dve", [XCpR], [VreR, VimR])
    for gb in range(4):
        parts = []
        for gh in range(2):
            g0 = gh * 32 + gb * 8
            parts.append((lambda t, gh=gh: t[:, gh * 1024:(gh + 1) * 1024].rearrange("p (g m) -> p g m", g=8), d["s5_BtT"][:, g0:g0 + 8, :]))
        slot, sR = kb.ws.get(("s5B", tl.kind, tl.seq, tl.t0, gb), parts, reads=[kb.s5mR])
        if not P.dry and gb == 0:
            pass
        for ri in range(2):
            bank = kb.pb[ri]
            for gl8 in range(8):
                for gh in range(2):
                    bt = slot[:, gh * 1024 + gl8 * 128 + ri * 64: gh * 1024 + gl8 * 128 + ri * 64 + 64]
                    g = gh * 32 + gb * 8 + gl8
                    kb.mm(bank[gh * 64:(gh + 1) * 64, gl8 * NCH:(gl8 + 1) * NCH], bt, Uall[:, g, 0:NCH], True, True,
                          reads=[sR, UR, kb.s5mR], writes=[kb.pbR[ri]], sig=(gl8 == 7 and gh == 1))
            dst, dR = (Vre, VreR) if ri == 0 else (Vim, VimR)
            kb.cp("act", dst[:, gb * 8:(gb + 1) * 8, 0:NCH], bank[:, 0:8 * NCH].rearrange("p (g j) -> p g j", g=8),
                  reads=[kb.pbR[ri]], writes=[dR])
    A = kb.s5A

    def Ab(idx, n):
        return A[:, idx, :].unsqueeze(2).broadcast_to([128, 32, n])
    Sp = tB.rearrange("p g j -> p (g j)").bitcast(BF16)
    Spre = Sp[:, 0:2048].rearrange("p (g j) -> p g j", g=32)
    Spim = Sp[:, 2048:4096].rearrange("p (g j) -> p g j", g=32)
    SpR = Res("Sp")
    if tl.kind == "p":
        car = kb.s5car
        if tl.t0 == 0:
            kb.memset("dve", car[:, :, :], 0.0, writes=[kb.s5carR])
        c_re, c_im = car[:, 0, :].unsqueeze(2), car[:, 1, :].unsqueeze(2)
        a1, a2 = tA[:, :, 0:1], tA[:, :, 1:2]
        kb.tt("dve", a1, c_re, Ab(0, 1), ALU.mult, reads=[kb.s5carR, kb.s5R], writes=[tAR])
        kb.tt("dve", a2, c_im, Ab(1, 1), ALU.mult, reads=[kb.s5carR, kb.s5R], writes=[tAR])
        kb.tt("dve", a1, a1, a2, ALU.subtract, reads=[tAR], writes=[tAR])
        kb.tt("dve", Vre[:, :, 0:1], Vre[:, :, 0:1], a1, ALU.add, reads=[tAR, VreR], writes=[VreR])
        kb.tt("dve", a1, c_re, Ab(1, 1), ALU.mult, reads=[kb.s5carR, kb.s5R], writes=[tAR])
        kb.tt("dve", a2, c_im, Ab(0, 1), ALU.mult, reads=[kb.s5carR, kb.s5R], writes=[tAR])
        kb.tt("dve", a1, a1, a2, ALU.add, reads=[tAR], writes=[tAR])
        kb.tt("dve", Vim[:, :, 0:1], Vim[:, :, 0:1], a1, ALU.add, reads=[tAR, VimR], writes=[VimR])
        s = 1
        k = 0
        while s < NCH:
            n = NCH - s
            ar_, ai_ = Ab(2 + 2 * k, n), Ab(3 + 2 * k, n)
            re_sh, im_sh = Vre[:, :, 0:n], Vim[:, :, 0:n]
            TA, TB = tA[:, :, 0:n], tB[:, :, 0:n]
            kb.tt("dve", TA, re_sh, ar_, ALU.mult, reads=[VreR, kb.s5R], writes=[tAR])
            kb.tt("dve", TB, im_sh, ai_, ALU.mult, reads=[VimR, kb.s5R], writes=[tBR])
            kb.tt("dve", TA, TA, TB, ALU.subtract, reads=[tAR, tBR], writes=[tAR])
            kb.tt("dve", TB, re_sh, ai_, ALU.mult, reads=[VreR, kb.s5R, tAR], writes=[tBR])
            kb.tt("dve", Vre[:, :, s:NCH], Vre[:, :, s:NCH], TA, ALU.add, reads=[tAR, tBR], writes=[VreR])
            kb.tt("dve", TA, im_sh, ar_, ALU.mult, reads=[VimR, kb.s5R], writes=[tAR])
            kb.tt("dve", TB, TB, TA, ALU.add, reads=[tAR], writes=[tBR])
            kb.tt("dve", Vim[:, :, s:NCH], Vim[:, :, s:NCH], TB, ALU.add, reads=[tBR], writes=[VimR])
            s *= 2
            k += 1
        kb.barrier("dve", [tBR], [SpR])
        kb.cp("dve", Spre[:, :, 0:1], c_re, reads=[kb.s5carR, tBR], writes=[SpR])
        kb.cp("dve", Spim[:, :, 0:1], c_im, reads=[kb.s5carR], writes=[SpR])
        kb.cp("dve", Spre[:, :, 1:NCH], Vre[:, :, 0:NCH - 1], reads=[VreR], writes=[SpR])
        kb.cp("act", Spim[:, :, 1:NCH], Vim[:, :, 0:NCH - 1], reads=[VimR], writes=[SpR])
        kb.cp("dve", car[:, 0, :].unsqueeze(2), Vre[:, :, NCH - 1:NCH], reads=[VreR, SpR], writes=[kb.s5carR])
        kb.cp("dve", car[:, 1, :].unsqueeze(2), Vim[:, :, NCH - 1:NCH], reads=[VimR], writes=[kb.s5carR])
        if tl.t0 + NT == SEQ:
            for ri, nm in ((0, "ssm_re_p"), (1, "ssm_im_p")):
                kb.tr(kb.pb[2][0:32, ri * 128:(ri + 1) * 128], car[:, ri, :], kb.identf[:, :], reads=[kb.s5carR, kb.cR], writes=[kb.pbR[2]], sig=(ri == 1))
            ot = kb.tmpf[0]; otR = kb.tmpfR[0]
            kb.cp("act", ot[0:32, 0:256], kb.pb[2][0:32, 0:256], reads=[kb.pbR[2]], writes=[otR])
            for ri, nm in ((0, "ssm_re_p"), (1, "ssm_im_p")):
                for gh in range(2):
                    kb.out_evs += P.dma("sp", d[nm][tl.seq, gh * 32:(gh + 1) * 32, :],
                                        ot[0:32, ri * 128 + gh * 64:ri * 128 + (gh + 1) * 64], reads=[otR])
    else:
        sin = kb.ysb[0]; sinR = kb.ysbR[0]
        Sin_re = tA[:, :, 0:NCH]; Sin_im = tA[:, :, NCH:2 * NCH]
        for ri, nm in ((0, "st_re"), (1, "st_im")):
            dstS = Sin_re if ri == 0 else Sin_im
            nat = kb.tmpf[ri]; natR = kb.tmpfR[ri]
            for rb in range(4):
                nv = nat[0:NSS, :].rearrange("q (g h p) -> q h g p", g=8, h=2)
                P.dma_group("sp", [(nv[:, gh, :, :], d[nm][:, gh * 32 + rb * 8:gh * 32 + rb * 8 + 8, :]) for gh in range(2)], writes=[natR])
                for gl8 in range(8):
                    kb.tr(kb.pb[2][:, gl8 * NSS:(gl8 + 1) * NSS], nat[0:NSS, gl8 * 128:(gl8 + 1) * 128], kb.identf[0:NSS, 0:NSS],
                          reads=[natR, kb.cR], writes=[kb.pbR[2]], sig=(gl8 == 7))
                kb.cp("act", dstS[:, rb * 8:(rb + 1) * 8, :], kb.pb[2][:, 0:8 * NSS].rearrange("p (g q) -> p g q", g=8),
                      reads=[kb.pbR[2]], writes=[tAR])
        kb.barrier("dve", [tBR], [SpR])
        kb.cp("dve", Spre[:, :, 0:NCH], Sin_re, reads=[tAR, tBR], writes=[SpR])
        kb.cp("dve", Spim[:, :, 0:NCH], Sin_im, reads=[tAR], writes=[SpR])
        X1, X2 = tA[:, :, 2 * NCH:3 * NCH], tA[:, :, 3 * NCH:4 * NCH]
        ar_, ai_ = Ab(0, NCH), Ab(1, NCH)
        kb.tt("dve", X1, Sin_re, ar_, ALU.mult, reads=[tAR, kb.s5R], writes=[tAR])
        kb.tt("dve", X2, Sin_im, ai_, ALU.mult, reads=[tAR, kb.s5R], writes=[tAR])
        kb.tt("dve", X1, X1, X2, ALU.subtract, reads=[tAR], writes=[tAR])
        kb.tt("dve", Vre[:, :, 0:NCH], Vre[:, :, 0:NCH], X1, ALU.add, reads=[tAR, VreR], writes=[VreR])
        kb.tt("dve", X1, Sin_re, ai_, ALU.mult, reads=[tAR, kb.s5R], writes=[tAR])
        kb.tt("dve", X2, Sin_im, ar_, ALU.mult, reads=[tAR, kb.s5R], writes=[tAR])
        kb.tt("dve", X1, X1, X2, ALU.add, reads=[tAR], writes=[tAR])
        kb.tt("dve", Vim[:, :, 0:NCH], Vim[:, :, 0:NCH], X1, ALU.add, reads=[tAR, VimR], writes=[VimR])
        for ri, nm in ((0, "ssm_re_s"), (1, "ssm_im_s")):
            srcS, sR_ = (Vre, VreR) if ri == 0 else (Vim, VimR)
            ot = kb.tmpf[ri]; otR = kb.tmpfR[ri]
            for r4 in range(8):
                for g4 in range(4):
                    kb.tr(kb.pb[3][0:NSS, g4 * 128:(g4 + 1) * 128], srcS[:, r4 * 4 + g4, 0:NCH], kb.identf[:, :],
                          reads=[sR_, kb.cR], writes=[kb.pbR[3]], sig=(g4 == 3))
                kb.cp("act", ot[0:NSS, 0:512].rearrange("q (h g p) -> q g h p", h=2, g=4),
                      kb.pb[3][0:NSS, :].rearrange("q (g h p) -> q g h p", g=4, h=2), reads=[kb.pbR[3]], writes=[otR])
                for gh in range(2):
                    kb.out_evs += P.dma("sp", d[nm][:, gh * 32 + r4 * 4:gh * 32 + r4 * 4 + 4, :].rearrange("q g p -> q (g p)"),
                                        ot[0:NSS, gh * 256:(gh + 1) * 256], reads=[otR])
    for gb in range(8):
        g0 = gb * 8
        gh = g0 // 32
        gl0 = g0 % 32
        hs = slice(gh * 64, (gh + 1) * 64)
        slot, sR = kb.ws.get(("s5T", tl.kind, tl.seq, tl.t0, gb), [
            (lambda t: t[:, 0:1024].rearrange("p (g m) -> p g m", g=8), d["s5_TgT"][:, g0:g0 + 8, :]),
            (lambda t, hs=hs: t[hs, 1024:3072].rearrange("p (g a m) -> p g a m", g=8, a=2), d["s5_Ct"][hs, gl0:gl0 + 8, :, :])], reads=[kb.s5mR])
        bank = kb.pb[gb % 2]
        bR = kb.pbR[gb % 2]
        for g8 in range(8):
            g = g0 + g8
            gl = gl0 + g8
            o = bank[:, g8 * NCH:(g8 + 1) * NCH]
            kb.mm(o, slot[:, g8 * 128:(g8 + 1) * 128], Uall[:, g, 0:NCH], True, False, reads=[sR, UR, kb.s5mR], writes=[bR])
            kb.mm(o, slot[hs, 1024 + g8 * 256:1024 + g8 * 256 + 128], Spre[hs, gl, 0:NCH], False, False, reads=[sR, SpR], writes=[bR])
            kb.mm(o, slot[hs, 1024 + g8 * 256 + 128:1024 + g8 * 256 + 256], Spim[hs, gl, 0:NCH], False, True, reads=[sR, SpR], writes=[bR], sig=(g8 == 7))
        tmp = kb.ysb[gb % 2][:, 0:8 * NCH].rearrange("p (g j) -> p g j", g=8)
        tR = kb.ysbR[gb % 2]
        Ug = Uall[:, g0:g0 + 8, 0:NCH]
        kb.tt("dve", tmp, Ug, kb.s5Dt[:, g0:g0 + 8].unsqueeze(2).broadcast_to([128, 8, NCH]), ALU.mult, reads=[UR, kb.s5R], writes=[tR])
        kb.tt("dve", tmp, tmp, bank[:, 0:8 * NCH].rearrange("p (g j) -> p g j", g=8), ALU.add, reads=[tR, bR], writes=[tR])
        kb.act(Ug, tmp, AF.Gelu_apprx_tanh, reads=[tR, UR], writes=[UR])
    for gb in range(8):
        for g8 in range(8):
            g = gb * 8 + g8
            kb.tr(kb.pt[0:NCH, g8 * 128:(g8 + 1) * 128], Uall[:, g, 0:NCH], kb.identb[:, :], reads=[UR, kb.cR], writes=[kb.pbR[7]], sig=(g8 == 7))
        kb.cp("act", XCp[0:NCH, gb * 1024:(gb + 1) * 1024], kb.pt[0:NCH, :], reads=[kb.pbR[7], VreR, VimR], writes=[XCpR])
    for hh in range(2):
        kb.cp("dve" if hh == 0 else "act", XC[0:NCH, :, :].rearrange("p i (g c) -> p i g c", c=16)[:, :, hh * 32:(hh + 1) * 32, :],
              XCp[0:NCH, :].rearrange("p (g i c) -> p i g c", g=64, i=8)[:, :, hh * 32:(hh + 1) * 32, :], reads=[XCpR], writes=[XCR])
    P.dma("sp", d["s5_z"][0:NT, :].rearrange("(j i) f -> j i f", i=8), XC[0:NCH, :, :], reads=[XCR], writes=[kb.s5zR])
    for nb in range(NB):
        b = nb % 2
        zb, zR = kb.xmb[b], kb.xmbR[b]
        P.dma("sp", zb[:, :], d["s5_z"][nb * 128:(nb + 1) * 128, :], reads=[kb.s5zR], writes=[zR])
        for k in range(8):
            kb.tr(kb.pt[:, k * 128:(k + 1) * 128], zb[:, k * 128:(k + 1) * 128], kb.identb[:, :], reads=[zR, kb.cR, tAR, SpR], writes=[kb.pbR[7]], sig=(k == 7))
        kb.cp("act", kb.xmT[:, :, nb * 128:(nb + 1) * 128], kb.pt[:, :].rearrange("p (k t) -> p k t", k=8),
              reads=[kb.pbR[7], tAR], writes=[kb.xmTR[nb]])
    xr = kb.xmTR[:NB]

    def evac(dc, ysb, yR, acc, aR):
        pass
    for dc in range(8):
        slot, sR = kb.ws.get(("s5o", tl.kind, tl.seq, tl.t0, dc), [
            (lambda t: t[:, 0:1024].rearrange("p (k c) -> p k c", k=8), d["s5_w_out"][0, :, dc * 128:(dc + 1) * 128].rearrange("(k p) c -> p k c", p=128)),
            (lambda t: t[:, 1024:2048].rearrange("p (k c) -> p k c", k=8), d["s5_w_gate"][0, :, dc * 128:(dc + 1) * 128].rearrange("(k p) c -> p k c", p=128))])
        b = dc % 2
        acc, aR = kb.pb[4 + b], kb.pbR[4 + b]
        acg, agR = kb.pb[2 + b], kb.pbR[2 + b]
        for k in range(8):
            kb.mm(acc[:, 0:NT], slot[:, k * 128:(k + 1) * 128], kb.xmT[:, k, 0:NT], k == 0, k == 7, reads=[sR] + xr, writes=[aR], sig=(k == 7))
        for k in range(8):
            kb.mm(acg[:, 0:NT], slot[:, 1024 + k * 128:1024 + (k + 1) * 128], kb.xmT[:, k, 0:NT], k == 0, k == 7, reads=[sR] + xr, writes=[agR], sig=(k == 7))
        sg, sgR = kb.tmpf[b], kb.tmpfR[b]
        kb.act(sg[:, 0:NT], acg[:, 0:NT], AF.Sigmoid, reads=[agR], writes=[sgR])
        ysb, yR = kb.ysb[b], kb.ysbR[b]
        kb.tt("dve", ysb[:, 0:NT], acc[:, 0:NT], sg[:, 0:NT], ALU.mult, reads=[aR, sgR], writes=[yR])
        _down_tail(kb, tl, dc, ysb, yR)
    _layer_norm(kb, tl)
    kb.barrier("dve", [XCR, UR, VreR, VimR, tAR, tBR, ZR, SpR, XCpR], kb.hR + kb.xmTR)


MIXERS[0] = _s5_mixer


def _ln_stats(kb, src_fn, nb_res, eps):
    st6, st6R, mv, mvR = kb.st6, kb.st6R, kb.mv, kb.mvR
    for h in range(2):
        kb.P.op("dve", lambda e, h=h: e.bn_stats(out=st6[:, h, :], in_=src_fn(h)), reads=[nb_res], writes=[st6R])
    kb.P.op("dve", lambda e: e.bn_aggr(out=mv[:, 0:2], in_=st6[:, :, :].rearrange("p a b -> p (a b)")),
            reads=[st6R], writes=[mvR])
    kb.act(mv[:, 2:3], mv[:, 1:2], AF.Sqrt, reads=[mvR], writes=[mvR], bias=eps, scale=1.0)
    kb.P.op("dve", lambda e: e.reciprocal(out=mv[:, 3:4], in_=mv[:, 2:3]), reads=[mvR], writes=[mvR])
    kb.stt(mv[:, 4:5], mv[:, 0:1], -1.0, mv[:, 3:4], ALU.mult, ALU.mult, reads=[mvR], writes=[mvR])


def _to_featT(kb, tl, nb, src, srcR):
    for k in range(8):
        kb.tr(kb.pt[:, k * 128:(k + 1) * 128], src[:, k * 128:(k + 1) * 128], kb.identb[:, :],
              reads=[srcR, kb.cR], writes=[kb.pbR[7]], sig=(k == 7))
    kb.cp("act", kb.xmT[:, :, nb * 128:(nb + 1) * 128], kb.pt[:, :].rearrange("p (k t) -> p k t", k=8),
          reads=[kb.pbR[7]], writes=[kb.xmTR[nb]])


def _gm_declare(kb):
    kb.din("k_tri", [128, 128])
    kb.gm_wsT = kb.sb("gm_wsT", [128, 8, 128], BF16)
    kb.gm_wsTs = kb.sb("gm_wsTs", [128, 8, 128], BF16)
    kb.gm_bs = kb.sb("gm_bs", [128, 16])
    kb.gmR = Res("gm_consts")


def _gm_setup(kb):
    P, d = kb.P, kb.dram
    tri = kb.tmpf[1][:, 0:128]; triR = kb.tmpfR[1]
    P.dma("sp", tri, d["k_tri"][:, :], writes=[triR])
    nat = kb.tmpf[0]; natR = kb.tmpfR[0]
    for var in range(2):
        natv = nat[:, :].rearrange("p (h s) -> p h s", h=8)
        if var == 0:
            P.dma("sp", natv, d["gm_w_s"][0].rearrange("h t s -> t h s"), writes=[natR])
        else:
            kb.memset("dve", nat[:, :], 0.0, writes=[natR])
            P.dma_group("sp", [(natv[q * 8:(q + 1) * 8, :, q * 8:(q + 1) * 8], d["gm_w_s"][0, :, 0:8, 0:8].rearrange("h i j -> i h j"))
                               for q in range(NSS)], writes=[natR])
        dst = kb.gm_wsT if var == 0 else kb.gm_wsTs
        for hb in range(2):
            for h4 in range(4):
                h = hb * 4 + h4
                kb.tr(kb.pb[0][:, h4 * 128:(h4 + 1) * 128], nat[:, h * 128:(h + 1) * 128], kb.identf[:, :],
                      reads=[natR, kb.cR], writes=[kb.pbR[0]], sig=(h4 == 3))
            kb.tt("dve", dst[:, hb * 4:(hb + 1) * 4, :], kb.pb[0][:, :].rearrange("p (h t) -> p h t", h=4),
                  tri.unsqueeze(1).broadcast_to([128, 4, 128]), ALU.mult, reads=[kb.pbR[0], triR], writes=[kb.gmR])
    bn = kb.ysb[0]; bnR = kb.ysbR[0]
    P.dma("sp", bn[0:8, 0:128], d["gm_b_s"][0, :, :], writes=[bnR])
    P.dma("sp", bn[0:8, 128:256].rearrange("h (q i) -> h q i", q=NSS), d["gm_b_s"][0, :, 0:8].unsqueeze(1).broadcast_to([8, NSS, 8]), writes=[bnR])
    for v in range(2):
        kb.tr(kb.pb[1][:, v * 8:(v + 1) * 8], bn[0:8, v * 128:(v + 1) * 128], kb.identf[0:8, 0:8], reads=[bnR, kb.cR], writes=[kb.pbR[1]], sig=(v == 1))
    kb.cp("dve", kb.gm_bs[:, 0:16], kb.pb[1][:, 0:16], reads=[kb.pbR[1]], writes=[kb.gmR])
    kb.barrier("dve", [natR, triR, bnR], [])


def _gm_mixer(kb, tl, l):
    P, d = kb.P, kb.dram
    NT, NB = tl.ntok, tl.NB
    NBM = T // 128
    _load_post(kb, tl, l, 1, 1.0)
    _modulate(kb, tl)
    kb.next_pre()
    W = kb.WS
    Wf = W[:, :].bitcast(F32)
    U = W[:, 0:NBM * D].rearrange("p (n f) -> p n f", n=NBM)
    Vf = Wf[:, 2048:2048 + NBM * D].rearrange("p (n f) -> p n f", n=NBM)
    Vn = W[:, 12288:12288 + NBM * D].rearrange("p (n f) -> p n f", n=NBM)
    Gg = Wf[:, 8192:9216]; Gb = Wf[:, 9216:10240]
    UR = [Res(f"gmU{i}") for i in range(NB)]; VR = [Res(f"gmV{i}") for i in range(NB)]; VnR = [Res(f"gmVn{i}") for i in range(NB)]
    GR = Res("gmG")
    kb.barrier("dve", kb.hR, UR + VR + VnR + [GR])
    P.dma("sp", Gg, d["gm_ln_g"][0:1, :].broadcast_to([128, D]), writes=[GR])
    P.dma("sp", Gb, d["gm_ln_b"][0:1, :].broadcast_to([128, D]), writes=[GR])
    for cb in range(4):
        slot, sR = kb.ws.get(("gm_in", tl.kind, tl.seq, tl.t0, cb), [
            (lambda t: t[:, 0:4096].rearrange("p (k f) -> p k f", k=8), d["gm_w_in"][0, :, cb * 512:(cb + 1) * 512].rearrange("(k p) f -> p k f", p=128))])
        sv = slot[:, 0:4096].rearrange("p (k f) -> p k f", k=8)
        for nb in range(NB):
            bank, bR = kb.pb[(cb * NB + nb) % 4], kb.pbR[(cb * NB + nb) % 4]
            for k in range(8):
                kb.mm(bank[:, :], kb.xmT[:, k, nb * 128:(nb + 1) * 128], sv[:, k, :], k == 0, k == 7, reads=[sR, kb.xmTR[nb]], writes=[bR], sig=(k == 7))
            if cb < 2:
                kb.act(U[:, nb, cb * 512:(cb + 1) * 512], bank[:, :], AF.Gelu_apprx_tanh, reads=[bR], writes=[UR[nb]])
            else:
                kb.act(Vf[:, nb, (cb - 2) * 512:(cb - 1) * 512], bank[:, :], AF.Gelu_apprx_tanh, reads=[bR], writes=[VR[nb]])
    for nb in range(NB):
        vb = Vf[:, nb, :]
        kb.rot_ln()
        _ln_stats(kb, lambda h, nb=nb: Vf[:, nb, h * 512:(h + 1) * 512], VR[nb], LN_EPS)
        kb.act(vb, vb, AF.Identity, reads=[VR[nb], kb.mvR], writes=[VR[nb]], scale=kb.mv[:, 3:4], bias=kb.mv[:, 4:5])
        kb.tt("dve", vb, vb, Gg, ALU.mult, reads=[VR[nb], GR], writes=[VR[nb]])
        kb.tt("dve", vb, vb, Gb, ALU.add, reads=[VR[nb], GR], writes=[VR[nb]])
        kb.cp("act", Vn[:, nb, :], vb, reads=[VR[nb]], writes=[VnR[nb]])
        if tl.kind == "s":
            kb.out_evs += P.dma("sp", d["gm_v_s"][:, :], vb, reads=[VR[nb]])
    wsT = kb.gm_wsT if tl.kind == "p" else kb.gm_wsTs
    bo = 0 if tl.kind == "p" else 8
    for nb in range(NB):
        g, gR = kb.xmb[nb % 2], kb.xmbR[nb % 2]
        for hb in range(2):
            bank, bR = kb.pb[4 + hb], kb.pbR[4 + hb]
            for h4 in range(4):
                h = hb * 4 + h4
                kb.mm(bank[:, h4 * 128:(h4 + 1) * 128], wsT[:, h, :], Vn[:, nb, h * 128:(h + 1) * 128], True, True,
                      reads=[kb.gmR, VnR[nb]], writes=[bR], sig=(h4 == 3))
            for h4 in range(4):
                h = hb * 4 + h4
                kb.stt(g[:, h * 128:(h + 1) * 128], bank[:, h4 * 128:(h4 + 1) * 128], kb.gm_bs[:, bo + h:bo + h + 1],
                       U[:, nb, h * 128:(h + 1) * 128], ALU.add, ALU.mult, reads=[bR, kb.gmR, UR[nb]], writes=[gR])
        _to_featT(kb, tl, nb, g, gR)
    _down_proj(kb, tl, ("gm_out", tl.kind, tl.seq, tl.t0), 8,
               lambda dc: [(lambda t: t[:, 0:1024].rearrange("p (k c) -> p k c", k=8),
                            d["gm_w_out"][0, :, dc * 128:(dc + 1) * 128].rearrange("(k p) c -> p k c", p=128))],
               lambda slot: (lambda kc, dc, v=slot[:, 0:1024].rearrange("p (k c) -> p k c", k=8): v[:, kc, :]),
               lambda kc: (kb.xmT[:, kc, 0:NT], kb.xmTR[:NB]))
    _layer_norm(kb, tl)
    kb.barrier("dve", UR + VR + VnR + [GR], kb.hR)


MIXERS[1] = _gm_mixer


def _pool_declare(kb):
    kb.din("k_pm", [5, 128, 4, 128])
    kb.din("k_prc", [128, 8])
    kb.pm = kb.sb("pm", [128, 5, 512], BF16)
    kb.prc = kb.sb("prc", [128, 8])
    kb.pscT = kb.sb("pscT", [128, 8])
    kb.pprev = kb.sb("pprev", [128, D], BF16)
    kb.pprevR = Res("pprev"); kb.plR = Res("pool_consts")


def _pool_setup(kb):
    P, d = kb.P, kb.dram
    P.dma("pool", kb.pm[:, :, :].rearrange("p a (g t) -> p a g t", g=4), d["k_pm"].rearrange("a s g t -> s a g t"), writes=[kb.plR])
    P.dma("sp", kb.prc[:, :], d["k_prc"][:, :], writes=[kb.plR])
    nat = kb.ysb[0]; natR = kb.ysbR[0]
    P.dma("sp", nat[0:8, 0:128], d["pool_scale"][0, :].rearrange("(k p) -> k p", p=128), writes=[natR])
    kb.tr(kb.pb[1][:, 0:8], nat[0:8, 0:128], kb.identf[0:8, 0:8], reads=[natR, kb.cR], writes=[kb.pbR[1]], sig=True)
    kb.cp("dve", kb.pscT[:, :], kb.pb[1][:, 0:8], reads=[kb.pbR[1]], writes=[kb.plR])
    kb.barrier("dve", [natR], [])


def _pool_mixer(kb, tl, l):
    P, d = kb.P, kb.dram
    NT, NB = tl.ntok, tl.NB
    NBM = T // 128
    _load_post(kb, tl, l, 1, 1.0)
    W = kb.WS
    XB = W[:, 0:NBM * D].rearrange("p (n f) -> p n f", n=NBM)
    XBR = [Res(f"plX{i}") for i in range(NB)]
    SPt = [W[:, 4096 + i * 1024:4096 + (i + 1) * 1024] for i in range(2)]
    SPR = Res("plSP")
    Wf = W[:, :].bitcast(F32)
    SPf = [Wf[:, 4096 + i * 1024:4096 + (i + 1) * 1024] for i in range(2)]
    kb.barrier("dve", kb.hR, XBR + [SPR])
    last_seq_blk = (tl.kind == "p" and tl.t0 + NT == SEQ)
    for nb in range(NB):
        b = nb % 2
        tmp, tR = kb.tmpf[b], kb.tmpfR[b]
        kb.tt("dve", tmp[:, :], kb.x[:, nb, :], kb.bc["S1"][:, :], ALU.mult, reads=[kb.xR[nb], kb.bcR["S1"]], writes=[tR])
        kb.tt("dve", XB[:, nb, :], tmp[:, :], kb.bc["Bm"][:, :], ALU.add, reads=[tR, kb.bcR["Bm"]], writes=[XBR[nb]])
        if (last_seq_blk and nb == NB - 1) or tl.kind == "s":
            kb.tt("dve", tmp[:, :], tmp[:, :], kb.bc["Bm"][:, :], ALU.add, reads=[tR, kb.bcR["Bm"]], writes=[tR])
            if tl.kind == "p":
                kb.out_evs += P.dma("sp", d["pool_p"][tl.seq, :, :], tmp[113:128, :], reads=[tR])
            else:
                kb.out_evs += P.dma_group("sp", [(d["pool_s"][q, 7:15, :], tmp[q * 8:(q + 1) * 8, :]) for q in range(NSS)], reads=[tR])
    kb.next_pre()
    if tl.kind == "s":
        for i in range(2):
            kb.memset("dve", SPf[i], 0.0, writes=[SPR])
        P.dma_group("sp", [(SPf[q // 8][(q % 8) * 16:(q % 8) * 16 + 15, :], d["st_pool"][q, :, :]) for q in range(NSS)], writes=[SPR])
        kb.out_evs += P.dma_group("sp", [(d["pool_s"][q, 0:7, :], SPf[q // 8][(q % 8) * 16 + 8:(q % 8) * 16 + 15, :]) for q in range(NSS)], reads=[SPR])
        SPb = [kb.xmb[i] for i in range(2)]
        for i in range(2):
            kb.cp("dve", SPb[i][:, :], SPf[i], reads=[SPR], writes=[kb.xmbR[i]])
    for nb in range(NB):
        pp, ppR = kb.sgt[nb % 2], kb.sgtR[nb % 2]
        first = (tl.kind == "p" and tl.t0 == 0 and nb == 0)
        pbuf, pR = kb.tmpf[nb % 2][:, :].bitcast(BF16)[:, 0:D], kb.tmpfR[nb % 2]
        for gq in range(4):
            bank, bR = kb.pb[gq % 2], kb.pbR[gq % 2]
            cs = slice(gq * 256, (gq + 1) * 256)
            if tl.kind == "p":
                kb.mm(bank[:, 0:256], kb.pm[:, 0, gq * 128:(gq + 1) * 128], XB[:, nb, cs], True, first, reads=[kb.plR, XBR[nb]], writes=[bR], sig=first)
                if not first:
                    prev, prevR = (XB[:, nb - 1, cs], XBR[nb - 1]) if nb > 0 else (kb.pprev[:, cs], kb.pprevR)
                    kb.mm(bank[:, 0:256], kb.pm[:, 1, gq * 128:(gq + 1) * 128], prev, False, True, reads=[kb.plR, prevR], writes=[bR], sig=True)
            else:
                kb.mm(bank[:, 0:256], kb.pm[:, 2, gq * 128:(gq + 1) * 128], XB[:, nb, cs], True, False, reads=[kb.plR, XBR[nb]], writes=[bR])
                kb.mm(bank[:, 0:256], kb.pm[:, 3, gq * 128:(gq + 1) * 128], SPb[0][:, cs], False, False, reads=[kb.plR, kb.xmbR[0]], writes=[bR])
                kb.mm(bank[:, 0:256], kb.pm[:, 4, gq * 128:(gq + 1) * 128], SPb[1][:, cs], False, True, reads=[kb.plR, kb.xmbR[1]], writes=[bR], sig=True)
            ro = 0 if first else 4
            kb.stt(pbuf[:, cs], bank[:, 0:256], kb.prc[:, ro + gq:ro + gq + 1], XB[:, nb, cs], ALU.mult, ALU.subtract,
                   reads=[bR, kb.plR, XBR[nb]], writes=[pR])
        _to_featT(kb, tl, nb, pbuf, pR)
    if tl.kind == "p":
        kb.cp("act", kb.pprev[:, :], XB[:, NB - 1, :], reads=[XBR[NB - 1]], writes=[kb.pprevR])

    def evac(dc, ysb, yR, acc, aR):
        kb.act(ysb[:, 0:NT], acc[:, 0:NT], AF.Identity, reads=[aR, kb.plR], writes=[yR], scale=kb.pscT[:, dc:dc + 1])
    _down_proj(kb, tl, ("pool_w", tl.kind, tl.seq, tl.t0), lambda dc: [2 * (dc // 2), 2 * (dc // 2) + 1],
               lambda dc: [(lambda t: t[:, 0:256].rearrange("p (k c) -> p k c", k=2),
                            d["pool_w"][0, dc // 2, :, (dc % 2) * 128:(dc % 2 + 1) * 128].rearrange("(k p) c -> p k c", p=128))],
               lambda slot: (lambda kc, dc, v=slot[:, 0:256].rearrange("p (k c) -> p k c", k=2): v[:, kc % 2, :]),
               lambda kc: (kb.xmT[:, kc, 0:NT], kb.xmTR[:NB]), evac=evac)
    _layer_norm(kb, tl)
    kb.barrier("dve", XBR + [SPR], kb.hR)


MIXERS[2] = _pool_mixer


def make_consts():
    c = {}
    c["k_ident"] = np.eye(128, dtype=np.float32)
    ri = np.arange(128)[:, None]
    ci = np.arange(128)[None, :]
    c["k_tmask"] = ((ci // 16) >= (ri // 16)).astype(np.float32)
    c["k_tri"] = (ci >= ri).astype(np.float32)
    wins = [2, 4, 8, 16]
    pm = np.zeros((5, 128, 4, 128), np.float32)
    prc = np.zeros((128, 8), np.float32)
    for g, w in enumerate(wins):
        s = np.arange(128)[:, None]
        t = np.arange(128)[None, :]
        pm[0, :, g, :] = ((s <= t) & (s > t - w))
        pm[1, :, g, :] = ((s - 128) > (t - w))
        qs, is_ = s // 8, s % 8
        qt, it = t // 8, t % 8
        pm[2, :, g, :] = ((qs == qt) & (is_ <= it) & (is_ > it - w))
        for h in range(2):
            q8, j = s // 16, s % 16
            pm[3 + h, :, g, :] = ((q8 + 8 * h == qt) & (j < 15) & ((j - 15) >= (it - w + 1)))
        prc[:, g] = 1.0 / np.minimum(w, np.arange(128) + 1)
        prc[:, 4 + g] = 1.0 / w
    c["k_pm"] = pm
    c["k_prc"] = prc
    half = 32
    inv_freq = np.power(np.float32(10000.0), -np.arange(half, dtype=np.float32) * np.float32(2.0 / 64)).astype(np.float32)

    def rope_tab(pos):
        ang = pos.astype(np.float32)[:, None] * inv_freq[None, :]
        return np.concatenate([np.cos(ang), np.sin(ang)], 1).astype(np.float32)
    c["k_rope_p"] = rope_tab(np.arange(SEQ))
    c["k_rope_s"] = rope_tab(PAST + (np.arange(128) % 8))
    c["k_iota32"] = (np.arange(128) % 32).astype(np.float32)[:, None]
    key = np.arange(128)[:, None]
    qi = np.arange(128)[None, :]
    c["k_smask"] = ((key // 8 == qi // 8) & (key % 8 <= qi % 8)).astype(np.float32)
    return c


def _mla_declare(kb):
    kb.din("k_rope_p", [SEQ, 64]); kb.din("k_rope_s", [128, 64])
    kb.din("k_iota32", [128, 1]); kb.din("k_smask", [128, 128])
    kb.ml_ukT = kb.sb("ml_ukT", [128, 8, 256], BF16)
    kb.ml_qn = kb.sb("ml_qn", [128, QL]); kb.ml_kvn = kb.sb("ml_kvn", [128, KVL])
    kb.ml_tri = kb.sb("ml_tri", [128, 128], BF16); kb.ml_smask = kb.sb("ml_smask", [128, 128], BF16)
    kb.ml_zero = kb.sb("ml_zero", [128, 512], BF16); kb.ml_one = kb.sb("ml_one", [128, 2], BF16)
    kb.ml_ckvT = kb.sb("ml_ckvT", [128, 2, SEQ], BF16); kb.ml_krT = kb.sb("ml_krT", [64, SEQ], BF16)
    kb.ml_Vx = kb.sb("ml_Vx", [128, SEQ // 128, KVL], BF16)
    kb.ml_idx = kb.X2[1][:, :, :].rearrange("p n f -> p (n f)").bitcast(U32)[:, 0:NSS * N_PAGES // 4]
    kb.ml_rt = kb.sb("ml_rt", [128, 4, 64])
    kb.ml_sm = kb.sb("ml_sm", [128, 16])
    kb.mlR = Res("mla_consts"); kb.kvR = Res("mla_kv"); kb.rtR = Res("mla_rt"); kb.smR = Res("mla_sm")


def _mla_setup(kb):
    P, d = kb.P, kb.dram
    nat = kb.WS[:, :].bitcast(F32)[:, 0:2048].rearrange("p (a f) -> p a f", a=2)
    natR = Res("mla_nat")
    kb.barrier("dve", kb.hR, [natR])
    P.dma("sp", nat, d["mla_w_uk"][0].rearrange("(a p) h e -> p a (h e)", p=128), writes=[natR])
    for cc in range(2):
        for hb in range(2):
            for h4 in range(4):
                h = hb * 4 + h4
                kb.tr(kb.pb[0][:, h4 * 128:(h4 + 1) * 128], nat[:, cc, h * 128:(h + 1) * 128], kb.identf[:, :],
                      reads=[natR, kb.cR], writes=[kb.pbR[0]], sig=(h4 == 3))
            kb.cp("dve", kb.ml_ukT[:, hb * 4:(hb + 1) * 4, cc * 128:(cc + 1) * 128], kb.pb[0][:, :].rearrange("p (h c) -> p h c", h=4),
                  reads=[kb.pbR[0]], writes=[kb.mlR])
    P.dma("sp", kb.ml_qn[:, :], d["mla_q_norm"][0:1, :].broadcast_to([128, QL]), writes=[kb.mlR])
    P.dma("sp", kb.ml_kvn[:, :], d["mla_kv_norm"][0:1, :].broadcast_to([128, KVL]), writes=[kb.mlR])
    P.dma("pool", kb.ml_tri[:, :], d["k_tri"][:, :], writes=[kb.mlR])
    P.dma("pool", kb.ml_smask[:, :], d["k_smask"][:, :], writes=[kb.mlR])
    kb.memset("dve", kb.ml_zero[:, :], 0.0, writes=[kb.mlR])
    kb.memset("dve", kb.ml_one[:, :], 1.0, writes=[kb.mlR])
    kb.barrier("dve", [natR], kb.hR)


def _mla_build_idx(kb):
    P, d = kb.P, kb.dram
    pti = kb.tmpf[0][:, :].bitcast(I32); ptf = kb.tmpf[1]; iot = kb.mv[:, 7:8]
    sel = kb.tmpf[0][:, 0:256]
    P.dma("sp", pti, d["ptab"].rearrange("q g -> (q g)").unsqueeze(0).broadcast_to([128, NSS * N_PAGES]), writes=[kb.tmpfR[0]])
    P.dma("sp", iot, d["k_iota32"][:, :], writes=[kb.mvR])
    kb.cp("dve", ptf[:, :], pti, reads=[kb.tmpfR[0]], writes=[kb.tmpfR[1]])
    pv = ptf[:, :].rearrange("p (q g r) -> p q g r", q=NSS, r=4)
    for r in range(4):
        kb.cp("dve", sel[32 * r:32 * (r + 1), :].rearrange("p (q g) -> p q g", q=NSS), pv[32 * r:32 * (r + 1), :, :, r],
              reads=[kb.tmpfR[1]], writes=[kb.tmpfR[0]])
    kb.ts("dve", sel, sel, 32.0, iot, ALU.mult, ALU.add, reads=[kb.tmpfR[0], kb.mvR], writes=[kb.tmpfR[0]])
    kb.cp("dve", kb.ml_idx, sel, reads=[kb.tmpfR[0]], writes=[kb.mlR] + kb.X2R[1])


def _rms_rstd(kb, bank, bR, n, col):
    junk = kb.tmpf[1]
    kb.P.op("act", lambda e: e.activation(out=junk[:, 0:n], in_=bank[:, 0:n], func=AF.Square, accum_out=kb.ml_sm[:, col:col + 1]),
            reads=[bR], writes=[kb.tmpfR[1], kb.smR])
    kb.act(kb.ml_sm[:, col + 1:col + 2], kb.ml_sm[:, col:col + 1], AF.Sqrt, reads=[kb.smR], writes=[kb.smR], bias=RMS_EPS, scale=1.0 / n)
    kb.P.op("dve", lambda e: e.reciprocal(out=kb.ml_sm[:, col + 2:col + 3], in_=kb.ml_sm[:, col + 1:col + 2]), reads=[kb.smR], writes=[kb.smR])
    return kb.ml_sm[:, col + 2:col + 3]


def _rope_tok(kb, X, XR, rt, outv, outR, nh, scratch):
    A, Bv = scratch
    cosb = rt[:, 0:32].unsqueeze(1).unsqueeze(1).broadcast_to([128, nh, 2, 32])
    sinb = rt[:, 32:64].unsqueeze(1).unsqueeze(1).broadcast_to([128, nh, 2, 32])
    kb.tt("dve", A, X, cosb, ALU.mult, reads=XR + [kb.rtR], writes=[kb.tmpfR[0]])
    kb.tt("dve", Bv, X, sinb, ALU.mult, reads=XR + [kb.rtR], writes=[kb.tmpfR[0]])
    kb.tt("dve", outv[:, :, 0, :], A[:, :, 0, :], Bv[:, :, 1, :], ALU.subtract, reads=[kb.tmpfR[0]], writes=[outR])
    kb.tt("dve", outv[:, :, 1, :], Bv[:, :, 0, :], A[:, :, 1, :], ALU.add, reads=[kb.tmpfR[0]], writes=[outR])


def _mla_mixer(kb, tl, l):
    P, d = kb.P, kb.dram
    NT, NB = tl.ntok, tl.NB
    samp = tl.kind == "s"
    if samp:
        _mla_build_idx(kb)
    _load_post(kb, tl, l, 1, 1.0)
    _modulate(kb, tl)
    kb.next_pre()
    W = kb.WS
    cqT = W[:, 0:1536].rearrange("p (a t) -> p a t", a=3)
    qnT = W[:, 1536:5632].rearrange("p (h t) -> p h t", h=8)
    olat = W[:, 1536:3584].rearrange("p (h c) -> p h c", h=8)
    olT = W[:, 3584:5632].rearrange("p (a h t) -> p a h t", a=2, h=8)
    qrT = W[:, 5632:9728].rearrange("p (h t) -> p h t", h=8)
    qlT = W[:, 9728:17920].rearrange("p (h a t) -> p h a t", h=8, a=2)
    PT = [W[:, 17920 + i * 1024:17920 + (i + 1) * 1024] for i in range(2)]
    KP = [W[:, 19968 + i * 1280:19968 + (i + 1) * 1280] for i in range(2)]
    KT = [W[:, 22528 + i * 384:22528 + (i + 1) * 384] for i in range(2)]
    cqR, qnR, qrR, qlR, olR, oltR = [Res(n) for n in ("cqT", "qnT", "qrT", "qlT", "olat", "olT")]
    PTR = [Res("PT0"), Res("PT1")]; KPR = [Res(f"KP{i}") for i in range(2)]; KTR = [Res(f"KT{i}") for i in range(2)]
    kb.barrier("dve", kb.hR, [cqR, qnR, qrR, qlR, olR, oltR] + PTR + KPR + KTR)
    for nb in range(NB):
        src = d["k_rope_s"][:, :] if samp else d["k_rope_p"][tl.t0 + nb * 128:tl.t0 + (nb + 1) * 128, :]
        P.dma("sp", kb.ml_rt[:, nb, :], src, writes=[kb.rtR])
    fA = kb.tmpf[0][:, 0:512]; fB = kb.tmpf[0][:, 512:1024]
    kb0 = 0 if samp else tl.t0 // 128
    ckvT, krT, Vx = kb.ml_ckvT, kb.ml_krT, kb.ml_Vx
    slot, sR = kb.ws.get(("ml_dkv", tl.kind, tl.seq, tl.t0), [
        (lambda t: t[:, 0:2560].rearrange("p (k f) -> p k f", k=8), d["mla_w_dkv"][0].rearrange("(k p) f -> p k f", p=128))])
    sv = slot[:, 0:2560].rearrange("p (k f) -> p k f", k=8)
    for nb in range(NB):
        bank, bR = kb.pb[nb % 2], kb.pbR[nb % 2]
        for k in range(8):
            kb.mm(bank[:, 0:320], kb.xmT[:, k, nb * 128:(nb + 1) * 128], sv[:, k, :], k == 0, k == 7, reads=[sR, kb.xmTR[nb]], writes=[bR], sig=(k == 7))
        rstd = _rms_rstd(kb, bank, bR, KVL, 0)
        cf, cfR = kb.ysb[nb % 2], kb.ysbR[nb % 2]
        kb.stt(cf[:, 0:KVL], bank[:, 0:KVL], rstd, kb.ml_kvn[:, :], ALU.mult, ALU.mult, reads=[bR, kb.smR, kb.mlR], writes=[cfR])
        A = fA[:, 0:64].rearrange("p (h a f) -> p h a f", h=1, a=2); Bv = fB[:, 0:64].rearrange("p (h a f) -> p h a f", h=1, a=2)
        _rope_tok(kb, bank[:, 256:320].rearrange("p (h a f) -> p h a f", h=1, a=2), [bR], kb.ml_rt[:, nb, :],
                  cf[:, 256:320].rearrange("p (h a f) -> p h a f", h=1, a=2), cfR, 1, (A, Bv))
        if samp:
            kb.out_evs += P.dma("sp", d["ckv_s"][:, :], cf[:, 0:KVL], reads=[cfR])
            kb.out_evs += P.dma("sp", d["kr_s"][:, :], cf[:, 256:320], reads=[cfR])
        else:
            r0 = tl.t0 + nb * 128
            kb.out_evs += P.dma("sp", d["ckv_p"][tl.seq, r0:r0 + 128, :], cf[:, 0:KVL], reads=[cfR])
            kb.out_evs += P.dma("sp", d["kr_p"][tl.seq, r0:r0 + 128, :], cf[:, 256:320], reads=[cfR])
        kblk = kb0 + nb
        kbf, kbfR = kb.xmb[nb % 2], kb.xmbR[nb % 2]
        kb.cp("act", kbf[:, 0:320], cf[:, 0:320], reads=[cfR], writes=[kbfR])
        kb.cp("dve", Vx[:, kblk, :], kbf[:, 0:KVL], reads=[kbfR], writes=[kb.kvR])
        for cc in range(2):
            kb.tr(kb.pt[:, cc * 128:(cc + 1) * 128], kbf[:, cc * 128:(cc + 1) * 128], kb.identb[:, :], reads=[kbfR, kb.cR], writes=[kb.pbR[7]])
        kb.tr(kb.pt[0:64, 256:384], kbf[:, 256:320], kb.identb[:, :], reads=[kbfR, kb.cR], writes=[kb.pbR[7]], sig=True)
        kb.cp("act", ckvT[:, :, kblk * 128:(kblk + 1) * 128], kb.pt[:, 0:256].rearrange("p (a t) -> p a t", a=2), reads=[kb.pbR[7]], writes=[kb.kvR])
        kb.cp("act", krT[:, kblk * 128:(kblk + 1) * 128], kb.pt[0:64, 256:384], reads=[kb.pbR[7]], writes=[kb.kvR])
    slot, sR = kb.ws.get(("ml_dq", tl.kind, tl.seq, tl.t0), [
        (lambda t: t[:, 0:3072].rearrange("p (k f) -> p k f", k=8), d["mla_w_dq"][0].rearrange("(k p) f -> p k f", p=128))])
    sv = slot[:, 0:3072].rearrange("p (k f) -> p k f", k=8)
    for nb in range(NB):
        bank, bR = kb.pb[2 + nb % 2], kb.pbR[2 + nb % 2]
        for k in range(8):
            kb.mm(bank[:, 0:QL], kb.xmT[:, k, nb * 128:(nb + 1) * 128], sv[:, k, :], k == 0, k == 7, reads=[sR, kb.xmTR[nb]], writes=[bR], sig=(k == 7))
        rstd = _rms_rstd(kb, bank, bR, QL, 4)
        cqb, cqbR = kb.xmb[nb % 2], kb.xmbR[nb % 2]
        kb.stt(cqb[:, 0:QL], bank[:, 0:QL], rstd, kb.ml_qn[:, :], ALU.mult, ALU.mult, reads=[bR, kb.smR, kb.mlR], writes=[cqbR])
        for a in range(3):
            kb.tr(kb.pt[:, a * 128:(a + 1) * 128], cqb[:, a * 128:(a + 1) * 128], kb.identb[:, :], reads=[cqbR, kb.cR], writes=[kb.pbR[7]], sig=(a == 2))
        kb.cp("act", cqT[:, :, nb * 128:(nb + 1) * 128], kb.pt[:, 0:384].rearrange("p (a t) -> p a t", a=3), reads=[kb.pbR[7]], writes=[cqR])
    for hg in range(2):
        slot, sR = kb.ws.get(("ml_uq", tl.kind, tl.seq, tl.t0, hg), [
            (lambda t: t[:, 0:2304].rearrange("p (a f) -> p a f", a=3),
             d["mla_w_uq"][0, :, hg * 4:(hg + 1) * 4, :].rearrange("(a p) h e -> p a (h e)", p=128))])
        sv = slot[:, 0:2304].rearrange("p (a f) -> p a f", a=3)
        for hh in range(4):
            h = hg * 4 + hh
            bank, bR = kb.pb[hh % 2], kb.pbR[hh % 2]
            for a in range(3):
                kb.mm(bank[:, 0:NT], sv[:, a, hh * 192:hh * 192 + 128], cqT[:, a, 0:NT], a == 0, a == 2, reads=[sR, cqR], writes=[bR], sig=(a == 2))
            kb.cp("act", qnT[:, h, 0:NT], bank[:, 0:NT], reads=[bR], writes=[qnR])
        for nb in range(NB):
            bank, bR = kb.pb[2 + nb % 2], kb.pbR[2 + nb % 2]
            for hh in range(4):
                for a in range(3):
                    kb.mm(bank[:, hh * 64:(hh + 1) * 64], cqT[:, a, nb * 128:(nb + 1) * 128], sv[:, a, hh * 192 + 128:hh * 192 + 192], a == 0, a == 2,
                          reads=[sR, cqR], writes=[bR], sig=(hh == 3 and a == 2))
            qrb, qrbR = kb.xmb[nb % 2], kb.xmbR[nb % 2]
            A = fA[:, 0:256].rearrange("p (h a f) -> p h a f", h=4, a=2); Bv = fB[:, 0:256].rearrange("p (h a f) -> p h a f", h=4, a=2)
            _rope_tok(kb, bank[:, 0:256].rearrange("p (h a f) -> p h a f", h=4, a=2), [bR], kb.ml_rt[:, nb, :],
                      qrb[:, 0:256].rearrange("p (h a f) -> p h a f", h=4, a=2), qrbR, 4, (A, Bv))
            for hh in range(4):
                kb.tr(kb.pt[0:64, hh * 128:(hh + 1) * 128], qrb[:, hh * 64:(hh + 1) * 64], kb.identb[:, :], reads=[qrbR, kb.cR], writes=[kb.pbR[7]], sig=(hh == 3))
            kb.act(qrT[0:64, hg * 4:(hg + 1) * 4, nb * 128:(nb + 1) * 128], kb.pt[0:64, 0:512].rearrange("p (h t) -> p h t", h=4), AF.Identity,
                   reads=[kb.pbR[7]], writes=[qrR], scale=ATTN_SCALE)
    for h in range(8):
        for cc in range(2):
            bank, bR = kb.pb[(h * 2 + cc) % 4], kb.pbR[(h * 2 + cc) % 4]
            kb.mm(bank[:, 0:NT], kb.ml_ukT[:, h, cc * 128:(cc + 1) * 128], qnT[:, h, 0:NT], True, True, reads=[kb.mlR, qnR], writes=[bR], sig=True)
            kb.act(qlT[:, h, cc, 0:NT], bank[:, 0:NT], AF.Identity, reads=[bR], writes=[qlR], scale=ATTN_SCALE)
    rsm = kb.ml_sm
    if not samp:
        for nb in range(NB):
            qb = kb0 + nb
            qs = slice(nb * 128, (nb + 1) * 128)
            for bk in range(4):
                kb.mm(kb.pb[2 + bk][:, :], kb.ml_zero[0:1, 0:128], kb.ml_zero[0:1, :], True, False, reads=[kb.mlR], writes=[kb.pbR[2 + bk]])
            kb.mm(kb.pb[6][:, 0:8], kb.ml_zero[0:1, 0:128], kb.ml_zero[0:1, 0:8], True, False, reads=[kb.mlR], writes=[kb.pbR[6]])
            for kk in range(qb + 1):
                ks = slice(kk * 128, (kk + 1) * 128)
                pt_, ptR = PT[kk % 2], PTR[kk % 2]
                for hf in range(2):
                    bank, bR = kb.pb[hf], kb.pbR[hf]
                    hs4 = slice(hf * 4, (hf + 1) * 4)
                    kb.mm(bank[:, :], ckvT[:, 0, ks], qlT[:, hs4, 0, qs], True, False, reads=[kb.kvR, qlR], writes=[bR])
                    kb.mm(bank[:, :], ckvT[:, 1, ks], qlT[:, hs4, 1, qs], False, False, reads=[kb.kvR, qlR], writes=[bR])
                    kb.mm(bank[:, :], krT[0:64, ks], qrT[0:64, hs4, qs], False, True, reads=[kb.kvR, qrR], writes=[bR], sig=True)
                    kb.act(pt_[:, hf * 512:(hf + 1) * 512], bank[:, :], AF.Exp, reads=[bR], writes=[ptR])
                if kk == qb:
                    pv = pt_[:, :].rearrange("p (h q) -> p h q", h=8)
                    kb.tt("dve", pv, pv, kb.ml_tri[:, :].unsqueeze(1).broadcast_to([128, 8, 128]), ALU.mult, reads=[ptR, kb.mlR], writes=[ptR])
                last = kk == qb
                for h in range(8):
                    kb.mm(kb.pb[2 + h // 2][:, (h % 2) * 256:(h % 2 + 1) * 256], pt_[:, h * 128:(h + 1) * 128], Vx[:, kk, :], False, last,
                          reads=[ptR, kb.kvR], writes=[kb.pbR[2 + h // 2]], sig=(last and h % 2 == 1))
                for h in range(8):
                    kb.mm(kb.pb[6][:, h:h + 1], pt_[:, h * 128:(h + 1) * 128], kb.ml_one[:, 0:1], False, last,
                          reads=[ptR, kb.mlR], writes=[kb.pbR[6]], sig=(last and h == 7))
            kb.P.op("dve", lambda e: e.reciprocal(out=rsm[:, 8:16], in_=kb.pb[6][:, 0:8]), reads=[kb.pbR[6]], writes=[kb.smR])
            for h in range(8):
                kb.act(olat[:, h, :], kb.pb[2 + h // 2][:, (h % 2) * 256:(h % 2 + 1) * 256], AF.Identity, reads=[kb.pbR[2 + h // 2], kb.smR],
                       writes=[olR], scale=rsm[:, 8 + h:9 + h])
            for cc in range(2):
                for h in range(8):
                    kb.tr(kb.pt[:, h * 128:(h + 1) * 128], olat[:, h, cc * 128:(cc + 1) * 128], kb.identb[:, :], reads=[olR, kb.cR], writes=[kb.pbR[7]], sig=(h == 7))
                kb.cp("act", olT[:, cc, :, :], kb.pt[:, :].rearrange("p (h t) -> p h t", h=8), reads=[kb.pbR[7]], writes=[oltR])
            _mla_uv(kb, tl, nb, olT, oltR)
    else:
        rows_c = d["cache_ckv"].rearrange("n (a b) c -> (n a) (b c)", b=4)
        rows_r = d["cache_kr"].rearrange("n (a b) c -> (n a) (b c)", b=4)
        n = 0
        KR = [W[0:64, 22528 + i * 512:22528 + (i + 1) * 512] for i in range(2)]
        for q in range(NSS):
            kb.mm(kb.pb[2][0:64, 0:256], kb.ml_zero[0:1, 0:64], kb.ml_zero[0:1, 0:256], True, False, reads=[kb.mlR], writes=[kb.pbR[2]])
            kb.mm(kb.pb[3][0:64, 0:2], kb.ml_zero[0:1, 0:64], kb.ml_zero[0:1, 0:2], True, False, reads=[kb.mlR], writes=[kb.pbR[3]])
            qsl = slice(q * 8, (q + 1) * 8)
            for g4 in range(N_PAGES // 4 + 1):
                last = g4 == N_PAGES // 4
                npg = 1 if last else 4
                b2 = n % 2
                pt_, ptR = PT[b2], PTR[b2]
                bank, bR = kb.pb[b2], kb.pbR[b2]
                if not last:
                    kpf, kpR = KP[b2], KPR[b2]
                    ktc, ktcR = kb.xmb[b2], kb.xmbR[b2]
                    ktr, ktrR = KR[b2], KTR[b2]
                    col = q * (N_PAGES // 4) + g4
                    parts = []
                    for (dst, rows) in ((kpf[:, 0:1024], rows_c), (kpf[:, 1024:1280], rows_r)):
                        parts.append((lambda e, dst=dst, rows=rows, col=col: e.indirect_dma_start(
                            out=dst, out_offset=None, in_=rows, in_offset=bass.IndirectOffsetOnAxis(ap=kb.ml_idx[:, col:col + 1], axis=0)), None))
                    P.dma_group("pool", parts, reads=[kb.mlR], writes=[kpR])
                    for j in range(4):
                        for cc in range(2):
                            o = j * 256 + cc * 128
                            kb.tr(kb.pt[:, o:o + 128], kpf[:, o:o + 128], kb.identb[:, :], reads=[kpR, kb.cR], writes=[kb.pbR[7]], sig=(j == 3 and cc == 1))
                    kb.cp("dve", ktc[:, :], kb.pt[:, :], reads=[kb.pbR[7]], writes=[ktcR])
                    for j in range(4):
                        kb.tr(kb.pt[0:64, j * 128:(j + 1) * 128], kpf[:, 1024 + j * 64:1024 + (j + 1) * 64], kb.identb[:, :],
                              reads=[kpR, kb.cR], writes=[kb.pbR[7]], sig=(j == 3))
                    kb.cp("act", ktr[:, :], kb.pt[0:64, 0:512], reads=[kb.pbR[7]], writes=[ktrR])
                for j in range(npg):
                    if not last:
                        kc0, kc1, kr_ = ktc[:, j * 256:j * 256 + 128], ktc[:, j * 256 + 128:(j + 1) * 256], ktr[:, j * 128:(j + 1) * 128]
                        rds = [ktcR, ktrR]
                    else:
                        kc0, kc1, kr_, rds = ckvT[:, 0, 0:128], ckvT[:, 1, 0:128], krT[0:64, 0:128], [kb.kvR]
                    o = bank[:, j * 64:(j + 1) * 64]
                    kb.mm(o, kc0, qlT[:, :, 0, qsl], True, False, reads=rds + [qlR], writes=[bR])
                    kb.mm(o, kc1, qlT[:, :, 1, qsl], False, False, reads=rds + [qlR], writes=[bR])
                    kb.mm(o, kr_, qrT[0:64, :, qsl], False, True, reads=rds + [qrR], writes=[bR], sig=(j == npg - 1))
                kb.act(pt_[:, 0:npg * 64], bank[:, 0:npg * 64], AF.Exp, reads=[bR], writes=[ptR])
                if last:
                    pv = pt_[:, 0:64].rearrange("p (h i) -> p h i", h=8)
                    kb.tt("dve", pv, pv, kb.ml_smask[:, q * 8:(q + 1) * 8].unsqueeze(1).broadcast_to([128, 8, 8]), ALU.mult, reads=[ptR, kb.mlR], writes=[ptR])
                for j in range(npg):
                    vv, vR = (kpf[:, j * 256:(j + 1) * 256], [kpR]) if not last else (Vx[:, 0, :], [kb.kvR])
                    fin = last and j == npg - 1
                    kb.mm(kb.pb[2][0:64, 0:256], pt_[:, j * 64:(j + 1) * 64], vv, False, fin, reads=[ptR] + vR, writes=[kb.pbR[2]], sig=fin)
                    kb.mm(kb.pb[3][0:64, 0:1], pt_[:, j * 64:(j + 1) * 64], kb.ml_one[:, 0:1], False, fin, reads=[ptR, kb.mlR], writes=[kb.pbR[3]], sig=(j == npg - 1))
                n += 1
            kb.P.op("dve", lambda e: e.reciprocal(out=rsm[0:64, 8:9], in_=kb.pb[3][0:64, 0:1]), reads=[kb.pbR[3]], writes=[kb.smR])
            ob, obR = kb.sgt[q % 2], kb.sgtR[q % 2]
            kb.act(ob[0:64, 0:256], kb.pb[2][0:64, 0:256], AF.Identity, reads=[kb.pbR[2], kb.smR], writes=[obR], scale=rsm[0:64, 8:9])
            for cc in range(2):
                kb.tr(kb.pt[:, cc * 64:(cc + 1) * 64], ob[0:64, cc * 128:(cc + 1) * 128], kb.identb[0:64, 0:64], reads=[obR, kb.cR], writes=[kb.pbR[7]], sig=(cc == 1))
            kb.cp("act", olT[:, :, :, q * 8:(q + 1) * 8], kb.pt[:, 0:128].rearrange("p (a h i) -> p a h i", a=2, h=8), reads=[kb.pbR[7]], writes=[oltR])
        _mla_uv(kb, tl, 0, olT, oltR)
    _down_proj(kb, tl, ("ml_o", tl.kind, tl.seq, tl.t0), 8,
               lambda dc: [(lambda t: t[:, 0:1024].rearrange("p (h c) -> p h c", h=8),
                            d["mla_w_o"][0, :, :, dc * 128:(dc + 1) * 128].rearrange("h p c -> p h c"))],
               lambda slot: (lambda kc, dc, v=slot[:, 0:1024].rearrange("p (h c) -> p h c", h=8): v[:, kc, :]),
               lambda kc: (kb.xmT[:, kc, 0:NT], kb.xmTR[:NB]))
    _layer_norm(kb, tl)
    kb.barrier("dve", [cqR, qnR, qrR, qlR, olR, oltR] + PTR + KPR + KTR, kb.hR)


def _mla_uv(kb, tl, nb, olT, oltR):
    d = kb.dram
    slot, sR = kb.ws.get(("ml_uv", tl.kind, tl.seq, tl.t0, nb), [
        (lambda t: t[:, 0:2048].rearrange("p (a f) -> p a f", a=2), d["mla_w_uv"][0].rearrange("(a p) h e -> p a (h e)", p=128))])
    sv = slot[:, 0:2048].rearrange("p (a f) -> p a f", a=2)
    for hb in range(2):
        bank, bR = kb.pb[hb], kb.pbR[hb]
        for h4 in range(4):
            h = hb * 4 + h4
            for cc in range(2):
                kb.mm(bank[:, h4 * 128:(h4 + 1) * 128], sv[:, cc, h * 128:(h + 1) * 128], olT[:, cc, h, :], cc == 0, cc == 1,
                      reads=[sR, oltR], writes=[bR], sig=(h4 == 3 and cc == 1))
        kb.cp("act", kb.xmT[:, hb * 4:(hb + 1) * 4, nb * 128:(nb + 1) * 128], bank[:, :].rearrange("p (h t) -> p h t", h=4),
              reads=[bR], writes=[kb.xmTR[nb]])


MIXERS[3] = _mla_mixer


N_CORES = 8
_W_NAMES = [n for n, _ in WEIGHT_SPECS]


def kernel(**inputs):
    f32 = np.float32
    a = {k: np.asarray(v) for k, v in inputs.items()}
    n_phys = a["cache_mla_ckv"].shape[1]
    tiles = [Tile("p", s, t0, T) for s in range(NPS) for t0 in range(0, SEQ, T)] + [Tile("s", 0, 0, 128)]
    cfg = dict(tiles=tiles, layers=[0, 1, 2, 3], n_phys=n_phys)
    kb = build(cfg)
    consts = make_consts()
    shared = {n: np.ascontiguousarray(a[n], dtype=f32) for n in _W_NAMES}
    shared.update(consts)
    shared["cache_ckv"] = np.ascontiguousarray(a["cache_mla_ckv"][0], dtype=f32)
    shared["cache_kr"] = np.ascontiguousarray(a["cache_mla_krope"][0], dtype=f32)
    in_maps = []
    for c in range(N_CORES):
        ps, ss = slice(NPS * c, NPS * (c + 1)), slice(NSS * c, NSS * (c + 1))
        m = dict(shared)
        m["xp"] = np.ascontiguousarray(a["x_prompt"][ps], dtype=f32)
        m["xs"] = np.ascontiguousarray(a["x_sample"][ss], dtype=f32).reshape(128, D)
        m["st_re"] = np.ascontiguousarray(a["state_ssm_re"][0, ss], dtype=f32)
        m["st_im"] = np.ascontiguousarray(a["state_ssm_im"][0, ss], dtype=f32)
        m["st_pool"] = np.ascontiguousarray(a["state_pool"][0, ss], dtype=f32)
        m["ptab"] = np.ascontiguousarray(a["page_table"][ss]).astype(np.int32)
        m["c_all"] = np.ascontiguousarray(np.concatenate([a["c_prompt"][ps], a["c_sample"][ss]], 0), dtype=f32)
        in_maps.append(m)
    res = run_bass_kernel_spmd(kb.nc, in_maps, core_ids=list(range(N_CORES)))
    r = res.results

    def cat(name):
        return np.concatenate([np.asarray(r[c][name]) for c in range(N_CORES)], 0)
    y_p = cat("y_p")
    y_s = cat("y_s").reshape(N_CORES * NSS, DEC, D)
    outs = (
        y_p, y_s,
        cat("ssm_re_p")[None], cat("ssm_im_p")[None], cat("ssm_re_s")[None], cat("ssm_im_s")[None],
        cat("gm_v_s").reshape(1, N_CORES * NSS, DEC, D),
        cat("pool_p")[None], cat("pool_s")[None],
        cat("ckv_p")[None], cat("kr_p")[None],
        cat("ckv_s").reshape(1, N_CORES * NSS, DEC, KVL), cat("kr_s").reshape(1, N_CORES * NSS, DEC, DR),
    )
    return tuple(np.ascontiguousarray(o, dtype=f32) for o in outs)
```
